# Optimizing a Trainium2 kernel written in Bass

```python
import math, functools
import jax, jax.numpy as jnp
from jax import lax
import numpy as np

D_MODEL = 1024
BATCH = 16
SEQ = 2048
DEPTH = 2
DEC_BATCH = 16
DEC_SEQ = 16
PAST_LEN = 2048

CHUNK = 64
HEAD_DIM = 64
N_MIXERS = 2
A_HEADS = 16
A_LEFT_CHUNKS = 8
A_REL_CLIP = 128
B_Q_HEADS = 16
B_KV_HEADS = 4
B_WINDOW = 128
B_LEFT_CHUNKS = B_WINDOW // CHUNK
D_FF = 2816
ROPE_THETA = 10000.0
LN_EPS = 1e-5
DEEPNORM_ALPHA = (2.0 * DEPTH) ** 0.25
DEEPNORM_BETA = (8.0 * DEPTH) ** -0.25
N_A_LAYERS = (DEPTH + 1) // 2
N_B_LAYERS = DEPTH // 2

kernel_name = "streaming_chunkband_sinkswa_macaron_step"


def _layer_norm(x, g, b):
    xf = x.astype(jnp.float32)
    mu = jnp.mean(xf, axis=-1, keepdims=True)
    var = jnp.mean(jnp.square(xf - mu), axis=-1, keepdims=True)
    y = (xf - mu) * lax.rsqrt(var + LN_EPS) * g.astype(jnp.float32) + b.astype(jnp.float32)
    return y.astype(x.dtype)


def _post_norm(x, sub, g, b):
    return _layer_norm(DEEPNORM_ALPHA * x + sub, g, b)


def _swiglu(x, w_in, w_down):
    gate, up = jnp.split(x @ w_in, 2, axis=-1)
    return (jax.nn.silu(gate) * up) @ w_down


def _rope(x, pos):
    half = HEAD_DIM // 2
    inv = ROPE_THETA ** (-jnp.arange(half, dtype=jnp.float32) / half)
    ang = pos.astype(jnp.float32)[:, None] * inv[None, :]
    cos = jnp.cos(ang)[None, :, None, :]
    sin = jnp.sin(ang)[None, :, None, :]
    xf = x.astype(jnp.float32)
    x1, x2 = xf[..., :half], xf[..., half:]
    return jnp.concatenate([x1 * cos - x2 * sin, x2 * cos + x1 * sin], axis=-1).astype(x.dtype)


def _split_qkv(h, hq, hkv):
    bsz, t, _ = h.shape
    q, k, v = jnp.split(h, [hq * HEAD_DIM, (hq + hkv) * HEAD_DIM], axis=-1)
    return (q.reshape(bsz, t, hq, HEAD_DIM), k.reshape(bsz, t, hkv, HEAD_DIM),
            v.reshape(bsz, t, hkv, HEAD_DIM))


def _rel_bias(table, q_off, k_off):
    rel = jnp.clip(q_off[:, None] - k_off[None, :], -A_REL_CLIP, A_REL_CLIP) + A_REL_CLIP
    return table.astype(jnp.float32)[:, rel][:, None]


def _attend(q, k, v, valid, bias, sinks):
    s = jnp.einsum("bqhgd,bkhd->bhgqk", q, k, preferred_element_type=jnp.float32) * (HEAD_DIM ** -0.5)
    if bias is not None:
        s = s + bias
    if valid is not None:
        s = jnp.where(valid, s, -jnp.inf)
    if sinks is None:
        p = jax.nn.softmax(s, axis=-1)
    else:
        sk = sinks.astype(jnp.float32)[None, :, :, None, None]
        m = jnp.maximum(jnp.max(s, axis=-1, keepdims=True), sk)
        e = jnp.exp(s - m)
        p = e / (jnp.sum(e, axis=-1, keepdims=True) + jnp.exp(sk - m))
    return jnp.einsum("bhgqk,bkhd->bqhgd", p.astype(v.dtype), v)


def _band_attention(q, k, v, n_left, bias, sinks):
    bsz, s_len, hkv, g, hd = q.shape
    n_chunks = s_len // CHUNK
    pad = n_left * CHUNK
    band = pad + CHUNK
    kp = jnp.pad(k, ((0, 0), (pad, 0), (0, 0), (0, 0)))
    vp = jnp.pad(v, ((0, 0), (pad, 0), (0, 0), (0, 0)))
    qc = jnp.moveaxis(q.reshape(bsz, n_chunks, CHUNK, hkv, g, hd), 1, 0)
    offs = jnp.arange(band) - pad

    def one_chunk(args):
        c, qb = args
        start = c * CHUNK
        kb = lax.dynamic_slice_in_dim(kp, start, band, axis=1)
        vb = lax.dynamic_slice_in_dim(vp, start, band, axis=1)
        valid = (start + offs >= 0)[None, :]
        return _attend(qb, kb, vb, valid, bias, sinks)

    out = lax.map(one_chunk, (jnp.arange(n_chunks), qc))
    return jnp.moveaxis(out, 0, 1).reshape(bsz, s_len, hkv * g * hd)


def _mixer_a_prompt(x, w_qkv, w_o, table):
    bsz, s_len, _ = x.shape
    q, k, v = _split_qkv(x @ w_qkv, A_HEADS, A_HEADS)
    pad = A_LEFT_CHUNKS * CHUNK
    bias = _rel_bias(table, jnp.arange(CHUNK), jnp.arange(pad + CHUNK) - pad)
    o = _band_attention(q.reshape(bsz, s_len, A_HEADS, 1, HEAD_DIM), k, v, A_LEFT_CHUNKS, bias, None)
    keep = min(pad, s_len)
    return o @ w_o, k[:, s_len - keep:], v[:, s_len - keep:]


def _mixer_a_step(x, w_qkv, w_o, table, k_cache, v_cache):
    bsz, t, _ = x.shape
    q, k, v = _split_qkv(x @ w_qkv, A_HEADS, A_HEADS)
    lc = k_cache.shape[1]
    kk = jnp.concatenate([k_cache, k], axis=1)
    vv = jnp.concatenate([v_cache, v], axis=1)
    bias = _rel_bias(table, jnp.arange(t), jnp.arange(lc + t) - lc)
    o = _attend(q.reshape(bsz, t, A_HEADS, 1, HEAD_DIM), kk, vv, None, bias, None)
    return o.reshape(bsz, t, A_HEADS * HEAD_DIM) @ w_o, k, v


def _mixer_b_prompt(x, w_qkv, w_o, sinks):
    bsz, s_len, _ = x.shape
    q, k, v = _split_qkv(x @ w_qkv, B_Q_HEADS, B_KV_HEADS)
    pos = jnp.arange(s_len)
    q, k = _rope(q, pos), _rope(k, pos)
    g = B_Q_HEADS // B_KV_HEADS
    o = _band_attention(q.reshape(bsz, s_len, B_KV_HEADS, g, HEAD_DIM), k, v, B_LEFT_CHUNKS,
                        None, sinks.reshape(B_KV_HEADS, g))
    keep = min(B_WINDOW, s_len)
    return o @ w_o, k[:, s_len - keep:], v[:, s_len - keep:]


def _mixer_b_step(x, w_qkv, w_o, sinks, k_cache, v_cache):
    bsz, t, _ = x.shape
    q, k, v = _split_qkv(x @ w_qkv, B_Q_HEADS, B_KV_HEADS)
    pos = PAST_LEN + jnp.arange(t)
    q, k = _rope(q, pos), _rope(k, pos)
    g = B_Q_HEADS // B_KV_HEADS
    kk = jnp.concatenate([k_cache, k], axis=1)
    vv = jnp.concatenate([v_cache, v], axis=1)
    o = _attend(q.reshape(bsz, t, B_KV_HEADS, g, HEAD_DIM), kk, vv, None, None,
                sinks.reshape(B_KV_HEADS, g))
    return o.reshape(bsz, t, B_Q_HEADS * HEAD_DIM) @ w_o, k, v


def _macaron_layer(x, i, mixer, ln_g, ln_b, w_ffn_in, w_ffn_down):
    x = _post_norm(x, 0.5 * _swiglu(x, w_ffn_in[i, 0], w_ffn_down[i, 0]), ln_g[i, 0], ln_b[i, 0])
    m, k_new, v_new = mixer(x)
    x = _post_norm(x, m, ln_g[i, 1], ln_b[i, 1])
    x = _post_norm(x, 0.5 * _swiglu(x, w_ffn_in[i, 1], w_ffn_down[i, 1]), ln_g[i, 2], ln_b[i, 2])
    return x, k_new, v_new


def setup_inputs(seed: int = 0) -> dict:
    key = jax.random.key(seed)
    ks = jax.random.split(key, 16)
    f32 = jnp.float32

    def nrm(k, shape, scale):
        return scale * jax.random.normal(k, shape, f32)

    la = min(A_LEFT_CHUNKS * CHUNK, PAST_LEN)
    lb = min(B_WINDOW, PAST_LEN)
    qkv_b_cols = (B_Q_HEADS + 2 * B_KV_HEADS) * HEAD_DIM
    return {
        "x_prompt": nrm(ks[0], (BATCH, SEQ, D_MODEL), 1.0),
        "x_sample": nrm(ks[1], (DEC_BATCH, DEC_SEQ, D_MODEL), 1.0),
        "cache_a_k": nrm(ks[2], (N_A_LAYERS, DEC_BATCH, la, A_HEADS, HEAD_DIM), 1.0),
        "cache_a_v": nrm(ks[3], (N_A_LAYERS, DEC_BATCH, la, A_HEADS, HEAD_DIM), 1.0),
        "cache_b_k": nrm(ks[4], (N_B_LAYERS, DEC_BATCH, lb, B_KV_HEADS, HEAD_DIM), 1.0),
        "cache_b_v": nrm(ks[5], (N_B_LAYERS, DEC_BATCH, lb, B_KV_HEADS, HEAD_DIM), 1.0),
        "ln_g": 1.0 + nrm(ks[6], (DEPTH, 3, D_MODEL), 0.02),
        "ln_b": nrm(ks[7], (DEPTH, 3, D_MODEL), 0.02),
        "w_ffn_in": nrm(ks[8], (DEPTH, 2, D_MODEL, 2 * D_FF), D_MODEL ** -0.5),
        "w_ffn_down": nrm(ks[9], (DEPTH, 2, D_FF, D_MODEL), DEEPNORM_BETA * D_FF ** -0.5),
        "w_qkv_a": nrm(ks[10], (N_A_LAYERS, D_MODEL, 3 * A_HEADS * HEAD_DIM), D_MODEL ** -0.5),
        "w_o_a": nrm(ks[11], (N_A_LAYERS, A_HEADS * HEAD_DIM, D_MODEL),
                     DEEPNORM_BETA * (A_HEADS * HEAD_DIM) ** -0.5),
        "rel_bias_a": nrm(ks[12], (N_A_LAYERS, A_HEADS, 2 * A_REL_CLIP + 1), 0.5),
        "w_qkv_b": nrm(ks[13], (N_B_LAYERS, D_MODEL, qkv_b_cols), D_MODEL ** -0.5),
        "w_o_b": nrm(ks[14], (N_B_LAYERS, B_Q_HEADS * HEAD_DIM, D_MODEL),
                     DEEPNORM_BETA * (B_Q_HEADS * HEAD_DIM) ** -0.5),
        "sinks_b": nrm(ks[15], (N_B_LAYERS, B_Q_HEADS), 1.0),
    }


def reference(x_prompt, x_sample, cache_a_k, cache_a_v, cache_b_k, cache_b_v,
              ln_g, ln_b, w_ffn_in, w_ffn_down, w_qkv_a, w_o_a, rel_bias_a,
              w_qkv_b, w_o_b, sinks_b):
    xp, xs = x_prompt, x_sample
    ak_p, av_p, bk_p, bv_p = [], [], [], []
    ak_s, av_s, bk_s, bv_s = [], [], [], []
    for i in range(DEPTH):
        j = i // N_MIXERS
        if i % N_MIXERS == 0:
            mix_p = functools.partial(_mixer_a_prompt, w_qkv=w_qkv_a[j], w_o=w_o_a[j], table=rel_bias_a[j])
            mix_s = functools.partial(_mixer_a_step, w_qkv=w_qkv_a[j], w_o=w_o_a[j], table=rel_bias_a[j],
                                      k_cache=cache_a_k[j], v_cache=cache_a_v[j])
        else:
            mix_p = functools.partial(_mixer_b_prompt, w_qkv=w_qkv_b[j], w_o=w_o_b[j], sinks=sinks_b[j])
            mix_s = functools.partial(_mixer_b_step, w_qkv=w_qkv_b[j], w_o=w_o_b[j], sinks=sinks_b[j],
                                      k_cache=cache_b_k[j], v_cache=cache_b_v[j])
        xp, kp_new, vp_new = _macaron_layer(xp, i, mix_p, ln_g, ln_b, w_ffn_in, w_ffn_down)
        xs, ks_new, vs_new = _macaron_layer(xs, i, mix_s, ln_g, ln_b, w_ffn_in, w_ffn_down)
        if i % N_MIXERS == 0:
            ak_p.append(kp_new); av_p.append(vp_new); ak_s.append(ks_new); av_s.append(vs_new)
        else:
            bk_p.append(kp_new); bv_p.append(vp_new); bk_s.append(ks_new); bv_s.append(vs_new)
    return (xp, xs,
            jnp.stack(ak_p), jnp.stack(av_p), jnp.stack(bk_p), jnp.stack(bv_p),
            jnp.stack(ak_s), jnp.stack(av_s), jnp.stack(bk_s), jnp.stack(bv_s))
```

```python
from contextlib import ExitStack
import types
import numpy as np
import concourse.bass as bass
import concourse.mybir as mybir
from concourse.bass_utils import run_bass_kernel_spmd

F32 = mybir.dt.float32
BF16 = mybir.dt.bfloat16
AF = mybir.ActivationFunctionType
ALU = mybir.AluOpType
AX = mybir.AxisListType

D = 1024
DFF = 2816
NFF = 22
SEQ = 2048
NCORES = 8
ALPHA = 2.0 ** 0.5
EPSP = 1e-5 / (ALPHA * ALPHA)
CF = 0.5 / ALPHA
CM = 1.0 / ALPHA
NEG = -30000.0
TW = 2080
FFG = [(0, 7), (7, 14), (14, 22)]


class Res:
    __slots__ = ("name", "w", "r")

    def __init__(self, name=""):
        self.name = name
        self.w = None
        self.r = []


def _freeze(fn):
    if fn is None or fn.__closure__ is None:
        return fn
    cells = []
    for c in fn.__closure__:
        try:
            cells.append(types.CellType(c.cell_contents))
        except ValueError:
            cells.append(c)
    return types.FunctionType(fn.__code__, fn.__globals__, fn.__name__, fn.__defaults__, tuple(cells))


class Prog:
    ENGS = ("pe", "act", "dve", "pool", "sp")
    EPOCH = 30000

    def __init__(self, nc):
        self.nc = nc
        self.es = ExitStack()
        self.lists = {e: [] for e in self.ENGS}
        self.sems = {}
        self.cnt = {}
        self.epoch = {e: 0 for e in self.ENGS}
        self.waited = {}
        self.pending = {e: False for e in self.ENGS}
        self.n_ops = {e: 0 for e in self.ENGS}
        for e in self.ENGS:
            self._mksem(("c", e, 0))

    def _mksem(self, key):
        name = "s_" + "_".join(str(k) for k in key)
        self.sems[key] = self.es.enter_context(self.nc.semaphore(name))
        self.cnt[key] = 0

    def sbuf(self, name, shape, dt):
        return self.es.enter_context(self.nc.sbuf_tensor(name, list(shape), dt))

    def psum(self, name, shape, dt):
        return self.es.enter_context(self.nc.psum_tensor(name, list(shape), dt))

    def _deps(self, eng, reads, writes, self_sync):
        evs = {}

        def add(ev):
            if ev is None:
                return
            k, v = ev
            if not self_sync and k[0] == "c" and k[1] == eng:
                return
            if evs.get(k, 0) < v:
                evs[k] = v
        for r in reads:
            add(r.w)
        for w in writes:
            add(w.w)
            for ev in w.r:
                add(ev)
        waits = []
        for k, v in evs.items():
            if self.waited.get((eng, k), 0) < v:
                self.waited[(eng, k)] = v
                waits.append((k, v))
        return waits

    def _update(self, ev, reads, writes):
        for r in reads:
            r.r.append(ev)
            if len(r.r) > 64:
                best = {}
                for k, v in r.r:
                    if best.get(k, 0) < v:
                        best[k] = v
                r.r = list(best.items())
        for w in writes:
            w.w = ev
            w.r = []

    def op(self, eng, fn, reads=(), writes=(), inc=True):
        waits = self._deps(eng, reads, writes, eng != "pe")
        key = ("c", eng, self.epoch[eng])
        if inc:
            self.cnt[key] += 1
            ev = (key, self.cnt[key])
            self.pending[eng] = False
        else:
            ev = (key, self.cnt[key] + 1)
            self.pending[eng] = True
        self._update(ev, reads, writes)
        self.lists[eng].append((waits, _freeze(fn), key if inc else None, 1))
        self.n_ops[eng] += 1
        if inc and self.cnt[key] >= self.EPOCH:
            self.epoch[eng] += 1
            self._mksem(("c", eng, self.epoch[eng]))
        return ev

    def dma(self, q, out, in_, reads=(), writes=(), sem=None):
        key = ("d", sem)
        if key not in self.sems:
            self._mksem(key)
        waits = self._deps(q, reads, writes, True)
        prev = self.cnt[key]
        if prev > 0 and self.waited.get((q, key), 0) < prev:
            self.waited[(q, key)] = prev
            waits.append((key, prev))
        self.cnt[key] += 16
        ev = (key, self.cnt[key])
        self._update(ev, reads, writes)
        fn = lambda e, out=out, in_=in_: e.dma_start(out=out, in_=in_)
        self.lists[q].append((waits, fn, key, 16))
        self.n_ops[q] += 1
        return ev

    def barrier(self):
        for e in self.ENGS:
            assert not self.pending[e]
        cur = [(k, v) for k, v in self.cnt.items() if v > 0]
        for e in self.ENGS:
            waits = []
            for k, v in cur:
                if self.waited.get((e, k), 0) < v:
                    self.waited[(e, k)] = v
                    waits.append((k, v))
            if waits:
                self.lists[e].append((waits, None, None, 0))

    def finish(self):
        self.barrier()
        block = self.es.enter_context(self.nc.Block())
        sems = self.sems

        def run(engine, lst):
            for waits, fn, inckey, n in lst:
                for k, v in waits:
                    engine.wait_ge(sems[k], v)
                if fn is not None:
                    ins = fn(engine)
                    if inckey is not None:
                        ins.then_inc(sems[inckey], n)

        @block.tensor
        def _(e):
            run(e, self.lists["pe"])

        @block.scalar
        def _(e):
            run(e, self.lists["act"])

        @block.vector
        def _(e):
            run(e, self.lists["dve"])

        @block.gpsimd
        def _(e):
            run(e, self.lists["pool"])

        @block.sync
        def _(e):
            run(e, self.lists["sp"])

        self.es.close()


class _Stop(Exception):
    pass


def build(do_sample=True, n_prompt=2, stop_after=None):
    nc = bass.Bass("TRN2", target_bir_lowering=False)
    P = Prog(nc)

    def ck(n):
        if stop_after == n:
            raise _Stop()

    def din(name, shape):
        return nc.dram_tensor(name, list(shape), F32, kind="ExternalInput").ap()

    def dout(name, shape):
        return nc.dram_tensor(name, list(shape), F32, kind="ExternalOutput").ap()

    xp_d = din("xp", [2, SEQ, D])
    xs_d = din("xs", [32, D])
    cak_d = din("cak", [2, 512, D])
    cav_d = din("cav", [2, 512, D])
    cbk_d = din("cbk", [2, 128, 256])
    cbv_d = din("cbv", [2, 128, 256])
    lng_d = din("lng", [6, D])
    lnb_d = din("lnb", [6, D])
    win_d = din("win", [4, D, 2 * DFF])
    wdn_d = din("wdn", [4, DFF, D])
    wqa_d = din("wqa", [D, 3 * D])
    woa_d = din("woa", [D, D])
    wqb_d = din("wqb", [D, 1536])
    wob_d = din("wob", [D, D])
    ext_d = din("ext", [16, 768])
    snk_d = din("snk", [1, 16])
    idn_d = din("idn", [128, 128])
    jrev_d = din("jrev", [128, 128])
    cosT_d = din("cosT", [128, SEQ])
    sinT_d = din("sinT", [128, SEQ])
    cosTs_d = din("cosTs", [128, 32])
    sinTs_d = din("sinTs", [128, 32])
    ctok_d = din("ctok", [128, 256])
    stok_d = din("stok", [128, 256])
    ctoks_d = din("ctoks", [16, 256])
    stoks_d = din("stoks", [16, 256])

    yp_d = dout("yp", [2, SEQ, D])
    ys_d = dout("ys", [32, D])
    akp_d = dout("akp", [2, 512, D])
    avp_d = dout("avp", [2, 512, D])
    bkp_d = dout("bkp", [2, 128, 256])
    bvp_d = dout("bvp", [2, 128, 256])
    aks_d = dout("aks", [32, D])
    avs_d = dout("avs", [32, D])
    bks_d = dout("bks", [32, 256])
    bvs_d = dout("bvs", [32, 256])

    X = P.sbuf("X", [128, 17, D], F32)
    XT = P.sbuf("XT", [128, 8, TW], BF16)
    SB = P.sbuf("SB", [128, 34688], BF16)
    SF = P.sbuf("SF", [128, 7776], F32)
    XB = P.sbuf("XB", [128, 2, D], BF16)
    IDENT = P.sbuf("IDENT", [128, 128], BF16)
    JREV = P.sbuf("JREV", [128, 128], BF16)
    ST = P.sbuf("ST", [128, 4, 2, 6], F32)
    MV = P.sbuf("MV", [128, 4, 4], F32)
    EPS = P.sbuf("EPS", [128, 1], F32)
    AM = P.sbuf("AM", [128, 4, 4], F32)
    SNK = P.sbuf("SNK", [128, 16], F32)
    PA = P.psum("PA", [128, 2, 1024], F32)
    PS1 = P.psum("PS1", [128, 2, 512], F32)
    PTR = P.psum("PTR", [128, 2, 1024], BF16)

    RX = [Res() for _ in range(17)]
    RXT = [Res() for _ in range(17)]
    RXB = [Res(), Res()]
    RST = [Res(), Res(), Res(), Res()]
    RPA = [Res(), Res()]
    RPS = [Res(), Res()]
    RPT = [Res(), Res()]
    RAM = [Res(), Res(), Res(), Res()]
    RCONST = Res()
    RLN = Res()
    cnt = {"pa": 0, "ps": 0, "pt": 0, "xb": 0, "st": 0, "am": 0, "sg": 0}

    def nxt(k, n=2):
        v = cnt[k]
        cnt[k] = (v + 1) % n
        return v

    def sbv(off, n):
        return SB[:, off:off + n]

    def sfv(off, n):
        return SF[:, off:off + n]

    P.dma("pool", IDENT[:], idn_d, writes=[RCONST], sem="c1")
    P.dma("pool", JREV[:], jrev_d, writes=[RCONST], sem="c2")
    P.dma("sp", SNK[:], bass.AP(tensor=snk_d.tensor, offset=0, ap=[[0, 128], [1, 16]]), writes=[RCONST], sem="c3")
    P.op("dve", lambda e: e.memset(EPS[:], EPSP), writes=[RCONST])

    def make_xT(t, ts):
        s = nxt("xb")
        P.op("act", lambda e: e.activation(out=XB[:ts, s, :], in_=X[:ts, t, :], func=AF.Copy),
             reads=[RX[t]], writes=[RXB[s]])
        b = nxt("pt")
        for c in range(8):
            P.op("pe", lambda e, c=c: e.transpose(out=PTR[:, b, c * 128:c * 128 + ts], in_=XB[:ts, s, c * 128:(c + 1) * 128],
                                                   identity=IDENT[:ts, :ts]),
                 reads=[RXB[s], RCONST], writes=[RPT[b]], inc=(c == 7))
        src = PTR[:, b, :].rearrange("p (c f) -> p c f", c=8)[:, :, 0:ts]
        P.op("act", lambda e: e.activation(out=XT[:, :, t * 128:t * 128 + ts], in_=src, func=AF.Copy),
             reads=[RPT[b]], writes=[RXT[t]])

    def load_ln(idx):
        P.dma("sp", SF[:, 0:1024], bass.AP(tensor=lng_d.tensor, offset=idx * D, ap=[[0, 128], [1, D]]), writes=[RLN], sem="lng")
        P.dma("sp", SF[:, 1024:2048], bass.AP(tensor=lnb_d.tensor, offset=idx * D, ap=[[0, 128], [1, D]]), writes=[RLN], sem="lnb")

    def ln_core(t, ts):
        s = nxt("st", 4)
        xt = X[:ts, t, :]
        P.op("dve", lambda e: e.bn_stats(out=ST[:ts, s, 0, :], in_=X[:ts, t, 0:512]), reads=[RX[t]], writes=[RST[s]])
        P.op("dve", lambda e: e.bn_stats(out=ST[:ts, s, 1, :], in_=X[:ts, t, 512:1024]), reads=[RX[t]], writes=[RST[s]])
        P.op("dve", lambda e: e.bn_aggr(out=MV[:ts, s, 0:2], in_=ST[:ts, s, :, :]), reads=[RST[s]], writes=[RST[s]])
        P.op("act", lambda e: e.activation(out=MV[:ts, s, 2:3], in_=MV[:ts, s, 1:2], func=AF.Sqrt, bias=EPS[:ts, :], scale=1.0),
             reads=[RST[s], RCONST], writes=[RST[s]])
        P.op("dve", lambda e: e.reciprocal(out=MV[:ts, s, 3:4], in_=MV[:ts, s, 2:3]), reads=[RST[s]], writes=[RST[s]])
        P.op("dve", lambda e: e.scalar_tensor_tensor(out=xt, in0=xt, scalar=MV[:ts, s, 0:1], in1=SF[:ts, 0:1024], op0=ALU.subtract, op1=ALU.mult),
             reads=[RST[s], RLN, RX[t]], writes=[RX[t]])
        P.op("dve", lambda e: e.scalar_tensor_tensor(out=xt, in0=xt, scalar=MV[:ts, s, 3:4], in1=SF[:ts, 1024:2048], op0=ALU.mult, op1=ALU.add),
             reads=[RST[s], RLN, RX[t]], writes=[RX[t]])

    def ln_tail(t, ts, out_ap=None, out_sem=None):
        if out_ap is not None:
            P.dma("sp", out_ap, X[:ts, t, :], reads=[RX[t]], sem=out_sem)
        else:
            make_xT(t, ts)

    def layer_norm(t, ts, out_ap=None, out_sem=None):
        ln_core(t, ts)
        ln_tail(t, ts, out_ap, out_sem)

    def ln_loop(NT, tsl):
        for t in range(NT + 1):
            if t < NT:
                ln_core(t, tsl if t == NT - 1 else 128)
            if t >= 1:
                ln_tail(t - 1, tsl if t - 1 == NT - 1 else 128)

    ACT_OFF, WI_OFF, WD_OFF, SG_OFF = 0, 16640, 24832, 33024
    RACT = [[Res() for _ in range(5)] for _ in range(8)]
    RWI = [Res(), Res()]
    RWD = [Res() for _ in range(8)]
    RSG = [Res(), Res()]
    wi_cnt = [0]

    def ffn(li, ln_idx, NT, ts, final_out=None):
        T = (NT - 1) * 128 + ts
        sts = [(o, min(512, T - o)) for o in range(0, T, 512)]
        actT = sbv(ACT_OFF, 8 * TW).rearrange("p (c t) -> p c t", c=8)
        wdv = sbv(WD_OFF, 8192).rearrange("p (c f) -> p c f", c=8)
        w_in = win_d[li].rearrange("(c p) f -> p c f", p=128)
        w_dn = wdn_d[li].rearrange("(c p) f -> p c f", p=128)
        load_ln(ln_idx)
        pend = []

        def ln_fin(t, tt):
            if final_out is not None:
                ln_tail(t, tt, out_ap=final_out(t, tt), out_sem=("yo", t % 2))
            else:
                ln_tail(t, tt)
        for gi, (j0, j1) in enumerate(FFG):
            ng = j1 - j0
            units = [(j, min(2, j1 - j)) for j in range(j0, j1, 2)]
            slots = []
            for _ in units:
                slots.append(wi_cnt[0] % 2)
                wi_cnt[0] += 1

            def wi_view(sl):
                return sbv(WI_OFF + sl * 4096, 4096).rearrange("p (c g f) -> p c g f", c=8, g=2)

            def wi_load(ui):
                ju, nu = units[ui]
                sl = slots[ui]
                wiv = wi_view(sl)
                P.dma("pool", wiv[:, :, 0, 0:nu * 128], w_in[:, :, ju * 128:(ju + nu) * 128], writes=[RWI[sl]], sem=("wi", sl, 0))
                P.dma("pool", wiv[:, :, 1, 0:nu * 128], w_in[:, :, DFF + ju * 128:DFF + (ju + nu) * 128], writes=[RWI[sl]], sem=("wi", sl, 1))
            for ui in range(min(2, len(units))):
                wi_load(ui)
            for jl in range(ng):
                P.dma("pool", wdv[:, jl, :], w_dn[:, j0 + jl, :], writes=[RWD[jl]], sem=("wd", jl))
            for ui, (ju, nu) in enumerate(units):
                sl = slots[ui]
                wiv = wi_view(sl)
                if ui >= 2:
                    wi_load(ui)
                for jj in range(nu):
                    jl = ju + jj - j0
                    for si, (so, sn) in enumerate(sts):
                        b = nxt("pa")
                        xtr = [RXT[t] for t in range(so // 128, (so + sn + 127) // 128)]
                        for half in range(2):
                            for k in range(8):
                                P.op("pe", lambda e, k=k, half=half, b=b, jj=jj, so=so, sn=sn, wiv=wiv:
                                     e.matmul(PA[:, b, half * 512:half * 512 + sn], lhsT=wiv[:, k, half, jj * 128:(jj + 1) * 128],
                                              rhs=XT[:, k, so:so + sn], start=(k == 0), stop=(k == 7)),
                                     reads=[RWI[sl]] + xtr, writes=[RPA[b]], inc=(half == 1 and k == 7))
                        sg = nxt("sg")
                        sgv = sbv(SG_OFF + sg * 512, 512)
                        P.op("act", lambda e, b=b, sn=sn, sgv=sgv: e.activation(out=sgv[:, 0:sn], in_=PA[:, b, 0:sn], func=AF.Silu),
                             reads=[RPA[b]], writes=[RSG[sg]])
                        P.op("dve", lambda e, b=b, sn=sn, sgv=sgv, jl=jl, so=so:
                             e.tensor_tensor(out=actT[:, jl, so:so + sn], in0=sgv[:, 0:sn], in1=PA[:, b, 512:512 + sn], op=ALU.mult),
                             reads=[RPA[b], RSG[sg]], writes=[RACT[jl][si]])
            for t in range(NT):
                tt = ts if t == NT - 1 else 128
                b = nxt("pa")
                for jl in range(ng):
                    for half in range(2):
                        P.op("pe", lambda e, jl=jl, half=half, b=b, t=t, tt=tt:
                             e.matmul(PA[:tt, b, half * 512:(half + 1) * 512], lhsT=actT[:, jl, t * 128:t * 128 + tt],
                                      rhs=wdv[:, jl, half * 512:(half + 1) * 512], start=(jl == 0), stop=(jl == ng - 1)),
                             reads=[RACT[jl][t // 4], RWD[jl]], writes=[RPA[b]], inc=(jl == ng - 1 and half == 1))
                P.op("dve", lambda e, b=b, t=t, tt=tt: e.scalar_tensor_tensor(out=X[:tt, t, :], in0=PA[:tt, b, :], scalar=CF, in1=X[:tt, t, :],
                                                                             op0=ALU.mult, op1=ALU.add),
                     reads=[RPA[b], RX[t]], writes=[RX[t]])
                if gi == len(FFG) - 1:
                    pend.append((t, tt))
                    if len(pend) >= 2:
                        ln_core(*pend[-2])
                    if len(pend) >= 3:
                        ln_fin(*pend[-3])
            if gi == len(FFG) - 1:
                if len(pend) >= 1:
                    ln_core(*pend[-1])
                if len(pend) >= 2:
                    ln_fin(*pend[-2])
                ln_fin(*pend[-1])

    att_q = []
    att_n = [0]

    def attend(nq, qT, ksegs, vblocks, bias_ap, nk, sink_h, o_out):
        n_ = att_n[0]
        att_n[0] += 1
        a = n_ % 4
        sbt, pbt, ptt, rsb, rpb, rptt = att_bufs[n_ % len(att_bufs)]
        r_qk, r_bias, r_v, w_o = list(att_reads), list(att_bias_reads), list(att_v_reads), list(att_o_writes)
        ne = nk + (1 if sink_h is not None else 0)
        nb = len(vblocks)

        def stage1a():
            b = nxt("pa")
            for i, (kap, off, n) in enumerate(ksegs):
                P.op("pe", lambda e, kap=kap, off=off, n=n: e.matmul(PA[:nq, b, off:off + n], lhsT=qT, rhs=kap, start=True, stop=True),
                     reads=r_qk, writes=[RPA[b]], inc=(i == len(ksegs) - 1))
            P.op("dve", lambda e: e.scalar_tensor_tensor(out=sbt[:nq, 0:ne], in0=PA[:nq, b, 0:ne], scalar=0.125, in1=bias_ap,
                                                         op0=ALU.mult, op1=ALU.add),
                 reads=[RPA[b]] + r_bias, writes=[rsb])

        def stage1b():
            P.op("dve", lambda e: e.tensor_reduce(out=AM[:nq, a, 1:2], in_=sbt[:nq, 0:ne], axis=AX.X, op=ALU.max, negate=True),
                 reads=[rsb], writes=[RAM[a]])
            P.op("act", lambda e: e.activation(out=pbt[:nq, 0:ne], in_=sbt[:nq, 0:ne], func=AF.Exp, bias=AM[:nq, a, 1:2], scale=1.0,
                                               accum_out=AM[:nq, a, 2:3]),
                 reads=[rsb, RAM[a]], writes=[rpb, RAM[a]])

        def stage2():
            tb = nxt("pt")
            off = 0
            for i, (vap, n) in enumerate(vblocks):
                P.op("pe", lambda e, i=i, off=off, n=n: e.transpose(out=PTR[:n, tb, i * 128:i * 128 + nq], in_=pbt[:nq, off:off + n],
                                                                      identity=IDENT[:nq, :nq]),
                     reads=[rpb, RCONST], writes=[RPT[tb]], inc=(i == nb - 1))
                off += n
            nfull = sum(1 for (_, n) in vblocks if n == 128)
            if nfull > 0:
                srcv = PTR[:, tb, 0:nfull * 128].rearrange("p (c f) -> p c f", c=nfull)[:, :, 0:nq]
                dstv = ptt[:, 0:nfull * 128].rearrange("p (c f) -> p c f", c=nfull)[:, :, 0:nq]
                P.op("act", lambda e: e.activation(out=dstv, in_=srcv, func=AF.Copy), reads=[RPT[tb]], writes=[rptt])
            for i, (vap, n) in enumerate(vblocks):
                if n != 128:
                    P.op("act", lambda e, i=i, n=n: e.activation(out=ptt[:n, i * 128:i * 128 + nq], in_=PTR[:n, tb, i * 128:i * 128 + nq], func=AF.Copy),
                         reads=[RPT[tb]], writes=[rptt])

        def stage3():
            ob = nxt("ps")
            for i, (vap, n) in enumerate(vblocks):
                P.op("pe", lambda e, i=i, vap=vap, n=n: e.matmul(PS1[:nq, ob, 0:64], lhsT=ptt[:n, i * 128:i * 128 + nq], rhs=vap,
                                                                 start=(i == 0), stop=(i == nb - 1)),
                     reads=[rptt] + r_v, writes=[RPS[ob]], inc=(i == nb - 1))
            P.op("dve", lambda e: e.reciprocal(out=AM[:nq, a, 3:4], in_=AM[:nq, a, 2:3]), reads=[RAM[a]], writes=[RAM[a]])
            P.op("dve", lambda e: e.tensor_scalar(out=o_out, in0=PS1[:nq, ob, 0:64], scalar1=AM[:nq, a, 3:4], scalar2=None, op0=ALU.mult),
                 reads=[RPS[ob], RAM[a]], writes=w_o)

        stage1a()
        att_q.append([stage1b, stage2, stage3])
        if len(att_q) >= 2:
            att_q[-2][0]()
        if len(att_q) >= 3:
            att_q[-3][1]()
        if len(att_q) >= 4:
            att_q[-4][2]()
            att_q.pop(0)

    def att_flush():
        k = len(att_q)
        done = [k - 1 - idx for idx in range(k)]
        while any(d < 3 for d in done):
            for idx in range(k - 1, -1, -1):
                if done[idx] < 3:
                    att_q[idx][done[idx]]()
                    done[idx] += 1
        att_q.clear()

    att_bufs = []
    att_reads = []
    att_bias_reads = []
    att_v_reads = []
    att_o_writes = []

    def out_proj_multi(tiles, OT_list, wo_list, r_ots, r_wos):
        n = len(OT_list)
        for (t, tt, co) in tiles:
            b = nxt("pa")
            for half in range(2):
                for i in range(n):
                    P.op("pe", lambda e, half=half, b=b, i=i, tt=tt, co=co: e.matmul(PA[:tt, b, half * 512:(half + 1) * 512], lhsT=OT_list[i][:, co:co + tt],
                                                                                     rhs=wo_list[i][:, half * 512:(half + 1) * 512], start=(i == 0), stop=(i == n - 1)),
                         reads=list(r_ots) + list(r_wos), writes=[RPA[b]], inc=(half == 1 and i == n - 1))
            P.op("dve", lambda e, b=b, t=t, tt=tt: e.scalar_tensor_tensor(out=X[:tt, t, :], in0=PA[:tt, b, :], scalar=CM, in1=X[:tt, t, :],
                                                                         op0=ALU.mult, op1=ALU.add),
                 reads=[RPA[b], RX[t]], writes=[RX[t]])

    def build_bias(h, dst, rdst, hk, rhk):
        hank, hi, lo = hk
        src = bass.AP(tensor=ext_d.tensor, offset=h * 768, ap=[[1, 128], [1, 640]])
        P.dma("sp", hank, src, writes=[rhk], sem="hank")
        P.op("act", lambda e: e.activation(out=hi, in_=hank, func=AF.Copy), reads=[rhk], writes=[rhk])
        P.op("dve", lambda e: e.tensor_tensor(out=lo, in0=hank, in1=hi, op=ALU.subtract), reads=[rhk], writes=[rhk])
        b = nxt("pa")
        for (o, n) in ((0, 512), (512, 128)):
            P.op("pe", lambda e, o=o, n=n: e.matmul(PA[:, b, o:o + n], lhsT=JREV[:], rhs=hi[:, o:o + n], start=True, stop=False),
                 reads=[rhk, RCONST], writes=[RPA[b]], inc=False)
            P.op("pe", lambda e, o=o, n=n: e.matmul(PA[:, b, o:o + n], lhsT=JREV[:], rhs=lo[:, o:o + n], start=False, stop=True),
                 reads=[rhk, RCONST], writes=[RPA[b]], inc=(o == 512))
        P.op("dve", lambda e: e.tensor_copy(out=dst, in_=PA[:, b, 0:640]), reads=[RPA[b]], writes=[rdst])

    def mixer_a_prompt(seq, ws=False):
        NT, ts, T = 16, 128, SEQ
        P.barrier()
        wq_v = [[sbv(s * 3072 + i * 1024, 1024).rearrange("p (c f) -> p c f", c=8) for i in range(3)] for s in range(2)]
        wo_v = [sbv(6144 + s * 1024, 1024) for s in range(2)]
        qT_v = [sbv(8192 + s * 2048, 2048) for s in range(2)]
        kT_v = [sbv(12288 + s * 2048, 2048) for s in range(2)]
        V_v = [sbv(16384 + s * 2048, 2048).rearrange("p (t f) -> p t f", t=16) for s in range(2)]
        O_v = sbv(20480, 2048).rearrange("p (t f) -> p t f", t=16)
        OT_v = [sbv(22528, 2048), sbv(24576, 2048)]
        att_bufs.clear()
        for a in range(3):
            att_bufs.append((sfv((2560, 3200, 5504)[a], 640), sbv(26624 + a * 640, 640), sbv(29824 + a * 640, 640), Res(), Res(), Res()))
        bias_v = [[sfv(s * 1280 + h2 * 640, 640) for h2 in range(2)] for s in range(2)]
        kst = sfv(3840, 512).rearrange("p (t f) -> p t f", t=4)
        vst = sfv(4352, 512).rearrange("p (t f) -> p t f", t=4)
        hk = (sfv(4864, 640), sbv(28544, 640), sbv(29184, 640))
        rhk = Res()
        RW = [Res(), Res()]
        RWO = [Res(), Res()]
        RQ = [Res(), Res()]
        RK = [Res(), Res()]
        RV = [Res(), Res()]
        RB = [[Res(), Res()], [Res(), Res()]]
        RO, RKST, RVST = Res(), Res(), Res()
        ROT = [Res(), Res()]
        wq3 = wqa_d.rearrange("(c p) f -> p c f", p=128)
        SO = 31744
        qTs, kTn = sbv(SO, 32), sbv(SO + 32, 32)
        kTs = [sbv(SO + 64 + s2 * 528, 528) for s2 in range(2)]
        vn = [sbv(SO + 1120 + s2 * 128, 128) for s2 in range(2)]
        Os = [sbv(SO + 1376 + s2 * 128, 128) for s2 in range(2)]
        OTs = [sbv(SO + 1632 + i * 32, 32) for i in range(2)]
        XBf = XB[:, :, :].rearrange("p a b -> p (a b)")
        kc = [XBf[:, s2 * 512:(s2 + 1) * 512].rearrange("p (t f) -> p t f", t=4) for s2 in range(2)]
        vc = [XBf[:, 1024 + s2 * 512:1024 + (s2 + 1) * 512].rearrange("p (t f) -> p t f", t=4) for s2 in range(2)]
        kst_s, vst_s = sfv(6784, 128), sfv(6912, 128)
        RQs, RKN, RKSTs, RVSTs = Res(), Res(), Res(), Res()
        ROTs = [Res(), Res()]
        RKSs, RKCs, RVCs, RVNs, ROs = ([Res(), Res()] for _ in range(5))

        def samp(hp):
            s = hp % 2
            for s2 in range(2):
                P.dma("pool", kc[s2], cak_d[s2, :, hp * 128:(hp + 1) * 128].rearrange("(t p) f -> p t f", p=128), writes=[RKCs[s2]], sem=("kca", s2))
                P.dma("pool", vc[s2], cav_d[s2, :, hp * 128:(hp + 1) * 128].rearrange("(t p) f -> p t f", p=128), writes=[RVCs[s2]], sem=("vca", s2))
            yield
            for i, (dstv, rr) in enumerate(((qTs, RQs), (kTn, RKN))):
                b = nxt("ps")
                for k in range(8):
                    P.op("pe", lambda e, k=k, b=b, i=i: e.matmul(PS1[:, b, 0:32], lhsT=wq_v[s][i][:, k, :], rhs=XT[:, k, 2048:2080], start=(k == 0), stop=(k == 7)),
                         reads=[RW[s], RXT[16]], writes=[RPS[b]], inc=(k == 7))
                P.op("act", lambda e, b=b, dstv=dstv: e.activation(out=dstv, in_=PS1[:, b, 0:32], func=AF.Copy), reads=[RPS[b]], writes=[rr])
                yield
            for s2 in range(2):
                tb = nxt("pt")
                for t in range(4):
                    P.op("pe", lambda e, tb=tb, t=t, s2=s2: e.transpose(out=PTR[:, tb, t * 128:(t + 1) * 128], in_=kc[s2][:, t, :], identity=IDENT[:]),
                         reads=[RKCs[s2], RCONST], writes=[RPT[tb]], inc=(t == 3))
                P.op("act", lambda e, tb=tb, s2=s2: e.activation(out=kTs[s2][:, 0:512], in_=PTR[:, tb, 0:512], func=AF.Copy), reads=[RPT[tb]], writes=[RKSs[s2]])
                P.op("act", lambda e, s2=s2: e.activation(out=kTs[s2][:, 512:528], in_=kTn[:, 16 * s2:16 * s2 + 16], func=AF.Copy), reads=[RKN], writes=[RKSs[s2]])
                yield
                for (wi_, stg, rstg, dst_d, semn) in ((2, vst_s, RVSTs, avs_d, "vsa"), (1, kst_s, RKSTs, aks_d, "ksa")):
                    b = nxt("ps")
                    for k in range(8):
                        P.op("pe", lambda e, k=k, b=b, wi_=wi_, s2=s2: e.matmul(PS1[:16, b, 0:128], lhsT=XT[:, k, 2048 + 16 * s2:2048 + 16 * s2 + 16], rhs=wq_v[s][wi_][:, k, :],
                                                                               start=(k == 0), stop=(k == 7)),
                             reads=[RW[s], RXT[16]], writes=[RPS[b]], inc=(k == 7))
                    P.op("dve", lambda e, b=b, stg=stg: e.tensor_copy(out=stg[:16, :], in_=PS1[:16, b, 0:128]), reads=[RPS[b]], writes=[rstg])
                    if wi_ == 2:
                        P.op("dve", lambda e, b=b, s2=s2: e.tensor_copy(out=vn[s2][:16, :], in_=PS1[:16, b, 0:128]), reads=[RPS[b]], writes=[RVNs[s2]])
                    P.dma("sp", dst_d[16 * s2:16 * s2 + 16, hp * 128:(hp + 1) * 128], stg[:16, :], reads=[rstg], sem=semn)
                    yield

        def prep(hp):
            s = hp % 2
            for i in range(3):
                P.dma("pool", wq_v[s][i], wq3[:, :, i * 1024 + hp * 128:i * 1024 + (hp + 1) * 128], writes=[RW[s]], sem=("wqa", s, i))
            for h2 in range(2):
                build_bias(2 * hp + h2, bias_v[s][h2], RB[s][h2], hk, rhk)
                bv = bias_v[s][h2]
                P.op("dve", lambda e, bv=bv: e.memset(bv[0:64, 576:640], NEG), writes=[RB[s][h2]])
                P.op("dve", lambda e, bv=bv: e.memset(bv[64:128, 0:64], NEG), writes=[RB[s][h2]])
            yield
            for st in range(4):
                xtr = [RXT[t] for t in range(4 * st, 4 * st + 4)]
                for i, (dstv, rr) in enumerate(((qT_v[s], RQ[s]), (kT_v[s], RK[s]))):
                    b = nxt("ps")
                    for k in range(8):
                        P.op("pe", lambda e, k=k, b=b, i=i, st=st: e.matmul(PS1[:, b, :], lhsT=wq_v[s][i][:, k, :], rhs=XT[:, k, st * 512:(st + 1) * 512],
                                                                            start=(k == 0), stop=(k == 7)),
                             reads=[RW[s]] + xtr, writes=[RPS[b]], inc=(k == 7))
                    if i == 0:
                        P.op("act", lambda e, b=b, dstv=dstv, st=st: e.activation(out=dstv[:, st * 512:(st + 1) * 512], in_=PS1[:, b, :], func=AF.Copy),
                             reads=[RPS[b]], writes=[rr])
                    else:
                        P.op("dve", lambda e, b=b, dstv=dstv, st=st: e.tensor_copy(out=dstv[:, st * 512:(st + 1) * 512], in_=PS1[:, b, :]),
                             reads=[RPS[b]], writes=[rr])
            yield
            for g4 in range(4):
                b = nxt("ps")
                for tt_ in range(4):
                    t = g4 * 4 + tt_
                    for k in range(8):
                        P.op("pe", lambda e, k=k, b=b, t=t, tt_=tt_: e.matmul(PS1[:, b, tt_ * 128:(tt_ + 1) * 128], lhsT=XT[:, k, t * 128:(t + 1) * 128],
                                                                              rhs=wq_v[s][2][:, k, :], start=(k == 0), stop=(k == 7)),
                             reads=[RW[s], RXT[t]], writes=[RPS[b]], inc=(tt_ == 3 and k == 7))
                P.op("act", lambda e, b=b, g4=g4: e.activation(out=V_v[s][:, g4 * 4:(g4 + 1) * 4, :], in_=PS1[:, b, :].rearrange("p (t f) -> p t f", t=4), func=AF.Copy),
                     reads=[RPS[b]], writes=[RV[s]])
                if g4 == 3:
                    P.op("dve", lambda e, b=b: e.tensor_copy(out=vst, in_=PS1[:, b, :].rearrange("p (t f) -> p t f", t=4)),
                         reads=[RPS[b], RV[s]], writes=[RVST])
                    P.dma("sp", avp_d[seq, :, hp * 128:(hp + 1) * 128].rearrange("(t p) f -> p t f", p=128), vst, reads=[RVST], sem="vst")
            b = nxt("ps")
            for tt_ in range(4):
                t = 12 + tt_
                for k in range(8):
                    P.op("pe", lambda e, k=k, b=b, t=t, tt_=tt_: e.matmul(PS1[:, b, tt_ * 128:(tt_ + 1) * 128], lhsT=XT[:, k, t * 128:(t + 1) * 128],
                                                                          rhs=wq_v[s][1][:, k, :], start=(k == 0), stop=(k == 7)),
                         reads=[RW[s], RXT[t]], writes=[RPS[b]], inc=(tt_ == 3 and k == 7))
            P.op("dve", lambda e, b=b: e.tensor_copy(out=kst, in_=PS1[:, b, :].rearrange("p (t f) -> p t f", t=4)), reads=[RPS[b]], writes=[RKST])
            P.dma("sp", akp_d[seq, :, hp * 128:(hp + 1) * 128].rearrange("(t p) f -> p t f", p=128), kst, reads=[RKST], sem="kst")
            yield

        def attn(hp, gen, pg=None):
            s = hp % 2
            sg = samp(hp) if ws else None
            for j in range(16):
                klo = max(0, (j - 4) * 128)
                khi = (j + 1) * 128
                nk = khi - klo
                c_lo = 640 - nk
                for h2 in range(2):
                    qT = qT_v[s][h2 * 64:(h2 + 1) * 64, j * 128:(j + 1) * 128]
                    ksegs = []
                    o = 0
                    while o < nk:
                        n = min(512 - (o % 512), nk - o)
                        ksegs.append((kT_v[s][h2 * 64:(h2 + 1) * 64, klo + o:klo + o + n], o, n))
                        o += n
                    vblocks = [(V_v[s][:, klo // 128 + i, h2 * 64:(h2 + 1) * 64], 128) for i in range(nk // 128)]
                    att_reads[:] = [RQ[s], RK[s]]
                    att_bias_reads[:] = [RB[s][h2]]
                    att_v_reads[:] = [RV[s]]
                    att_o_writes[:] = [RO]
                    attend(128, qT, ksegs, vblocks, bias_v[s][h2][:, c_lo:640], nk, None, O_v[:, j, h2 * 64:(h2 + 1) * 64])
                    if gen is not None:
                        next(gen, None)
                    if sg is not None:
                        next(sg, None)
                    if pg is not None:
                        next(pg, None)
            if pg is not None:
                for _ in pg:
                    pass
            if sg is not None:
                for _ in sg:
                    pass
                for s2 in range(2):
                    for h2 in range(2):
                        hs = slice(h2 * 64, (h2 + 1) * 64)
                        ksegs = [(kTs[s2][hs, 0:512], 0, 512), (kTs[s2][hs, 512:528], 512, 16)]
                        vblocks = [(vc[s2][:, t, hs], 128) for t in range(4)] + [(vn[s2][:16, hs], 16)]
                        att_reads[:] = [RQs, RKSs[s2]]
                        att_bias_reads[:] = [RB[s][h2]]
                        att_v_reads[:] = [RVCs[s2], RVNs[s2]]
                        att_o_writes[:] = [ROs[s2]]
                        attend(16, qTs[hs, 16 * s2:16 * s2 + 16], ksegs, vblocks, bias_v[s][h2][0:16, 0:528], 528, None, Os[s2][:16, hs])
            if gen is not None:
                for _ in gen:
                    pass
            att_flush()

        def post(hp):
            s = hp % 2
            for g8 in range(2):
                tb = nxt("pt")
                for i in range(8):
                    t = g8 * 8 + i
                    P.op("pe", lambda e, tb=tb, i=i, t=t: e.transpose(out=PTR[:, tb, i * 128:(i + 1) * 128], in_=O_v[:, t, :], identity=IDENT[:]),
                         reads=[RO, RCONST], writes=[RPT[tb]], inc=(i == 7))
                P.op("act", lambda e, tb=tb, g8=g8: e.activation(out=OT_v[s][:, g8 * 1024:(g8 + 1) * 1024], in_=PTR[:, tb, :], func=AF.Copy),
                     reads=[RPT[tb]], writes=[ROT[s]])
            if ws:
                for s2 in range(2):
                    tb = nxt("pt")
                    P.op("pe", lambda e, tb=tb, s2=s2: e.transpose(out=PTR[:, tb, 0:16], in_=Os[s2][:16, :], identity=IDENT[:16, :16]), reads=[ROs[s2], RCONST], writes=[RPT[tb]])
                    P.op("act", lambda e, tb=tb, s2=s2: e.activation(out=OTs[s][:, 16 * s2:16 * s2 + 16], in_=PTR[:, tb, 0:16], func=AF.Copy), reads=[RPT[tb]], writes=[ROTs[s]])
            yield
            if s == 1:
                for t in range(16):
                    out_proj_multi([(t, 128, t * 128)], OT_v, wo_v, ROT, RWO)
                    if t % 8 == 7:
                        yield
                if ws:
                    out_proj_multi([(16, 32, 0)], OTs, wo_v, ROTs, RWO)
                for i in range(2):
                    if hp + 1 + i < 8:
                        P.dma("pool", wo_v[i], woa_d[(hp + 1 + i) * 128:(hp + 2 + i) * 128, :], writes=[RWO[i]], sem=("woa", i))

        for i in range(2):
            P.dma("pool", wo_v[i], woa_d[i * 128:(i + 1) * 128, :], writes=[RWO[i]], sem=("woa", i))
        for _ in prep(0):
            pass
        pg = None
        for hp in range(8):
            gen = prep(hp + 1) if hp < 7 else None
            attn(hp, gen, pg)
            pg = post(hp)
        for _ in pg:
            pass
        P.barrier()

    def mixer_b_prompt(seq, ws=False, ws2=True):
        NT, ts, T = 16, 128, SEQ
        P.barrier()
        wk_raw = sbv(0, 2048).rearrange("p (c f) -> p c f", c=8)
        wv_raw = sbv(2048, 2048).rearrange("p (c f) -> p c f", c=8)
        kT_pair = [sbv(4096 + pr * 2048, 2048) for pr in range(2)]
        kp_raw = sbv(8192, 2048).rearrange("p (c f) -> p c f", c=8)
        kTn_pair = [sbv(10240 + pr * 32, 32) for pr in range(2)]
        RKP = [Res(), Res()]
        RKNP = [Res(), Res()]
        kT_v = [sbv(12288 + g * 2048, 2048) for g in range(4)]
        V_v = sbv(20480, 4096).rearrange("p (t f) -> p t f", t=16)
        wq_v = [sbv(24576 + s * 1024, 1024).rearrange("p (c f) -> p c f", c=8) for s in range(2)]
        wqp_v = [sbv(26624 + s * 1024, 1024).rearrange("p (c f) -> p c f", c=8) for s in range(2)]
        wo_v = [sbv(28672 + s * 1024, 1024) for s in range(2)]
        cosT = sfv(0, 2048)
        sinT = sfv(2048, 2048)
        tmp = [sfv(4096 + i * 512, 512) for i in range(2)]
        maskB = sfv(5120, 256)
        kst = sfv(5376, 256)
        vst = sfv(5632, 256)
        ctok = sfv(5888, 256)
        stok = sfv(6144, 256)
        ktmp = sfv(6400, 256)
        RWK, RWV, RKD, RKT, RVV, RTAB, RTMP, RMASK = Res(), Res(), Res(), [Res() for _ in range(4)], Res(), Res(), [Res(), Res()], Res()
        RKST, RVST = Res(), Res()
        wq3 = wqb_d.rearrange("(c p) f -> p c f", p=128)
        P.dma("pool", wk_raw, wq3[:, :, 1024:1280], writes=[RWK], sem="wkb")
        P.dma("pool", wv_raw, wq3[:, :, 1280:1536], writes=[RWV], sem="wvb")
        P.dma("sp", cosT, cosT_d, writes=[RTAB], sem="t1")
        P.dma("sp", sinT, sinT_d, writes=[RTAB], sem="t2")
        P.dma("sp", ctok, ctok_d, writes=[RTAB], sem="t3")
        P.dma("sp", stok, stok_d, writes=[RTAB], sem="t4")
        P.op("dve", lambda e: e.memset(maskB, 0.0), writes=[RMASK])
        P.op("dve", lambda e: e.memset(maskB[0:64, 192:256], NEG), writes=[RMASK])
        P.op("dve", lambda e: e.memset(maskB[64:128, 0:64], NEG), writes=[RMASK])
        for g in range(4):
            P.op("act", lambda e, g=g: e.activation(out=kp_raw[:, :, g * 64:g * 64 + 32], in_=wk_raw[:, :, g * 64 + 32:g * 64 + 64], func=AF.Copy),
                 reads=[RWK], writes=[RKD])
            P.op("act", lambda e, g=g: e.activation(out=kp_raw[:, :, g * 64 + 32:g * 64 + 64], in_=wk_raw[:, :, g * 64:g * 64 + 32], func=AF.Copy),
                 reads=[RWK], writes=[RKD])

        def rope_proj(w_a, w_b, rw, dst, rdst, ncols, cT, sT, xcols, tco=None, dco=None):
            so, sn = xcols
            tco = so if tco is None else tco
            dco = so if dco is None else dco
            xtr = [RXT[t] for t in range(so // 128, (so + sn + 127) // 128)]
            bb = []
            for w in (w_a, w_b):
                b = nxt("ps")
                bb.append(b)
                for k in range(8):
                    P.op("pe", lambda e, k=k, b=b, w=w: e.matmul(PS1[:, b, 0:sn], lhsT=w[:, k, :], rhs=XT[:, k, so:so + sn], start=(k == 0), stop=(k == 7)),
                         reads=rw + xtr, writes=[RPS[b]], inc=(k == 7))
            P.op("dve", lambda e: e.tensor_tensor(out=tmp[0][:, 0:sn], in0=PS1[:, bb[0], 0:sn], in1=cT[:, tco:tco + sn], op=ALU.mult),
                 reads=[RPS[bb[0]], RTAB], writes=[RTMP[0]])
            P.op("dve", lambda e: e.tensor_tensor(out=tmp[1][:, 0:sn], in0=PS1[:, bb[1], 0:sn], in1=sT[:, tco:tco + sn], op=ALU.mult),
                 reads=[RPS[bb[1]], RTAB], writes=[RTMP[1]])
            P.op("pool", lambda e: e.tensor_tensor(out=dst[:, dco:dco + sn], in0=tmp[0][:, 0:sn], in1=tmp[1][:, 0:sn], op=ALU.add),
                 reads=[RTMP[0], RTMP[1]], writes=[rdst])

        for pr in range(2):
            for st in range(4):
                rope_proj(wk_raw[:, :, pr * 128:(pr + 1) * 128], kp_raw[:, :, pr * 128:(pr + 1) * 128], [RWK, RKD], kT_pair[pr], RKP[pr], 128, cosT, sinT,
                          (st * 512, 512))
        for g in range(4):
            src = kT_pair[g // 2][(g % 2) * 64:(g % 2) * 64 + 64, :]
            for hh in range(2):
                P.dma("sp", kT_v[g][hh * 64:(hh + 1) * 64, :], src, reads=[RKP[g // 2]], writes=[RKT[g]], sem=("kdup", g, hh))
        for t2 in range(8):
            b = nxt("ps")
            for i in range(2):
                t = t2 * 2 + i
                for k in range(8):
                    P.op("pe", lambda e, k=k, b=b, t=t, i=i: e.matmul(PS1[:, b, i * 256:(i + 1) * 256], lhsT=XT[:, k, t * 128:(t + 1) * 128], rhs=wv_raw[:, k, :],
                                                                      start=(k == 0), stop=(k == 7)),
                         reads=[RWV, RXT[t]], writes=[RPS[b]], inc=(i == 1 and k == 7))
            P.op("act", lambda e, b=b, t2=t2: e.activation(out=V_v[:, t2 * 2:t2 * 2 + 2, :], in_=PS1[:, b, :].rearrange("p (t f) -> p t f", t=2), func=AF.Copy),
                 reads=[RPS[b]], writes=[RVV])
            if t2 == 7:
                P.op("dve", lambda e, b=b: e.tensor_copy(out=vst, in_=PS1[:, b, 256:512]), reads=[RPS[b], RVV], writes=[RVST])
                P.dma("sp", bvp_d[seq], vst, reads=[RVST], sem="vstb")
        b = nxt("ps")
        for k in range(8):
            P.op("pe", lambda e, k=k, b=b: e.matmul(PS1[:, b, 0:256], lhsT=XT[:, k, 15 * 128:16 * 128], rhs=wk_raw[:, k, :], start=(k == 0), stop=(k == 7)),
                 reads=[RWK, RXT[15]], writes=[RPS[b]], inc=(k == 7))
        rope_tok(128, PS1[:, b, 0:256], RPS[b], ctok, stok, RTAB, kst, ktmp, RKST)
        P.dma("sp", bkp_d[seq], kst, reads=[RKST], sem="kstb")
        cosTs, sinTs = sfv(7712, 32), sfv(7744, 32)
        XBf = XB[:, :, :].rearrange("p a b -> p (a b)")
        kTn_s = [XBf[:, g * 32:(g + 1) * 32] for g in range(4)]
        kTs_s = [[XBf[:, 128 + (s2 * 4 + g) * 144:128 + (s2 * 4 + g + 1) * 144] for g in range(4)] for s2 in range(2)]
        vc_s = [XBf[:, 1280 + s2 * 256:1280 + (s2 + 1) * 256] for s2 in range(2)]
        vn_s = [sbv(32896 + s2 * 256, 256) for s2 in range(2)]
        kcd = sbv(33408, 128)
        RKNs = [Res() for _ in range(4)]
        RKSs = [[Res() for _ in range(4)] for _ in range(2)]
        RVCs, RVNs, ROs = [Res(), Res()], [Res(), Res()], [Res(), Res()]
        RKCD, RQs = Res(), Res()
        ROTs = [Res(), Res()]
        if ws:
            P.dma("sp", cosTs, cosTs_d, writes=[RTAB], sem="t1")
            P.dma("sp", sinTs, sinTs_d, writes=[RTAB], sem="t2")
            P.dma("sp", ctok[:16, :], ctoks_d, writes=[RTAB], sem="t3")
            P.dma("sp", stok[:16, :], stoks_d, writes=[RTAB], sem="t4")
            for pr in range(2):
                rope_proj(wk_raw[:, :, pr * 128:(pr + 1) * 128], kp_raw[:, :, pr * 128:(pr + 1) * 128], [RWK, RKD], kTn_pair[pr], RKNP[pr], 128, cosTs, sinTs,
                          (2048, 32), tco=0, dco=0)
            for g in range(4):
                srcn = kTn_pair[g // 2][(g % 2) * 64:(g % 2) * 64 + 64, :]
                for hh in range(2):
                    P.dma("sp", kTn_s[g][hh * 64:(hh + 1) * 64, :], srcn, reads=[RKNP[g // 2]], writes=[RKNs[g]], sem=("kdupn", g, hh))
            for s2 in range(2):
                P.dma("pool", vc_s[s2], cbv_d[s2], writes=[RVCs[s2]], sem=("vcb", s2))
                for g in range(4):
                    for hh in range(2):
                        P.dma("pool", kcd[:, hh * 64:(hh + 1) * 64], cbk_d[s2, :, g * 64:(g + 1) * 64], writes=[RKCD], sem=("kcb", hh))
                    tb = nxt("pt")
                    P.op("pe", lambda e, tb=tb: e.transpose(out=PTR[:, tb, 0:128], in_=kcd, identity=IDENT[:]), reads=[RKCD, RCONST], writes=[RPT[tb]])
                    P.op("act", lambda e, tb=tb, s2=s2, g=g: e.activation(out=kTs_s[s2][g][:, 0:128], in_=PTR[:, tb, 0:128], func=AF.Copy), reads=[RPT[tb]], writes=[RKSs[s2][g]])
                    P.op("act", lambda e, s2=s2, g=g: e.activation(out=kTs_s[s2][g][:, 128:144], in_=kTn_s[g][:, 16 * s2:16 * s2 + 16], func=AF.Copy), reads=[RKNs[g]], writes=[RKSs[s2][g]])
                b = nxt("ps")
                for k in range(8):
                    P.op("pe", lambda e, k=k, b=b, s2=s2: e.matmul(PS1[:16, b, 0:256], lhsT=XT[:, k, 2048 + 16 * s2:2048 + 16 * s2 + 16], rhs=wv_raw[:, k, :], start=(k == 0), stop=(k == 7)),
                         reads=[RWV, RXT[16]], writes=[RPS[b]], inc=(k == 7))
                P.op("dve", lambda e, b=b: e.tensor_copy(out=vst[:16, :], in_=PS1[:16, b, 0:256]), reads=[RPS[b]], writes=[RVST])
                P.op("dve", lambda e, b=b, s2=s2: e.tensor_copy(out=vn_s[s2][:16, :], in_=PS1[:16, b, 0:256]), reads=[RPS[b]], writes=[RVNs[s2]])
                P.dma("sp", bvs_d[16 * s2:16 * s2 + 16, :], vst[:16, :], reads=[RVST], sem="vsb")
                b = nxt("ps")
                for k in range(8):
                    P.op("pe", lambda e, k=k, b=b, s2=s2: e.matmul(PS1[:16, b, 0:256], lhsT=XT[:, k, 2048 + 16 * s2:2048 + 16 * s2 + 16], rhs=wk_raw[:, k, :], start=(k == 0), stop=(k == 7)),
                         reads=[RWK, RXT[16]], writes=[RPS[b]], inc=(k == 7))
                rope_tok(16, PS1[:16, b, 0:256], RPS[b], ctok, stok, RTAB, kst, ktmp, RKST)
                P.dma("sp", bks_d[16 * s2:16 * s2 + 16, :], kst[:16, :], reads=[RKST], sem="ksb")
        P.barrier()
        qTs = sbv(0, 32)
        Os_s = [sbv(64 + s2 * 128, 128) for s2 in range(2)]
        OTs = [sbv(320, 32), sbv(352, 32)]
        ws = ws and ws2
        maskH = [[sfv(5376 + (s_ * 2 + h2) * 257, 257) for h2 in range(2)] for s_ in range(2)]
        zbH = [sfv(6404 + h2 * 145, 145) for h2 in range(2)]
        RMH = [[Res(), Res()], [Res(), Res()]]
        RZB = [Res(), Res()]
        kz = sbv(400, 2)
        RKZ = Res()
        P.op("pool", lambda e: e.memset(kz, 0.0), writes=[RKZ])
        for s_ in range(2):
            for h2 in range(2):
                P.op("pool", lambda e, s_=s_, h2=h2: e.tensor_copy(out=maskH[s_][h2][:, 0:256], in_=maskB), reads=[RMASK], writes=[RMH[s_][h2]])
        if ws:
            for h2 in range(2):
                P.op("pool", lambda e, h2=h2: e.memset(zbH[h2], 0.0), writes=[RZB[h2]])
        qT_v = [sbv(4096 + s * 2048, 2048) for s in range(2)]
        O_v = sbv(8192, 2048).rearrange("p (t f) -> p t f", t=16)
        OT_v = [sbv(10240, 2048), sbv(2048, 2048)]
        att_bufs.clear()
        for a in range(3):
            att_bufs.append((sfv(6700 + a * 257, 257), sbv(30720 + a * 288, 288), sbv(31872 + a * 256, 256), Res(), Res(), Res()))
        RW = [Res(), Res()]
        RWP = [Res(), Res()]
        RWO = [Res(), Res()]
        RQ = [Res(), Res()]
        RO = Res()
        ROT = [Res(), Res()]
        def prep(hp):
            s = hp % 2
            P.dma("pool", wq_v[s], wq3[:, :, hp * 128:(hp + 1) * 128], writes=[RW[s]], sem=("wqb", s))
            for h2 in range(2):
                P.op("act", lambda e, h2=h2: e.activation(out=maskH[s][h2][:, 256:257], in_=SNK[:, 2 * hp + h2:2 * hp + h2 + 1], func=AF.Copy),
                     reads=[RCONST], writes=[RMH[s][h2]])
            for hh in range(2):
                P.op("act", lambda e, hh=hh: e.activation(out=wqp_v[s][:, :, hh * 64:hh * 64 + 32], in_=wq_v[s][:, :, hh * 64 + 32:hh * 64 + 64], func=AF.Copy),
                     reads=[RW[s]], writes=[RWP[s]])
                P.op("act", lambda e, hh=hh: e.activation(out=wqp_v[s][:, :, hh * 64 + 32:hh * 64 + 64], in_=wq_v[s][:, :, hh * 64:hh * 64 + 32], func=AF.Copy),
                     reads=[RW[s]], writes=[RWP[s]])
            yield
            for st in range(4):
                rope_proj(wq_v[s], wqp_v[s], [RW[s], RWP[s]], qT_v[s], RQ[s], 128, cosT, sinT, (st * 512, 512))
            yield

        def attn(hp, gen, pg=None):
            s = hp % 2
            g = hp // 2
            if ws:
                for h2 in range(2):
                    P.op("act", lambda e, h2=h2: e.activation(out=zbH[h2][:16, 144:145], in_=SNK[:16, 2 * hp + h2:2 * hp + h2 + 1], func=AF.Copy),
                         reads=[RCONST], writes=[RZB[h2]])
                rope_proj(wq_v[s], wqp_v[s], [RW[s], RWP[s]], qTs, RQs, 128, cosTs, sinTs, (2048, 32), tco=0, dco=0)
            for j in range(16):
                klo = max(0, (j - 1) * 128)
                khi = (j + 1) * 128
                nk = khi - klo
                c_lo = 256 - nk
                for h2 in range(2):
                    qT = qT_v[s][h2 * 64:(h2 + 1) * 64, j * 128:(j + 1) * 128]
                    ksegs = [(kT_v[g][h2 * 64:(h2 + 1) * 64, klo:khi], 0, nk), (kz[h2 * 64:(h2 + 1) * 64, 0:1], nk, 1)]
                    vblocks = [(V_v[:, klo // 128 + i, g * 64:(g + 1) * 64], 128) for i in range(nk // 128)]
                    att_reads[:] = [RQ[s], RKT[g], RKZ]
                    att_bias_reads[:] = [RMH[s][h2]]
                    att_v_reads[:] = [RVV]
                    att_o_writes[:] = [RO]
                    attend(128, qT, ksegs, vblocks, maskH[s][h2][:, c_lo:257], nk, 2 * hp + h2, O_v[:, j, h2 * 64:(h2 + 1) * 64])
                    if gen is not None and (j * 2 + h2) % 4 == 3:
                        next(gen, None)
                    if pg is not None:
                        next(pg, None)
            if pg is not None:
                for _ in pg:
                    pass
            if ws:
                for s2 in range(2):
                    for h2 in range(2):
                        hs = slice(h2 * 64, (h2 + 1) * 64)
                        gs = slice(g * 64, (g + 1) * 64)
                        ksegs = [(kTs_s[s2][g][hs, 0:144], 0, 144), (kz[hs, 0:1], 144, 1)]
                        vblocks = [(vc_s[s2][:, gs], 128), (vn_s[s2][:16, gs], 16)]
                        att_reads[:] = [RQs, RKSs[s2][g], RKZ]
                        att_bias_reads[:] = [RZB[h2]]
                        att_v_reads[:] = [RVCs[s2], RVNs[s2]]
                        att_o_writes[:] = [ROs[s2]]
                        attend(16, qTs[hs, 16 * s2:16 * s2 + 16], ksegs, vblocks, zbH[h2][0:16, 0:145], 144, 2 * hp + h2, Os_s[s2][:16, hs])
            if gen is not None:
                for _ in gen:
                    pass
            att_flush()

        def post(hp):
            s = hp % 2
            for g8 in range(2):
                tb = nxt("pt")
                for i in range(8):
                    t = g8 * 8 + i
                    P.op("pe", lambda e, tb=tb, i=i, t=t: e.transpose(out=PTR[:, tb, i * 128:(i + 1) * 128], in_=O_v[:, t, :], identity=IDENT[:]),
                         reads=[RO, RCONST], writes=[RPT[tb]], inc=(i == 7))
                P.op("act", lambda e, tb=tb, g8=g8: e.activation(out=OT_v[s][:, g8 * 1024:(g8 + 1) * 1024], in_=PTR[:, tb, :], func=AF.Copy),
                     reads=[RPT[tb]], writes=[ROT[s]])
            if ws:
                for s2 in range(2):
                    tb = nxt("pt")
                    P.op("pe", lambda e, tb=tb, s2=s2: e.transpose(out=PTR[:, tb, 0:16], in_=Os_s[s2][:16, :], identity=IDENT[:16, :16]), reads=[ROs[s2], RCONST], writes=[RPT[tb]])
                    P.op("act", lambda e, tb=tb, s2=s2: e.activation(out=OTs[s][:, 16 * s2:16 * s2 + 16], in_=PTR[:, tb, 0:16], func=AF.Copy), reads=[RPT[tb]], writes=[ROTs[s]])
            yield
            if s == 1:
                for t in range(16):
                    out_proj_multi([(t, 128, t * 128)], OT_v, wo_v, ROT, RWO)
                    if t % 8 == 7:
                        yield
                if ws:
                    out_proj_multi([(16, 32, 0)], OTs, wo_v, ROTs, RWO)
                for i in range(2):
                    if hp + 1 + i < 8:
                        P.dma("pool", wo_v[i], wob_d[(hp + 1 + i) * 128:(hp + 2 + i) * 128, :], writes=[RWO[i]], sem=("wob", i))

        for i in range(2):
            P.dma("pool", wo_v[i], wob_d[i * 128:(i + 1) * 128, :], writes=[RWO[i]], sem=("wob", i))
        for _ in prep(0):
            pass
        pg = None
        for hp in range(8):
            gen = prep(hp + 1) if hp < 7 else None
            attn(hp, gen, pg)
            pg = post(hp)
        for _ in pg:
            pass
        P.barrier()

    def rope_tok(n, kps, rkps, ctok, stok, rtab, dst, ktmp, rdst):
        k4 = kps.rearrange("p (h a d) -> p h a d", h=4, a=2)
        s4 = stok[:n, :].rearrange("p (h a d) -> p h a d", h=4, a=2)
        t4 = ktmp[:n, :].rearrange("p (h a d) -> p h a d", h=4, a=2)
        P.op("dve", lambda e: e.tensor_tensor(out=dst[:n, :], in0=kps, in1=ctok[:n, :], op=ALU.mult), reads=[rkps, rtab], writes=[rdst])
        P.op("dve", lambda e: e.tensor_tensor(out=t4[:, :, 0, :], in0=k4[:, :, 1, :], in1=s4[:, :, 0, :], op=ALU.mult), reads=[rkps, rtab], writes=[rdst])
        P.op("dve", lambda e: e.tensor_tensor(out=t4[:, :, 1, :], in0=k4[:, :, 0, :], in1=s4[:, :, 1, :], op=ALU.mult), reads=[rkps, rtab], writes=[rdst])
        P.op("dve", lambda e: e.tensor_tensor(out=dst[:n, :], in0=dst[:n, :], in1=ktmp[:n, :], op=ALU.add), reads=[rdst], writes=[rdst])

    def load_x(src, NT, ts, t0=0):
        for t in range(NT):
            tt = ts if t == NT - 1 else 128
            P.dma("sp", X[:tt, t0 + t, :], src[t * 128:t * 128 + tt, :], writes=[RX[t0 + t]], sem=("xin", t % 4))
            make_xT(t0 + t, tt)

    def prompt_pass(seq, ws):
        NT, tsl = (17, 32) if ws else (16, 128)
        load_x(xp_d[seq], 16, 128)
        if ws:
            load_x(xs_d, 1, 32, t0=16)

        def fo(t, tt):
            return ys_d[0:32, :] if t == 16 else yp_d[seq, t * 128:t * 128 + tt, :]
        ffn(0, 0, NT, tsl)
        mixer_a_prompt(seq, ws and stop_after != 'wsb')
        load_ln(1)
        ln_loop(NT, tsl)
        ffn(1, 2, NT, tsl)
        ffn(2, 3, NT, tsl)
        mixer_b_prompt(seq, ws and stop_after != 'wsa', ws2=(stop_after != 'wsb1'))
        load_ln(4)
        ln_loop(NT, tsl)
        ffn(3, 5, NT, tsl, final_out=fo)

    try:
        for seq in range(n_prompt):
            prompt_pass(seq, do_sample and seq == 0)
    except _Stop:
        pass

    P.finish()
    return nc, P


def _rope_tables():
    half = 32
    inv = (10000.0 ** (-np.arange(half, dtype=np.float32) / half)).astype(np.float32)

    def feat(pos):
        ang = pos.astype(np.float32)[None, :] * inv[:, None]
        c = np.cos(ang).astype(np.float32)
        s = np.sin(ang).astype(np.float32)
        cT = np.concatenate([c, c, c, c], axis=0)
        sT = np.concatenate([-s, s, -s, s], axis=0)
        return np.ascontiguousarray(cT), np.ascontiguousarray(sT)

    def tok(pos):
        ang = pos.astype(np.float32)[:, None] * inv[None, :]
        c = np.cos(ang).astype(np.float32)
        s = np.sin(ang).astype(np.float32)
        ct = np.tile(np.concatenate([c, c], axis=1), (1, 4))
        st = np.tile(np.concatenate([-s, s], axis=1), (1, 4))
        return np.ascontiguousarray(ct), np.ascontiguousarray(st)
    return feat, tok


_CACHE = {}


def kernel(x_prompt, x_sample, cache_a_k, cache_a_v, cache_b_k, cache_b_v, ln_g, ln_b, w_ffn_in, w_ffn_down,
           w_qkv_a, w_o_a, rel_bias_a, w_qkv_b, w_o_b, sinks_b):
    f = lambda a: np.ascontiguousarray(np.asarray(a, dtype=np.float32))
    if "nc" not in _CACHE:
        _CACHE["nc"] = build()[0]
    nc = _CACHE["nc"]
    feat, tok = _rope_tables()
    cosT, sinT = feat(np.arange(SEQ))
    spos = 2048 + np.concatenate([np.arange(16), np.arange(16)])
    cosTs, sinTs = feat(spos)
    ctok, stok = tok(np.arange(SEQ - 128, SEQ))
    ctoks, stoks = tok(2048 + np.arange(16))
    idx = np.clip(639 - np.arange(768), -128, 128) + 128
    ext = f(np.asarray(rel_bias_a)[0][:, idx])
    shared = {
        "lng": f(np.asarray(ln_g).reshape(6, D)), "lnb": f(np.asarray(ln_b).reshape(6, D)),
        "win": f(np.asarray(w_ffn_in).reshape(4, D, 2 * DFF)), "wdn": f(np.asarray(w_ffn_down).reshape(4, DFF, D)),
        "wqa": f(np.asarray(w_qkv_a)[0]), "woa": f(np.asarray(w_o_a)[0]), "wqb": f(np.asarray(w_qkv_b)[0]), "wob": f(np.asarray(w_o_b)[0]),
        "ext": ext, "snk": f(np.asarray(sinks_b).reshape(1, 16)),
        "idn": np.eye(128, dtype=np.float32), "jrev": np.ascontiguousarray(np.eye(128, dtype=np.float32)[::-1]),
        "cosT": cosT, "sinT": sinT, "cosTs": cosTs, "sinTs": sinTs, "ctok": ctok, "stok": stok, "ctoks": ctoks, "stoks": stoks,
    }
    xp = np.asarray(x_prompt, dtype=np.float32)
    xs = np.asarray(x_sample, dtype=np.float32)
    cak = np.asarray(cache_a_k, dtype=np.float32)[0].reshape(16, 512, D)
    cav = np.asarray(cache_a_v, dtype=np.float32)[0].reshape(16, 512, D)
    cbk = np.asarray(cache_b_k, dtype=np.float32)[0].reshape(16, 128, 256)
    cbv = np.asarray(cache_b_v, dtype=np.float32)[0].reshape(16, 128, 256)
    in_maps = []
    for c in range(NCORES):
        m = dict(shared)
        sl = slice(2 * c, 2 * c + 2)
        m["xp"] = f(xp[sl])
        m["xs"] = f(xs[sl].reshape(32, D))
        m["cak"] = f(cak[sl])
        m["cav"] = f(cav[sl])
        m["cbk"] = f(cbk[sl])
        m["cbv"] = f(cbv[sl])
        in_maps.append(m)
    res = run_bass_kernel_spmd(nc, in_maps, core_ids=list(range(NCORES)))
    R = res.results
    cat = lambda k: np.concatenate([r[k] for r in R], axis=0)
    yp = cat("yp")
    ys = cat("ys").reshape(16, 16, D)
    akp = cat("akp").reshape(1, 16, 512, 16, 64)
    avp = cat("avp").reshape(1, 16, 512, 16, 64)
    bkp = cat("bkp").reshape(1, 16, 128, 4, 64)
    bvp = cat("bvp").reshape(1, 16, 128, 4, 64)
    aks = cat("aks").reshape(1, 16, 16, 16, 64)
    avs = cat("avs").reshape(1, 16, 16, 16, 64)
    bks = cat("bks").reshape(1, 16, 16, 4, 64)
    bvs = cat("bvs").reshape(1, 16, 16, 4, 64)
    return (yp, ys, akp, avp, bkp, bvp, aks, avs, bks, bvs)
```

```python
from contextlib import ExitStack
import types
import numpy as np
import concourse.bass as bass
import concourse.mybir as mybir
from concourse.bass_utils import run_bass_kernel_spmd

F32 = mybir.dt.float32
BF16 = mybir.dt.bfloat16
AF = mybir.ActivationFunctionType
ALU = mybir.AluOpType
AX = mybir.AxisListType

D = 1024
DFF = 2816
NFF = 22
SEQ = 2048
NCORES = 8
ALPHA = 2.0 ** 0.5
EPSP = 1e-5 / (ALPHA * ALPHA)
CF = 0.5 / ALPHA
CM = 1.0 / ALPHA
NEG = -30000.0
TW = 2080
FFG = [(0, 7), (7, 14), (14, 22)]


class Res:
    __slots__ = ("name", "w", "r")

    def __init__(self, name=""):
        self.name = name
        self.w = None
        self.r = []


def _freeze(fn):
    if fn is None or fn.__closure__ is None:
        return fn
    cells = []
    for c in fn.__closure__:
        try:
            cells.append(types.CellType(c.cell_contents))
        except ValueError:
            cells.append(c)
    return types.FunctionType(fn.__code__, fn.__globals__, fn.__name__, fn.__defaults__, tuple(cells))


class Prog:
    ENGS = ("pe", "act", "dve", "pool", "sp")
    EPOCH = 30000

    def __init__(self, nc):
        self.nc = nc
        self.es = ExitStack()
        self.lists = {e: [] for e in self.ENGS}
        self.sems = {}
        self.cnt = {}
        self.epoch = {e: 0 for e in self.ENGS}
        self.waited = {}
        self.pending = {e: False for e in self.ENGS}
        self.n_ops = {e: 0 for e in self.ENGS}
        for e in self.ENGS:
            self._mksem(("c", e, 0))

    def _mksem(self, key):
        name = "s_" + "_".join(str(k) for k in key)
        self.sems[key] = self.es.enter_context(self.nc.semaphore(name))
        self.cnt[key] = 0

    def sbuf(self, name, shape, dt):
        return self.es.enter_context(self.nc.sbuf_tensor(name, list(shape), dt))

    def psum(self, name, shape, dt):
        return self.es.enter_context(self.nc.psum_tensor(name, list(shape), dt))

    def _deps(self, eng, reads, writes, self_sync):
        evs = {}

        def add(ev):
            if ev is None:
                return
            k, v = ev
            if not self_sync and k[0] == "c" and k[1] == eng:
                return
            if evs.get(k, 0) < v:
                evs[k] = v
        for r in reads:
            add(r.w)
        for w in writes:
            add(w.w)
            for ev in w.r:
                add(ev)
        waits = []
        for k, v in evs.items():
            if self.waited.get((eng, k), 0) < v:
                self.waited[(eng, k)] = v
                waits.append((k, v))
        return waits

    def _update(self, ev, reads, writes):
        for r in reads:
            r.r.append(ev)
            if len(r.r) > 64:
                best = {}
                for k, v in r.r:
                    if best.get(k, 0) < v:
                        best[k] = v
                r.r = list(best.items())
        for w in writes:
            w.w = ev
            w.r = []

    def op(self, eng, fn, reads=(), writes=(), inc=True):
        waits = self._deps(eng, reads, writes, eng != "pe")
        key = ("c", eng, self.epoch[eng])
        if inc:
            self.cnt[key] += 1
            ev = (key, self.cnt[key])
            self.pending[eng] = False
        else:
            ev = (key, self.cnt[key] + 1)
            self.pending[eng] = True
        self._update(ev, reads, writes)
        self.lists[eng].append((waits, _freeze(fn), key if inc else None, 1))
        self.n_ops[eng] += 1
        if inc and self.cnt[key] >= self.EPOCH:
            self.epoch[eng] += 1
            self._mksem(("c", eng, self.epoch[eng]))
        return ev

    def dma(self, q, out, in_, reads=(), writes=(), sem=None):
        key = ("d", sem)
        if key not in self.sems:
            self._mksem(key)
        waits = self._deps(q, reads, writes, True)
        prev = self.cnt[key]
        if prev > 0 and self.waited.get((q, key), 0) < prev:
            self.waited[(q, key)] = prev
            waits.append((key, prev))
        self.cnt[key] += 16
        ev = (key, self.cnt[key])
        self._update(ev, reads, writes)
        fn = lambda e, out=out, in_=in_: e.dma_start(out=out, in_=in_)
        self.lists[q].append((waits, fn, key, 16))
        self.n_ops[q] += 1
        return ev

    def barrier(self):
        for e in self.ENGS:
            assert not self.pending[e]
        cur = [(k, v) for k, v in self.cnt.items() if v > 0]
        for e in self.ENGS:
            waits = []
            for k, v in cur:
                if self.waited.get((e, k), 0) < v:
                    self.waited[(e, k)] = v
                    waits.append((k, v))
            if waits:
                self.lists[e].append((waits, None, None, 0))

    def finish(self):
        self.barrier()
        block = self.es.enter_context(self.nc.Block())
        sems = self.sems

        def run(engine, lst):
            for waits, fn, inckey, n in lst:
                for k, v in waits:
                    engine.wait_ge(sems[k], v)
                if fn is not None:
                    ins = fn(engine)
                    if inckey is not None:
                        ins.then_inc(sems[inckey], n)

        @block.tensor
        def _(e):
            run(e, self.lists["pe"])

        @block.scalar
        def _(e):
            run(e, self.lists["act"])

        @block.vector
        def _(e):
            run(e, self.lists["dve"])

        @block.gpsimd
        def _(e):
            run(e, self.lists["pool"])

        @block.sync
        def _(e):
            run(e, self.lists["sp"])

        self.es.close()


class _Stop(Exception):
    pass


def build(do_sample=True, n_prompt=2, stop_after=None):
    nc = bass.Bass("TRN2", target_bir_lowering=False)
    P = Prog(nc)

    def ck(n):
        if stop_after == n:
            raise _Stop()

    def din(name, shape):
        return nc.dram_tensor(name, list(shape), F32, kind="ExternalInput").ap()

    def dout(name, shape):
        return nc.dram_tensor(name, list(shape), F32, kind="ExternalOutput").ap()

    xp_d = din("xp", [2, SEQ, D])
    xs_d = din("xs", [32, D])
    cak_d = din("cak", [2, 512, D])
    cav_d = din("cav", [2, 512, D])
    cbk_d = din("cbk", [2, 128, 256])
    cbv_d = din("cbv", [2, 128, 256])
    lng_d = din("lng", [6, D])
    lnb_d = din("lnb", [6, D])
    win_d = din("win", [4, D, 2 * DFF])
    wdn_d = din("wdn", [4, DFF, D])
    wqa_d = din("wqa", [D, 3 * D])
    woa_d = din("woa", [D, D])
    wqb_d = din("wqb", [D, 1536])
    wob_d = din("wob", [D, D])
    ext_d = din("ext", [16, 768])
    snk_d = din("snk", [1, 16])
    idn_d = din("idn", [128, 128])
    jrev_d = din("jrev", [128, 128])
    cosT_d = din("cosT", [128, SEQ])
    sinT_d = din("sinT", [128, SEQ])
    cosTs_d = din("cosTs", [128, 32])
    sinTs_d = din("sinTs", [128, 32])
    ctok_d = din("ctok", [128, 256])
    stok_d = din("stok", [128, 256])
    ctoks_d = din("ctoks", [16, 256])
    stoks_d = din("stoks", [16, 256])

    yp_d = dout("yp", [2, SEQ, D])
    ys_d = dout("ys", [32, D])
    akp_d = dout("akp", [2, 512, D])
    avp_d = dout("avp", [2, 512, D])
    bkp_d = dout("bkp", [2, 128, 256])
    bvp_d = dout("bvp", [2, 128, 256])
    aks_d = dout("aks", [32, D])
    avs_d = dout("avs", [32, D])
    bks_d = dout("bks", [32, 256])
    bvs_d = dout("bvs", [32, 256])

    X = P.sbuf("X", [128, 17, D], F32)
    XT = P.sbuf("XT", [128, 8, TW], BF16)
    SB = P.sbuf("SB", [128, 34688], BF16)
    SF = P.sbuf("SF", [128, 7776], F32)
    XB = P.sbuf("XB", [128, 2, D], BF16)
    IDENT = P.sbuf("IDENT", [128, 128], BF16)
    JREV = P.sbuf("JREV", [128, 128], BF16)
    ST = P.sbuf("ST", [128, 4, 2, 6], F32)
    MV = P.sbuf("MV", [128, 4, 4], F32)
    EPS = P.sbuf("EPS", [128, 1], F32)
    AM = P.sbuf("AM", [128, 4, 4], F32)
    SNK = P.sbuf("SNK", [128, 16], F32)
    PA = P.psum("PA", [128, 2, 1024], F32)
    PS1 = P.psum("PS1", [128, 2, 512], F32)
    PTR = P.psum("PTR", [128, 2, 1024], BF16)

    RX = [Res() for _ in range(17)]
    RXT = [Res() for _ in range(17)]
    RXB = [Res(), Res()]
    RST = [Res(), Res(), Res(), Res()]
    RPA = [Res(), Res()]
    RPS = [Res(), Res()]
    RPT = [Res(), Res()]
    RAM = [Res(), Res(), Res(), Res()]
    RCONST = Res()
    RLN = Res()
    cnt = {"pa": 0, "ps": 0, "pt": 0, "xb": 0, "st": 0, "am": 0, "sg": 0}

    def nxt(k, n=2):
        v = cnt[k]
        cnt[k] = (v + 1) % n
        return v

    def sbv(off, n):
        return SB[:, off:off + n]

    def sfv(off, n):
        return SF[:, off:off + n]

    P.dma("pool", IDENT[:], idn_d, writes=[RCONST], sem="c1")
    P.dma("pool", JREV[:], jrev_d, writes=[RCONST], sem="c2")
    P.dma("sp", SNK[:], bass.AP(tensor=snk_d.tensor, offset=0, ap=[[0, 128], [1, 16]]), writes=[RCONST], sem="c3")
    P.op("dve", lambda e: e.memset(EPS[:], EPSP), writes=[RCONST])

    def make_xT(t, ts):
        s = nxt("xb")
        P.op("act", lambda e: e.activation(out=XB[:ts, s, :], in_=X[:ts, t, :], func=AF.Copy),
             reads=[RX[t]], writes=[RXB[s]])
        b = nxt("pt")
        for c in range(8):
            P.op("pe", lambda e, c=c: e.transpose(out=PTR[:, b, c * 128:c * 128 + ts], in_=XB[:ts, s, c * 128:(c + 1) * 128],
                                                   identity=IDENT[:ts, :ts]),
                 reads=[RXB[s], RCONST], writes=[RPT[b]], inc=(c == 7))
        src = PTR[:, b, :].rearrange("p (c f) -> p c f", c=8)[:, :, 0:ts]
        P.op("act", lambda e: e.activation(out=XT[:, :, t * 128:t * 128 + ts], in_=src, func=AF.Copy),
             reads=[RPT[b]], writes=[RXT[t]])

    def load_ln(idx):
        P.dma("sp", SF[:, 0:1024], bass.AP(tensor=lng_d.tensor, offset=idx * D, ap=[[0, 128], [1, D]]), writes=[RLN], sem="lng")
        P.dma("sp", SF[:, 1024:2048], bass.AP(tensor=lnb_d.tensor, offset=idx * D, ap=[[0, 128], [1, D]]), writes=[RLN], sem="lnb")

    def ln_core(t, ts):
        s = nxt("st", 4)
        xt = X[:ts, t, :]
        P.op("dve", lambda e: e.bn_stats(out=ST[:ts, s, 0, :], in_=X[:ts, t, 0:512]), reads=[RX[t]], writes=[RST[s]])
        P.op("dve", lambda e: e.bn_stats(out=ST[:ts, s, 1, :], in_=X[:ts, t, 512:1024]), reads=[RX[t]], writes=[RST[s]])
        P.op("dve", lambda e: e.bn_aggr(out=MV[:ts, s, 0:2], in_=ST[:ts, s, :, :]), reads=[RST[s]], writes=[RST[s]])
        P.op("act", lambda e: e.activation(out=MV[:ts, s, 2:3], in_=MV[:ts, s, 1:2], func=AF.Sqrt, bias=EPS[:ts, :], scale=1.0),
             reads=[RST[s], RCONST], writes=[RST[s]])
        P.op("dve", lambda e: e.reciprocal(out=MV[:ts, s, 3:4], in_=MV[:ts, s, 2:3]), reads=[RST[s]], writes=[RST[s]])
        P.op("dve", lambda e: e.scalar_tensor_tensor(out=xt, in0=xt, scalar=MV[:ts, s, 0:1], in1=SF[:ts, 0:1024], op0=ALU.subtract, op1=ALU.mult),
             reads=[RST[s], RLN, RX[t]], writes=[RX[t]])
        P.op("dve", lambda e: e.scalar_tensor_tensor(out=xt, in0=xt, scalar=MV[:ts, s, 3:4], in1=SF[:ts, 1024:2048], op0=ALU.mult, op1=ALU.add),
             reads=[RST[s], RLN, RX[t]], writes=[RX[t]])

    def ln_tail(t, ts, out_ap=None, out_sem=None):
        if out_ap is not None:
            P.dma("sp", out_ap, X[:ts, t, :], reads=[RX[t]], sem=out_sem)
        else:
            make_xT(t, ts)

    def layer_norm(t, ts, out_ap=None, out_sem=None):
        ln_core(t, ts)
        ln_tail(t, ts, out_ap, out_sem)

    def ln_loop(NT, tsl):
        for t in range(NT + 1):
            if t < NT:
                ln_core(t, tsl if t == NT - 1 else 128)
            if t >= 1:
                ln_tail(t - 1, tsl if t - 1 == NT - 1 else 128)

    ACT_OFF, WI_OFF, WD_OFF, SG_OFF = 0, 16640, 24832, 33024
    RACT = [[Res() for _ in range(5)] for _ in range(8)]
    RWI = [Res(), Res()]
    RWD = [Res() for _ in range(8)]
    RSG = [Res(), Res()]
    wi_cnt = [0]

    def ffn(li, ln_idx, NT, ts, final_out=None):
        T = (NT - 1) * 128 + ts
        sts = [(o, min(512, T - o)) for o in range(0, T, 512)]
        actT = sbv(ACT_OFF, 8 * TW).rearrange("p (c t) -> p c t", c=8)
        wdv = sbv(WD_OFF, 8192).rearrange("p (c f) -> p c f", c=8)
        w_in = win_d[li].rearrange("(c p) f -> p c f", p=128)
        w_dn = wdn_d[li].rearrange("(c p) f -> p c f", p=128)
        load_ln(ln_idx)
        pend = []

        def ln_fin(t, tt):
            if final_out is not None:
                ln_tail(t, tt, out_ap=final_out(t, tt), out_sem=("yo", t % 2))
            else:
                ln_tail(t, tt)
        for gi, (j0, j1) in enumerate(FFG):
            ng = j1 - j0
            units = [(j, min(2, j1 - j)) for j in range(j0, j1, 2)]
            slots = []
            for _ in units:
                slots.append(wi_cnt[0] % 2)
                wi_cnt[0] += 1

            def wi_view(sl):
                return sbv(WI_OFF + sl * 4096, 4096).rearrange("p (c g f) -> p c g f", c=8, g=2)

            def wi_load(ui):
                ju, nu = units[ui]
                sl = slots[ui]
                wiv = wi_view(sl)
                P.dma("pool", wiv[:, :, 0, 0:nu * 128], w_in[:, :, ju * 128:(ju + nu) * 128], writes=[RWI[sl]], sem=("wi", sl, 0))
                P.dma("pool", wiv[:, :, 1, 0:nu * 128], w_in[:, :, DFF + ju * 128:DFF + (ju + nu) * 128], writes=[RWI[sl]], sem=("wi", sl, 1))
            for ui in range(min(2, len(units))):
                wi_load(ui)
            for jl in range(ng):
                P.dma("pool", wdv[:, jl, :], w_dn[:, j0 + jl, :], writes=[RWD[jl]], sem=("wd", jl))
            for ui, (ju, nu) in enumerate(units):
                sl = slots[ui]
                wiv = wi_view(sl)
                if ui >= 2:
                    wi_load(ui)
                order = [(jj, si) for jj in range(nu) for si in range(len(sts))]
                if gi == 0 and ui == 0:
                    order = [(jj, si) for si in range(len(sts)) for jj in range(nu)]
                for (jj, si) in order:
                    jl = ju + jj - j0
                    so, sn = sts[si]
                    if True:
                        b = nxt("pa")
                        xtr = [RXT[t] for t in range(so // 128, (so + sn + 127) // 128)]
                        for half in range(2):
                            for k in range(8):
                                P.op("pe", lambda e, k=k, half=half, b=b, jj=jj, so=so, sn=sn, wiv=wiv:
                                     e.matmul(PA[:, b, half * 512:half * 512 + sn], lhsT=wiv[:, k, half, jj * 128:(jj + 1) * 128],
                                              rhs=XT[:, k, so:so + sn], start=(k == 0), stop=(k == 7)),
                                     reads=[RWI[sl]] + xtr, writes=[RPA[b]], inc=(half == 1 and k == 7))
                        sg = nxt("sg")
                        sgv = sbv(SG_OFF + sg * 512, 512)
                        P.op("act", lambda e, b=b, sn=sn, sgv=sgv: e.activation(out=sgv[:, 0:sn], in_=PA[:, b, 0:sn], func=AF.Silu),
                             reads=[RPA[b]], writes=[RSG[sg]])
                        P.op("dve", lambda e, b=b, sn=sn, sgv=sgv, jl=jl, so=so:
                             e.tensor_tensor(out=actT[:, jl, so:so + sn], in0=sgv[:, 0:sn], in1=PA[:, b, 512:512 + sn], op=ALU.mult),
                             reads=[RPA[b], RSG[sg]], writes=[RACT[jl][si]])
            for t in range(NT):
                tt = ts if t == NT - 1 else 128
                b = nxt("pa")
                for jl in range(ng):
                    for half in range(2):
                        P.op("pe", lambda e, jl=jl, half=half, b=b, t=t, tt=tt:
                             e.matmul(PA[:tt, b, half * 512:(half + 1) * 512], lhsT=actT[:, jl, t * 128:t * 128 + tt],
                                      rhs=wdv[:, jl, half * 512:(half + 1) * 512], start=(jl == 0), stop=(jl == ng - 1)),
                             reads=[RACT[jl][t // 4], RWD[jl]], writes=[RPA[b]], inc=(jl == ng - 1 and half == 1))
                P.op("dve", lambda e, b=b, t=t, tt=tt: e.scalar_tensor_tensor(out=X[:tt, t, :], in0=PA[:tt, b, :], scalar=CF, in1=X[:tt, t, :],
                                                                             op0=ALU.mult, op1=ALU.add),
                     reads=[RPA[b], RX[t]], writes=[RX[t]])
                if gi == len(FFG) - 1:
                    pend.append((t, tt))
                    if len(pend) >= 2:
                        ln_core(*pend[-2])
                    if len(pend) >= 3:
                        ln_fin(*pend[-3])
            if gi == len(FFG) - 1:
                if len(pend) >= 1:
                    ln_core(*pend[-1])
                if len(pend) >= 2:
                    ln_fin(*pend[-2])
                ln_fin(*pend[-1])

    att_q = []
    att_n = [0]

    def attend(nq, qT, ksegs, vblocks, bias_ap, nk, sink_h, o_out):
        n_ = att_n[0]
        att_n[0] += 1
        a = n_ % 4
        sbt, pbt, ptt, rsb, rpb, rptt = att_bufs[n_ % len(att_bufs)]
        r_qk, r_bias, r_v, w_o = list(att_reads), list(att_bias_reads), list(att_v_reads), list(att_o_writes)
        ne = nk + (1 if sink_h is not None else 0)
        nb = len(vblocks)

        def stage1a():
            b = nxt("pa")
            for i, (kap, off, n) in enumerate(ksegs):
                P.op("pe", lambda e, kap=kap, off=off, n=n: e.matmul(PA[:nq, b, off:off + n], lhsT=qT, rhs=kap, start=True, stop=True),
                     reads=r_qk, writes=[RPA[b]], inc=(i == len(ksegs) - 1))
            P.op("dve", lambda e: e.scalar_tensor_tensor(out=sbt[:nq, 0:ne], in0=PA[:nq, b, 0:ne], scalar=0.125, in1=bias_ap,
                                                         op0=ALU.mult, op1=ALU.add),
                 reads=[RPA[b]] + r_bias, writes=[rsb])

        def stage1b():
            P.op("dve", lambda e: e.tensor_reduce(out=AM[:nq, a, 1:2], in_=sbt[:nq, 0:ne], axis=AX.X, op=ALU.max, negate=True),
                 reads=[rsb], writes=[RAM[a]])
            P.op("act", lambda e: e.activation(out=pbt[:nq, 0:ne], in_=sbt[:nq, 0:ne], func=AF.Exp, bias=AM[:nq, a, 1:2], scale=1.0,
                                               accum_out=AM[:nq, a, 2:3]),
                 reads=[rsb, RAM[a]], writes=[rpb, RAM[a]])

        def stage2():
            tb = nxt("pt")
            off = 0
            for i, (vap, n) in enumerate(vblocks):
                P.op("pe", lambda e, i=i, off=off, n=n: e.transpose(out=PTR[:n, tb, i * 128:i * 128 + nq], in_=pbt[:nq, off:off + n],
                                                                      identity=IDENT[:nq, :nq]),
                     reads=[rpb, RCONST], writes=[RPT[tb]], inc=(i == nb - 1))
                off += n
            nfull = sum(1 for (_, n) in vblocks if n == 128)
            if nfull > 0:
                srcv = PTR[:, tb, 0:nfull * 128].rearrange("p (c f) -> p c f", c=nfull)[:, :, 0:nq]
                dstv = ptt[:, 0:nfull * 128].rearrange("p (c f) -> p c f", c=nfull)[:, :, 0:nq]
                P.op("act", lambda e: e.activation(out=dstv, in_=srcv, func=AF.Copy), reads=[RPT[tb]], writes=[rptt])
            for i, (vap, n) in enumerate(vblocks):
                if n != 128:
                    P.op("act", lambda e, i=i, n=n: e.activation(out=ptt[:n, i * 128:i * 128 + nq], in_=PTR[:n, tb, i * 128:i * 128 + nq], func=AF.Copy),
                         reads=[RPT[tb]], writes=[rptt])

        def stage3():
            ob = nxt("ps")
            for i, (vap, n) in enumerate(vblocks):
                P.op("pe", lambda e, i=i, vap=vap, n=n: e.matmul(PS1[:nq, ob, 0:64], lhsT=ptt[:n, i * 128:i * 128 + nq], rhs=vap,
                                                                 start=(i == 0), stop=(i == nb - 1)),
                     reads=[rptt] + r_v, writes=[RPS[ob]], inc=(i == nb - 1))
            P.op("dve", lambda e: e.reciprocal(out=AM[:nq, a, 3:4], in_=AM[:nq, a, 2:3]), reads=[RAM[a]], writes=[RAM[a]])
            P.op("dve", lambda e: e.tensor_scalar(out=o_out, in0=PS1[:nq, ob, 0:64], scalar1=AM[:nq, a, 3:4], scalar2=None, op0=ALU.mult),
                 reads=[RPS[ob], RAM[a]], writes=w_o)

        stage1a()
        att_q.append([stage1b, stage2, stage3])
        if len(att_q) >= 2:
            att_q[-2][0]()
        if len(att_q) >= 3:
            att_q[-3][1]()
        if len(att_q) >= 4:
            att_q[-4][2]()
            att_q.pop(0)

    def att_flush():
        k = len(att_q)
        done = [k - 1 - idx for idx in range(k)]
        while any(d < 3 for d in done):
            for idx in range(k - 1, -1, -1):
                if done[idx] < 3:
                    att_q[idx][done[idx]]()
                    done[idx] += 1
        att_q.clear()

    att_bufs = []
    att_reads = []
    att_bias_reads = []
    att_v_reads = []
    att_o_writes = []

    def out_proj_multi(tiles, OT_list, wo_list, r_ots, r_wos):
        n = len(OT_list)
        for (t, tt, co) in tiles:
            b = nxt("pa")
            for half in range(2):
                for i in range(n):
                    P.op("pe", lambda e, half=half, b=b, i=i, tt=tt, co=co: e.matmul(PA[:tt, b, half * 512:(half + 1) * 512], lhsT=OT_list[i][:, co:co + tt],
                                                                                     rhs=wo_list[i][:, half * 512:(half + 1) * 512], start=(i == 0), stop=(i == n - 1)),
                         reads=list(r_ots) + list(r_wos), writes=[RPA[b]], inc=(half == 1 and i == n - 1))
            P.op("dve", lambda e, b=b, t=t, tt=tt: e.scalar_tensor_tensor(out=X[:tt, t, :], in0=PA[:tt, b, :], scalar=CM, in1=X[:tt, t, :],
                                                                         op0=ALU.mult, op1=ALU.add),
                 reads=[RPA[b], RX[t]], writes=[RX[t]])

    def build_bias(h, dst, rdst, hk, rhk):
        hank, hi, lo = hk
        src = bass.AP(tensor=ext_d.tensor, offset=h * 768, ap=[[1, 128], [1, 640]])
        P.dma("sp", hank, src, writes=[rhk], sem="hank")
        P.op("act", lambda e: e.activation(out=hi, in_=hank, func=AF.Copy), reads=[rhk], writes=[rhk])
        P.op("dve", lambda e: e.tensor_tensor(out=lo, in0=hank, in1=hi, op=ALU.subtract), reads=[rhk], writes=[rhk])
        b = nxt("pa")
        for (o, n) in ((0, 512), (512, 128)):
            P.op("pe", lambda e, o=o, n=n: e.matmul(PA[:, b, o:o + n], lhsT=JREV[:], rhs=hi[:, o:o + n], start=True, stop=False),
                 reads=[rhk, RCONST], writes=[RPA[b]], inc=False)
            P.op("pe", lambda e, o=o, n=n: e.matmul(PA[:, b, o:o + n], lhsT=JREV[:], rhs=lo[:, o:o + n], start=False, stop=True),
                 reads=[rhk, RCONST], writes=[RPA[b]], inc=(o == 512))
        P.op("dve", lambda e: e.tensor_copy(out=dst, in_=PA[:, b, 0:640]), reads=[RPA[b]], writes=[rdst])

    def mixer_a_prompt(seq, ws=False):
        NT, ts, T = 16, 128, SEQ
        P.barrier()
        wq_v = [[sbv(s * 3072 + i * 1024, 1024).rearrange("p (c f) -> p c f", c=8) for i in range(3)] for s in range(2)]
        wo_v = [sbv(6144 + s * 1024, 1024) for s in range(2)]
        qT_v = [sbv(8192 + s * 2048, 2048) for s in range(2)]
        kT_v = [sbv(12288 + s * 2048, 2048) for s in range(2)]
        V_v = [sbv(16384 + s * 2048, 2048).rearrange("p (t f) -> p t f", t=16) for s in range(2)]
        O_v = sbv(20480, 2048).rearrange("p (t f) -> p t f", t=16)
        OT_v = [sbv(22528, 2048), sbv(24576, 2048)]
        att_bufs.clear()
        for a in range(3):
            att_bufs.append((sfv((2560, 3200, 5504)[a], 640), sbv(26624 + a * 640, 640), sbv(29824 + a * 640, 640), Res(), Res(), Res()))
        bias_v = [[sfv(s * 1280 + h2 * 640, 640) for h2 in range(2)] for s in range(2)]
        kst = sfv(3840, 512).rearrange("p (t f) -> p t f", t=4)
        vst = sfv(4352, 512).rearrange("p (t f) -> p t f", t=4)
        hk = (sfv(4864, 640), sbv(28544, 640), sbv(29184, 640))
        rhk = Res()
        RW = [Res(), Res()]
        RWO = [Res(), Res()]
        RQ = [Res(), Res()]
        RK = [Res(), Res()]
        RV = [Res(), Res()]
        RB = [[Res(), Res()], [Res(), Res()]]
        RO, RKST, RVST = Res(), Res(), Res()
        ROT = [Res(), Res()]
        wq3 = wqa_d.rearrange("(c p) f -> p c f", p=128)
        SO = 31744
        qTs, kTn = sbv(SO, 32), sbv(SO + 32, 32)
        kTs = [sbv(SO + 64 + s2 * 528, 528) for s2 in range(2)]
        vn = [sbv(SO + 1120 + s2 * 128, 128) for s2 in range(2)]
        Os = [sbv(SO + 1376 + s2 * 128, 128) for s2 in range(2)]
        OTs = [sbv(SO + 1632 + i * 32, 32) for i in range(2)]
        XBf = XB[:, :, :].rearrange("p a b -> p (a b)")
        kc = [XBf[:, s2 * 512:(s2 + 1) * 512].rearrange("p (t f) -> p t f", t=4) for s2 in range(2)]
        vc = [XBf[:, 1024 + s2 * 512:1024 + (s2 + 1) * 512].rearrange("p (t f) -> p t f", t=4) for s2 in range(2)]
        kst_s, vst_s = sfv(6784, 128), sfv(6912, 128)
        RQs, RKN, RKSTs, RVSTs = Res(), Res(), Res(), Res()
        ROTs = [Res(), Res()]
        RKSs, RKCs, RVCs, RVNs, ROs = ([Res(), Res()] for _ in range(5))

        def samp(hp):
            s = hp % 2
            for s2 in range(2):
                P.dma("pool", kc[s2], cak_d[s2, :, hp * 128:(hp + 1) * 128].rearrange("(t p) f -> p t f", p=128), writes=[RKCs[s2]], sem=("kca", s2))
                P.dma("pool", vc[s2], cav_d[s2, :, hp * 128:(hp + 1) * 128].rearrange("(t p) f -> p t f", p=128), writes=[RVCs[s2]], sem=("vca", s2))
            yield
            for i, (dstv, rr) in enumerate(((qTs, RQs), (kTn, RKN))):
                b = nxt("ps")
                for k in range(8):
                    P.op("pe", lambda e, k=k, b=b, i=i: e.matmul(PS1[:, b, 0:32], lhsT=wq_v[s][i][:, k, :], rhs=XT[:, k, 2048:2080], start=(k == 0), stop=(k == 7)),
                         reads=[RW[s], RXT[16]], writes=[RPS[b]], inc=(k == 7))
                P.op("act", lambda e, b=b, dstv=dstv: e.activation(out=dstv, in_=PS1[:, b, 0:32], func=AF.Copy), reads=[RPS[b]], writes=[rr])
                yield
            for s2 in range(2):
                tb = nxt("pt")
                for t in range(4):
                    P.op("pe", lambda e, tb=tb, t=t, s2=s2: e.transpose(out=PTR[:, tb, t * 128:(t + 1) * 128], in_=kc[s2][:, t, :], identity=IDENT[:]),
                         reads=[RKCs[s2], RCONST], writes=[RPT[tb]], inc=(t == 3))
                P.op("act", lambda e, tb=tb, s2=s2: e.activation(out=kTs[s2][:, 0:512], in_=PTR[:, tb, 0:512], func=AF.Copy), reads=[RPT[tb]], writes=[RKSs[s2]])
                P.op("act", lambda e, s2=s2: e.activation(out=kTs[s2][:, 512:528], in_=kTn[:, 16 * s2:16 * s2 + 16], func=AF.Copy), reads=[RKN], writes=[RKSs[s2]])
                yield
                for (wi_, stg, rstg, dst_d, semn) in ((2, vst_s, RVSTs, avs_d, "vsa"), (1, kst_s, RKSTs, aks_d, "ksa")):
                    b = nxt("ps")
                    for k in range(8):
                        P.op("pe", lambda e, k=k, b=b, wi_=wi_, s2=s2: e.matmul(PS1[:16, b, 0:128], lhsT=XT[:, k, 2048 + 16 * s2:2048 + 16 * s2 + 16], rhs=wq_v[s][wi_][:, k, :],
                                                                               start=(k == 0), stop=(k == 7)),
                             reads=[RW[s], RXT[16]], writes=[RPS[b]], inc=(k == 7))
                    P.op("dve", lambda e, b=b, stg=stg: e.tensor_copy(out=stg[:16, :], in_=PS1[:16, b, 0:128]), reads=[RPS[b]], writes=[rstg])
                    if wi_ == 2:
                        P.op("dve", lambda e, b=b, s2=s2: e.tensor_copy(out=vn[s2][:16, :], in_=PS1[:16, b, 0:128]), reads=[RPS[b]], writes=[RVNs[s2]])
                    P.dma("sp", dst_d[16 * s2:16 * s2 + 16, hp * 128:(hp + 1) * 128], stg[:16, :], reads=[rstg], sem=semn)
                    yield

        def prep(hp):
            s = hp % 2
            for i in range(3):
                P.dma("pool", wq_v[s][i], wq3[:, :, i * 1024 + hp * 128:i * 1024 + (hp + 1) * 128], writes=[RW[s]], sem=("wqa", s, i))
            for h2 in range(2):
                build_bias(2 * hp + h2, bias_v[s][h2], RB[s][h2], hk, rhk)
                bv = bias_v[s][h2]
                P.op("dve", lambda e, bv=bv: e.memset(bv[0:64, 576:640], NEG), writes=[RB[s][h2]])
                P.op("dve", lambda e, bv=bv: e.memset(bv[64:128, 0:64], NEG), writes=[RB[s][h2]])
            yield
            for st in range(4):
                xtr = [RXT[t] for t in range(4 * st, 4 * st + 4)]
                for i, (dstv, rr) in enumerate(((qT_v[s], RQ[s]), (kT_v[s], RK[s]))):
                    b = nxt("ps")
                    for k in range(8):
                        P.op("pe", lambda e, k=k, b=b, i=i, st=st: e.matmul(PS1[:, b, :], lhsT=wq_v[s][i][:, k, :], rhs=XT[:, k, st * 512:(st + 1) * 512],
                                                                            start=(k == 0), stop=(k == 7)),
                             reads=[RW[s]] + xtr, writes=[RPS[b]], inc=(k == 7))
                    if i == 0:
                        P.op("act", lambda e, b=b, dstv=dstv, st=st: e.activation(out=dstv[:, st * 512:(st + 1) * 512], in_=PS1[:, b, :], func=AF.Copy),
                             reads=[RPS[b]], writes=[rr])
                    else:
                        P.op("dve", lambda e, b=b, dstv=dstv, st=st: e.tensor_copy(out=dstv[:, st * 512:(st + 1) * 512], in_=PS1[:, b, :]),
                             reads=[RPS[b]], writes=[rr])
            yield
            for g4 in range(4):
                b = nxt("ps")
                for tt_ in range(4):
                    t = g4 * 4 + tt_
                    for k in range(8):
                        P.op("pe", lambda e, k=k, b=b, t=t, tt_=tt_: e.matmul(PS1[:, b, tt_ * 128:(tt_ + 1) * 128], lhsT=XT[:, k, t * 128:(t + 1) * 128],
                                                                              rhs=wq_v[s][2][:, k, :], start=(k == 0), stop=(k == 7)),
                             reads=[RW[s], RXT[t]], writes=[RPS[b]], inc=(tt_ == 3 and k == 7))
                P.op("act", lambda e, b=b, g4=g4: e.activation(out=V_v[s][:, g4 * 4:(g4 + 1) * 4, :], in_=PS1[:, b, :].rearrange("p (t f) -> p t f", t=4), func=AF.Copy),
                     reads=[RPS[b]], writes=[RV[s]])
                if g4 == 3:
                    P.op("dve", lambda e, b=b: e.tensor_copy(out=vst, in_=PS1[:, b, :].rearrange("p (t f) -> p t f", t=4)),
                         reads=[RPS[b], RV[s]], writes=[RVST])
                    P.dma("sp", avp_d[seq, :, hp * 128:(hp + 1) * 128].rearrange("(t p) f -> p t f", p=128), vst, reads=[RVST], sem="vst")
            b = nxt("ps")
            for tt_ in range(4):
                t = 12 + tt_
                for k in range(8):
                    P.op("pe", lambda e, k=k, b=b, t=t, tt_=tt_: e.matmul(PS1[:, b, tt_ * 128:(tt_ + 1) * 128], lhsT=XT[:, k, t * 128:(t + 1) * 128],
                                                                          rhs=wq_v[s][1][:, k, :], start=(k == 0), stop=(k == 7)),
                         reads=[RW[s], RXT[t]], writes=[RPS[b]], inc=(tt_ == 3 and k == 7))
            P.op("dve", lambda e, b=b: e.tensor_copy(out=kst, in_=PS1[:, b, :].rearrange("p (t f) -> p t f", t=4)), reads=[RPS[b]], writes=[RKST])
            P.dma("sp", akp_d[seq, :, hp * 128:(hp + 1) * 128].rearrange("(t p) f -> p t f", p=128), kst, reads=[RKST], sem="kst")
            yield

        def attn(hp, gen, pg=None):
            s = hp % 2
            sg = samp(hp) if ws else None
            for j in range(16):
                klo = max(0, (j - 4) * 128)
                khi = (j + 1) * 128
                nk = khi - klo
                c_lo = 640 - nk
                for h2 in range(2):
                    qT = qT_v[s][h2 * 64:(h2 + 1) * 64, j * 128:(j + 1) * 128]
                    ksegs = []
                    o = 0
                    while o < nk:
                        n = min(512 - (o % 512), nk - o)
                        ksegs.append((kT_v[s][h2 * 64:(h2 + 1) * 64, klo + o:klo + o + n], o, n))
                        o += n
                    vblocks = [(V_v[s][:, klo // 128 + i, h2 * 64:(h2 + 1) * 64], 128) for i in range(nk // 128)]
                    att_reads[:] = [RQ[s], RK[s]]
                    att_bias_reads[:] = [RB[s][h2]]
                    att_v_reads[:] = [RV[s]]
                    att_o_writes[:] = [RO]
                    attend(128, qT, ksegs, vblocks, bias_v[s][h2][:, c_lo:640], nk, None, O_v[:, j, h2 * 64:(h2 + 1) * 64])
                    if gen is not None:
                        next(gen, None)
                    if sg is not None:
                        next(sg, None)
                    if pg is not None:
                        next(pg, None)
            if pg is not None:
                for _ in pg:
                    pass
            if sg is not None:
                for _ in sg:
                    pass
                for s2 in range(2):
                    for h2 in range(2):
                        hs = slice(h2 * 64, (h2 + 1) * 64)
                        ksegs = [(kTs[s2][hs, 0:512], 0, 512), (kTs[s2][hs, 512:528], 512, 16)]
                        vblocks = [(vc[s2][:, t, hs], 128) for t in range(4)] + [(vn[s2][:16, hs], 16)]
                        att_reads[:] = [RQs, RKSs[s2]]
                        att_bias_reads[:] = [RB[s][h2]]
                        att_v_reads[:] = [RVCs[s2], RVNs[s2]]
                        att_o_writes[:] = [ROs[s2]]
                        attend(16, qTs[hs, 16 * s2:16 * s2 + 16], ksegs, vblocks, bias_v[s][h2][0:16, 0:528], 528, None, Os[s2][:16, hs])
            if gen is not None:
                for _ in gen:
                    pass
            att_flush()

        def post(hp):
            s = hp % 2
            for g8 in range(2):
                tb = nxt("pt")
                for i in range(8):
                    t = g8 * 8 + i
                    P.op("pe", lambda e, tb=tb, i=i, t=t: e.transpose(out=PTR[:, tb, i * 128:(i + 1) * 128], in_=O_v[:, t, :], identity=IDENT[:]),
                         reads=[RO, RCONST], writes=[RPT[tb]], inc=(i == 7))
                P.op("act", lambda e, tb=tb, g8=g8: e.activation(out=OT_v[s][:, g8 * 1024:(g8 + 1) * 1024], in_=PTR[:, tb, :], func=AF.Copy),
                     reads=[RPT[tb]], writes=[ROT[s]])
            if ws:
                for s2 in range(2):
                    tb = nxt("pt")
                    P.op("pe", lambda e, tb=tb, s2=s2: e.transpose(out=PTR[:, tb, 0:16], in_=Os[s2][:16, :], identity=IDENT[:16, :16]), reads=[ROs[s2], RCONST], writes=[RPT[tb]])
                    P.op("act", lambda e, tb=tb, s2=s2: e.activation(out=OTs[s][:, 16 * s2:16 * s2 + 16], in_=PTR[:, tb, 0:16], func=AF.Copy), reads=[RPT[tb]], writes=[ROTs[s]])
            yield
            if s == 1:
                for t in range(16):
                    out_proj_multi([(t, 128, t * 128)], OT_v, wo_v, ROT, RWO)
                    if t % 8 == 7:
                        yield
                if ws:
                    out_proj_multi([(16, 32, 0)], OTs, wo_v, ROTs, RWO)
                for i in range(2):
                    if hp + 1 + i < 8:
                        P.dma("pool", wo_v[i], woa_d[(hp + 1 + i) * 128:(hp + 2 + i) * 128, :], writes=[RWO[i]], sem=("woa", i))

        for i in range(2):
            P.dma("pool", wo_v[i], woa_d[i * 128:(i + 1) * 128, :], writes=[RWO[i]], sem=("woa", i))
        for _ in prep(0):
            pass
        pg = None
        for hp in range(8):
            gen = prep(hp + 1) if hp < 7 else None
            attn(hp, gen, pg)
            pg = post(hp)
        for _ in pg:
            pass
        P.barrier()

    def mixer_b_prompt(seq, ws=False, ws2=True):
        NT, ts, T = 16, 128, SEQ
        P.barrier()
        wk_raw = sbv(0, 2048).rearrange("p (c f) -> p c f", c=8)
        wv_raw = sbv(2048, 2048).rearrange("p (c f) -> p c f", c=8)
        kT_pair = [sbv(4096 + pr * 2048, 2048) for pr in range(2)]
        kp_raw = sbv(8192, 2048).rearrange("p (c f) -> p c f", c=8)
        kTn_pair = [sbv(10240 + pr * 32, 32) for pr in range(2)]
        RKP = [Res(), Res()]
        RKNP = [Res(), Res()]
        kT_v = [sbv(12288 + g * 2048, 2048) for g in range(4)]
        V_v = sbv(20480, 4096).rearrange("p (t f) -> p t f", t=16)
        wq_v = [sbv(24576 + s * 1024, 1024).rearrange("p (c f) -> p c f", c=8) for s in range(2)]
        wqp_v = [sbv(26624 + s * 1024, 1024).rearrange("p (c f) -> p c f", c=8) for s in range(2)]
        wo_v = [sbv(28672 + s * 1024, 1024) for s in range(2)]
        cosT = sfv(0, 2048)
        sinT = sfv(2048, 2048)
        tmp = [sfv(4096 + i * 512, 512) for i in range(2)]
        maskB = sfv(5120, 256)
        kst = sfv(5376, 256)
        vst = sfv(5632, 256)
        ctok = sfv(5888, 256)
        stok = sfv(6144, 256)
        ktmp = sfv(6400, 256)
        RWK, RWV, RKD, RKT, RVV, RTAB, RTMP, RMASK = Res(), Res(), Res(), [Res() for _ in range(4)], Res(), Res(), [Res(), Res()], Res()
        RKST, RVST = Res(), Res()
        wq3 = wqb_d.rearrange("(c p) f -> p c f", p=128)
        P.dma("pool", wk_raw, wq3[:, :, 1024:1280], writes=[RWK], sem="wkb")
        P.dma("pool", wv_raw, wq3[:, :, 1280:1536], writes=[RWV], sem="wvb")
        P.dma("sp", cosT, cosT_d, writes=[RTAB], sem="t1")
        P.dma("sp", sinT, sinT_d, writes=[RTAB], sem="t2")
        P.dma("sp", ctok, ctok_d, writes=[RTAB], sem="t3")
        P.dma("sp", stok, stok_d, writes=[RTAB], sem="t4")
        P.op("dve", lambda e: e.memset(maskB, 0.0), writes=[RMASK])
        P.op("dve", lambda e: e.memset(maskB[0:64, 192:256], NEG), writes=[RMASK])
        P.op("dve", lambda e: e.memset(maskB[64:128, 0:64], NEG), writes=[RMASK])
        for g in range(4):
            P.op("act", lambda e, g=g: e.activation(out=kp_raw[:, :, g * 64:g * 64 + 32], in_=wk_raw[:, :, g * 64 + 32:g * 64 + 64], func=AF.Copy),
                 reads=[RWK], writes=[RKD])
            P.op("act", lambda e, g=g: e.activation(out=kp_raw[:, :, g * 64 + 32:g * 64 + 64], in_=wk_raw[:, :, g * 64:g * 64 + 32], func=AF.Copy),
                 reads=[RWK], writes=[RKD])

        def rope_proj(w_a, w_b, rw, dst, rdst, ncols, cT, sT, xcols, tco=None, dco=None):
            so, sn = xcols
            tco = so if tco is None else tco
            dco = so if dco is None else dco
            xtr = [RXT[t] for t in range(so // 128, (so + sn + 127) // 128)]
            bb = []
            for w in (w_a, w_b):
                b = nxt("ps")
                bb.append(b)
                for k in range(8):
                    P.op("pe", lambda e, k=k, b=b, w=w: e.matmul(PS1[:, b, 0:sn], lhsT=w[:, k, :], rhs=XT[:, k, so:so + sn], start=(k == 0), stop=(k == 7)),
                         reads=rw + xtr, writes=[RPS[b]], inc=(k == 7))
            P.op("dve", lambda e: e.tensor_tensor(out=tmp[0][:, 0:sn], in0=PS1[:, bb[0], 0:sn], in1=cT[:, tco:tco + sn], op=ALU.mult),
                 reads=[RPS[bb[0]], RTAB], writes=[RTMP[0]])
            P.op("dve", lambda e: e.tensor_tensor(out=tmp[1][:, 0:sn], in0=PS1[:, bb[1], 0:sn], in1=sT[:, tco:tco + sn], op=ALU.mult),
                 reads=[RPS[bb[1]], RTAB], writes=[RTMP[1]])
            P.op("pool", lambda e: e.tensor_tensor(out=dst[:, dco:dco + sn], in0=tmp[0][:, 0:sn], in1=tmp[1][:, 0:sn], op=ALU.add),
                 reads=[RTMP[0], RTMP[1]], writes=[rdst])

        for pr in range(2):
            for st in range(4):
                rope_proj(wk_raw[:, :, pr * 128:(pr + 1) * 128], kp_raw[:, :, pr * 128:(pr + 1) * 128], [RWK, RKD], kT_pair[pr], RKP[pr], 128, cosT, sinT,
                          (st * 512, 512))
        for g in range(4):
            src = kT_pair[g // 2][(g % 2) * 64:(g % 2) * 64 + 64, :]
            for hh in range(2):
                P.dma("sp", kT_v[g][hh * 64:(hh + 1) * 64, :], src, reads=[RKP[g // 2]], writes=[RKT[g]], sem=("kdup", g, hh))
        for t2 in range(8):
            b = nxt("ps")
            for i in range(2):
                t = t2 * 2 + i
                for k in range(8):
                    P.op("pe", lambda e, k=k, b=b, t=t, i=i: e.matmul(PS1[:, b, i * 256:(i + 1) * 256], lhsT=XT[:, k, t * 128:(t + 1) * 128], rhs=wv_raw[:, k, :],
                                                                      start=(k == 0), stop=(k == 7)),
                         reads=[RWV, RXT[t]], writes=[RPS[b]], inc=(i == 1 and k == 7))
            P.op("act", lambda e, b=b, t2=t2: e.activation(out=V_v[:, t2 * 2:t2 * 2 + 2, :], in_=PS1[:, b, :].rearrange("p (t f) -> p t f", t=2), func=AF.Copy),
                 reads=[RPS[b]], writes=[RVV])
            if t2 == 7:
                P.op("dve", lambda e, b=b: e.tensor_copy(out=vst, in_=PS1[:, b, 256:512]), reads=[RPS[b], RVV], writes=[RVST])
                P.dma("sp", bvp_d[seq], vst, reads=[RVST], sem="vstb")
        b = nxt("ps")
        for k in range(8):
            P.op("pe", lambda e, k=k, b=b: e.matmul(PS1[:, b, 0:256], lhsT=XT[:, k, 15 * 128:16 * 128], rhs=wk_raw[:, k, :], start=(k == 0), stop=(k == 7)),
                 reads=[RWK, RXT[15]], writes=[RPS[b]], inc=(k == 7))
        rope_tok(128, PS1[:, b, 0:256], RPS[b], ctok, stok, RTAB, kst, ktmp, RKST)
        P.dma("sp", bkp_d[seq], kst, reads=[RKST], sem="kstb")
        cosTs, sinTs = sfv(7712, 32), sfv(7744, 32)
        XBf = XB[:, :, :].rearrange("p a b -> p (a b)")
        kTn_s = [XBf[:, g * 32:(g + 1) * 32] for g in range(4)]
        kTs_s = [[XBf[:, 128 + (s2 * 4 + g) * 144:128 + (s2 * 4 + g + 1) * 144] for g in range(4)] for s2 in range(2)]
        vc_s = [XBf[:, 1280 + s2 * 256:1280 + (s2 + 1) * 256] for s2 in range(2)]
        vn_s = [sbv(32896 + s2 * 256, 256) for s2 in range(2)]
        kcd = sbv(33408, 128)
        RKNs = [Res() for _ in range(4)]
        RKSs = [[Res() for _ in range(4)] for _ in range(2)]
        RVCs, RVNs, ROs = [Res(), Res()], [Res(), Res()], [Res(), Res()]
        RKCD, RQs = Res(), Res()
        ROTs = [Res(), Res()]
        if ws:
            P.dma("sp", cosTs, cosTs_d, writes=[RTAB], sem="t1")
            P.dma("sp", sinTs, sinTs_d, writes=[RTAB], sem="t2")
            P.dma("sp", ctok[:16, :], ctoks_d, writes=[RTAB], sem="t3")
            P.dma("sp", stok[:16, :], stoks_d, writes=[RTAB], sem="t4")
            for pr in range(2):
                rope_proj(wk_raw[:, :, pr * 128:(pr + 1) * 128], kp_raw[:, :, pr * 128:(pr + 1) * 128], [RWK, RKD], kTn_pair[pr], RKNP[pr], 128, cosTs, sinTs,
                          (2048, 32), tco=0, dco=0)
            for g in range(4):
                srcn = kTn_pair[g // 2][(g % 2) * 64:(g % 2) * 64 + 64, :]
                for hh in range(2):
                    P.dma("sp", kTn_s[g][hh * 64:(hh + 1) * 64, :], srcn, reads=[RKNP[g // 2]], writes=[RKNs[g]], sem=("kdupn", g, hh))
            for s2 in range(2):
                P.dma("pool", vc_s[s2], cbv_d[s2], writes=[RVCs[s2]], sem=("vcb", s2))
                for g in range(4):
                    for hh in range(2):
                        P.dma("pool", kcd[:, hh * 64:(hh + 1) * 64], cbk_d[s2, :, g * 64:(g + 1) * 64], writes=[RKCD], sem=("kcb", hh))
                    tb = nxt("pt")
                    P.op("pe", lambda e, tb=tb: e.transpose(out=PTR[:, tb, 0:128], in_=kcd, identity=IDENT[:]), reads=[RKCD, RCONST], writes=[RPT[tb]])
                    P.op("act", lambda e, tb=tb, s2=s2, g=g: e.activation(out=kTs_s[s2][g][:, 0:128], in_=PTR[:, tb, 0:128], func=AF.Copy), reads=[RPT[tb]], writes=[RKSs[s2][g]])
                    P.op("act", lambda e, s2=s2, g=g: e.activation(out=kTs_s[s2][g][:, 128:144], in_=kTn_s[g][:, 16 * s2:16 * s2 + 16], func=AF.Copy), reads=[RKNs[g]], writes=[RKSs[s2][g]])
                b = nxt("ps")
                for k in range(8):
                    P.op("pe", lambda e, k=k, b=b, s2=s2: e.matmul(PS1[:16, b, 0:256], lhsT=XT[:, k, 2048 + 16 * s2:2048 + 16 * s2 + 16], rhs=wv_raw[:, k, :], start=(k == 0), stop=(k == 7)),
                         reads=[RWV, RXT[16]], writes=[RPS[b]], inc=(k == 7))
                P.op("dve", lambda e, b=b: e.tensor_copy(out=vst[:16, :], in_=PS1[:16, b, 0:256]), reads=[RPS[b]], writes=[RVST])
                P.op("dve", lambda e, b=b, s2=s2: e.tensor_copy(out=vn_s[s2][:16, :], in_=PS1[:16, b, 0:256]), reads=[RPS[b]], writes=[RVNs[s2]])
                P.dma("sp", bvs_d[16 * s2:16 * s2 + 16, :], vst[:16, :], reads=[RVST], sem="vsb")
                b = nxt("ps")
                for k in range(8):
                    P.op("pe", lambda e, k=k, b=b, s2=s2: e.matmul(PS1[:16, b, 0:256], lhsT=XT[:, k, 2048 + 16 * s2:2048 + 16 * s2 + 16], rhs=wk_raw[:, k, :], start=(k == 0), stop=(k == 7)),
                         reads=[RWK, RXT[16]], writes=[RPS[b]], inc=(k == 7))
                rope_tok(16, PS1[:16, b, 0:256], RPS[b], ctok, stok, RTAB, kst, ktmp, RKST)
                P.dma("sp", bks_d[16 * s2:16 * s2 + 16, :], kst[:16, :], reads=[RKST], sem="ksb")
        P.barrier()
        qTs = sbv(0, 32)
        Os_s = [sbv(64 + s2 * 128, 128) for s2 in range(2)]
        OTs = [sbv(320, 32), sbv(352, 32)]
        ws = ws and ws2
        maskH = [[sfv(5376 + (s_ * 2 + h2) * 257, 257) for h2 in range(2)] for s_ in range(2)]
        zbH = [sfv(6404 + h2 * 145, 145) for h2 in range(2)]
        RMH = [[Res(), Res()], [Res(), Res()]]
        RZB = [Res(), Res()]
        kz = sbv(400, 2)
        RKZ = Res()
        P.op("pool", lambda e: e.memset(kz, 0.0), writes=[RKZ])
        for s_ in range(2):
            for h2 in range(2):
                P.op("pool", lambda e, s_=s_, h2=h2: e.tensor_copy(out=maskH[s_][h2][:, 0:256], in_=maskB), reads=[RMASK], writes=[RMH[s_][h2]])
        if ws:
            for h2 in range(2):
                P.op("pool", lambda e, h2=h2: e.memset(zbH[h2], 0.0), writes=[RZB[h2]])
        qT_v = [sbv(4096 + s * 2048, 2048) for s in range(2)]
        O_v = sbv(8192, 2048).rearrange("p (t f) -> p t f", t=16)
        OT_v = [sbv(10240, 2048), sbv(2048, 2048)]
        att_bufs.clear()
        for a in range(3):
            att_bufs.append((sfv(6700 + a * 257, 257), sbv(30720 + a * 288, 288), sbv(31872 + a * 256, 256), Res(), Res(), Res()))
        RW = [Res(), Res()]
        RWP = [Res(), Res()]
        RWO = [Res(), Res()]
        RQ = [Res(), Res()]
        RO = Res()
        ROT = [Res(), Res()]
        def prep(hp):
            s = hp % 2
            P.dma("pool", wq_v[s], wq3[:, :, hp * 128:(hp + 1) * 128], writes=[RW[s]], sem=("wqb", s))
            for h2 in range(2):
                P.op("act", lambda e, h2=h2: e.activation(out=maskH[s][h2][:, 256:257], in_=SNK[:, 2 * hp + h2:2 * hp + h2 + 1], func=AF.Copy),
                     reads=[RCONST], writes=[RMH[s][h2]])
            for hh in range(2):
                P.op("act", lambda e, hh=hh: e.activation(out=wqp_v[s][:, :, hh * 64:hh * 64 + 32], in_=wq_v[s][:, :, hh * 64 + 32:hh * 64 + 64], func=AF.Copy),
                     reads=[RW[s]], writes=[RWP[s]])
                P.op("act", lambda e, hh=hh: e.activation(out=wqp_v[s][:, :, hh * 64 + 32:hh * 64 + 64], in_=wq_v[s][:, :, hh * 64:hh * 64 + 32], func=AF.Copy),
                     reads=[RW[s]], writes=[RWP[s]])
            yield
            for st in range(4):
                rope_proj(wq_v[s], wqp_v[s], [RW[s], RWP[s]], qT_v[s], RQ[s], 128, cosT, sinT, (st * 512, 512))
            yield

        def attn(hp, gen, pg=None):
            s = hp % 2
            g = hp // 2
            if ws:
                for h2 in range(2):
                    P.op("act", lambda e, h2=h2: e.activation(out=zbH[h2][:16, 144:145], in_=SNK[:16, 2 * hp + h2:2 * hp + h2 + 1], func=AF.Copy),
                         reads=[RCONST], writes=[RZB[h2]])
                rope_proj(wq_v[s], wqp_v[s], [RW[s], RWP[s]], qTs, RQs, 128, cosTs, sinTs, (2048, 32), tco=0, dco=0)
            for j in range(16):
                klo = max(0, (j - 1) * 128)
                khi = (j + 1) * 128
                nk = khi - klo
                c_lo = 256 - nk
                for h2 in range(2):
                    qT = qT_v[s][h2 * 64:(h2 + 1) * 64, j * 128:(j + 1) * 128]
                    ksegs = [(kT_v[g][h2 * 64:(h2 + 1) * 64, klo:khi], 0, nk), (kz[h2 * 64:(h2 + 1) * 64, 0:1], nk, 1)]
                    vblocks = [(V_v[:, klo // 128 + i, g * 64:(g + 1) * 64], 128) for i in range(nk // 128)]
                    att_reads[:] = [RQ[s], RKT[g], RKZ]
                    att_bias_reads[:] = [RMH[s][h2]]
                    att_v_reads[:] = [RVV]
                    att_o_writes[:] = [RO]
                    attend(128, qT, ksegs, vblocks, maskH[s][h2][:, c_lo:257], nk, 2 * hp + h2, O_v[:, j, h2 * 64:(h2 + 1) * 64])
                    if gen is not None and (j * 2 + h2) % 4 == 3:
                        next(gen, None)
                    if pg is not None:
                        next(pg, None)
            if pg is not None:
                for _ in pg:
                    pass
            if ws:
                for s2 in range(2):
                    for h2 in range(2):
                        hs = slice(h2 * 64, (h2 + 1) * 64)
                        gs = slice(g * 64, (g + 1) * 64)
                        ksegs = [(kTs_s[s2][g][hs, 0:144], 0, 144), (kz[hs, 0:1], 144, 1)]
                        vblocks = [(vc_s[s2][:, gs], 128), (vn_s[s2][:16, gs], 16)]
                        att_reads[:] = [RQs, RKSs[s2][g], RKZ]
                        att_bias_reads[:] = [RZB[h2]]
                        att_v_reads[:] = [RVCs[s2], RVNs[s2]]
                        att_o_writes[:] = [ROs[s2]]
                        attend(16, qTs[hs, 16 * s2:16 * s2 + 16], ksegs, vblocks, zbH[h2][0:16, 0:145], 144, 2 * hp + h2, Os_s[s2][:16, hs])
            if gen is not None:
                for _ in gen:
                    pass
            att_flush()

        def post(hp):
            s = hp % 2
            for g8 in range(2):
                tb = nxt("pt")
                for i in range(8):
                    t = g8 * 8 + i
                    P.op("pe", lambda e, tb=tb, i=i, t=t: e.transpose(out=PTR[:, tb, i * 128:(i + 1) * 128], in_=O_v[:, t, :], identity=IDENT[:]),
                         reads=[RO, RCONST], writes=[RPT[tb]], inc=(i == 7))
                P.op("act", lambda e, tb=tb, g8=g8: e.activation(out=OT_v[s][:, g8 * 1024:(g8 + 1) * 1024], in_=PTR[:, tb, :], func=AF.Copy),
                     reads=[RPT[tb]], writes=[ROT[s]])
            if ws:
                for s2 in range(2):
                    tb = nxt("pt")
                    P.op("pe", lambda e, tb=tb, s2=s2: e.transpose(out=PTR[:, tb, 0:16], in_=Os_s[s2][:16, :], identity=IDENT[:16, :16]), reads=[ROs[s2], RCONST], writes=[RPT[tb]])
                    P.op("act", lambda e, tb=tb, s2=s2: e.activation(out=OTs[s][:, 16 * s2:16 * s2 + 16], in_=PTR[:, tb, 0:16], func=AF.Copy), reads=[RPT[tb]], writes=[ROTs[s]])
            yield
            if s == 1:
                for t in range(16):
                    out_proj_multi([(t, 128, t * 128)], OT_v, wo_v, ROT, RWO)
                    if t % 8 == 7:
                        yield
                if ws:
                    out_proj_multi([(16, 32, 0)], OTs, wo_v, ROTs, RWO)
                for i in range(2):
                    if hp + 1 + i < 8:
                        P.dma("pool", wo_v[i], wob_d[(hp + 1 + i) * 128:(hp + 2 + i) * 128, :], writes=[RWO[i]], sem=("wob", i))

        for i in range(2):
            P.dma("pool", wo_v[i], wob_d[i * 128:(i + 1) * 128, :], writes=[RWO[i]], sem=("wob", i))
        for _ in prep(0):
            pass
        pg = None
        for hp in range(8):
            gen = prep(hp + 1) if hp < 7 else None
            attn(hp, gen, pg)
            pg = post(hp)
        for _ in pg:
            pass
        P.barrier()

    def rope_tok(n, kps, rkps, ctok, stok, rtab, dst, ktmp, rdst):
        k4 = kps.rearrange("p (h a d) -> p h a d", h=4, a=2)
        s4 = stok[:n, :].rearrange("p (h a d) -> p h a d", h=4, a=2)
        t4 = ktmp[:n, :].rearrange("p (h a d) -> p h a d", h=4, a=2)
        P.op("dve", lambda e: e.tensor_tensor(out=dst[:n, :], in0=kps, in1=ctok[:n, :], op=ALU.mult), reads=[rkps, rtab], writes=[rdst])
        P.op("dve", lambda e: e.tensor_tensor(out=t4[:, :, 0, :], in0=k4[:, :, 1, :], in1=s4[:, :, 0, :], op=ALU.mult), reads=[rkps, rtab], writes=[rdst])
        P.op("dve", lambda e: e.tensor_tensor(out=t4[:, :, 1, :], in0=k4[:, :, 0, :], in1=s4[:, :, 1, :], op=ALU.mult), reads=[rkps, rtab], writes=[rdst])
        P.op("dve", lambda e: e.tensor_tensor(out=dst[:n, :], in0=dst[:n, :], in1=ktmp[:n, :], op=ALU.add), reads=[rdst], writes=[rdst])

    def load_x(src, NT, ts, t0=0):
        for t in range(NT):
            tt = ts if t == NT - 1 else 128
            P.dma("sp", X[:tt, t0 + t, :], src[t * 128:t * 128 + tt, :], writes=[RX[t0 + t]], sem=("xin", t % 4))
            make_xT(t0 + t, tt)

    def prompt_pass(seq, ws):
        NT, tsl = (17, 32) if ws else (16, 128)
        load_x(xp_d[seq], 16, 128)
        if ws:
            load_x(xs_d, 1, 32, t0=16)

        def fo(t, tt):
            return ys_d[0:32, :] if t == 16 else yp_d[seq, t * 128:t * 128 + tt, :]
        ffn(0, 0, NT, tsl)
        mixer_a_prompt(seq, ws and stop_after != 'wsb')
        load_ln(1)
        ln_loop(NT, tsl)
        ffn(1, 2, NT, tsl)
        ffn(2, 3, NT, tsl)
        mixer_b_prompt(seq, ws and stop_after != 'wsa', ws2=(stop_after != 'wsb1'))
        load_ln(4)
        ln_loop(NT, tsl)
        ffn(3, 5, NT, tsl, final_out=fo)

    try:
        for seq in range(n_prompt):
            prompt_pass(seq, do_sample and seq == 0)
    except _Stop:
        pass

    P.finish()
    return nc, P


def _rope_tables():
    half = 32
    inv = (10000.0 ** (-np.arange(half, dtype=np.float32) / half)).astype(np.float32)

    def feat(pos):
        ang = pos.astype(np.float32)[None, :] * inv[:, None]
        c = np.cos(ang).astype(np.float32)
        s = np.sin(ang).astype(np.float32)
        cT = np.concatenate([c, c, c, c], axis=0)
        sT = np.concatenate([-s, s, -s, s], axis=0)
        return np.ascontiguousarray(cT), np.ascontiguousarray(sT)

    def tok(pos):
        ang = pos.astype(np.float32)[:, None] * inv[None, :]
        c = np.cos(ang).astype(np.float32)
        s = np.sin(ang).astype(np.float32)
        ct = np.tile(np.concatenate([c, c], axis=1), (1, 4))
        st = np.tile(np.concatenate([-s, s], axis=1), (1, 4))
        return np.ascontiguousarray(ct), np.ascontiguousarray(st)
    return feat, tok


_CACHE = {}


def kernel(x_prompt, x_sample, cache_a_k, cache_a_v, cache_b_k, cache_b_v, ln_g, ln_b, w_ffn_in, w_ffn_down,
           w_qkv_a, w_o_a, rel_bias_a, w_qkv_b, w_o_b, sinks_b):
    f = lambda a: np.ascontiguousarray(np.asarray(a, dtype=np.float32))
    if "nc" not in _CACHE:
        _CACHE["nc"] = build()[0]
    nc = _CACHE["nc"]
    feat, tok = _rope_tables()
    cosT, sinT = feat(np.arange(SEQ))
    spos = 2048 + np.concatenate([np.arange(16), np.arange(16)])
    cosTs, sinTs = feat(spos)
    ctok, stok = tok(np.arange(SEQ - 128, SEQ))
    ctoks, stoks = tok(2048 + np.arange(16))
    idx = np.clip(639 - np.arange(768), -128, 128) + 128
    ext = f(np.asarray(rel_bias_a)[0][:, idx])
    shared = {
        "lng": f(np.asarray(ln_g).reshape(6, D)), "lnb": f(np.asarray(ln_b).reshape(6, D)),
        "win": f(np.asarray(w_ffn_in).reshape(4, D, 2 * DFF)), "wdn": f(np.asarray(w_ffn_down).reshape(4, DFF, D)),
        "wqa": f(np.asarray(w_qkv_a)[0]), "woa": f(np.asarray(w_o_a)[0]), "wqb": f(np.asarray(w_qkv_b)[0]), "wob": f(np.asarray(w_o_b)[0]),
        "ext": ext, "snk": f(np.asarray(sinks_b).reshape(1, 16)),
        "idn": np.eye(128, dtype=np.float32), "jrev": np.ascontiguousarray(np.eye(128, dtype=np.float32)[::-1]),
        "cosT": cosT, "sinT": sinT, "cosTs": cosTs, "sinTs": sinTs, "ctok": ctok, "stok": stok, "ctoks": ctoks, "stoks": stoks,
    }
    xp = np.asarray(x_prompt, dtype=np.float32)
    xs = np.asarray(x_sample, dtype=np.float32)
    cak = np.asarray(cache_a_k, dtype=np.float32)[0].reshape(16, 512, D)
    cav = np.asarray(cache_a_v, dtype=np.float32)[0].reshape(16, 512, D)
    cbk = np.asarray(cache_b_k, dtype=np.float32)[0].reshape(16, 128, 256)
    cbv = np.asarray(cache_b_v, dtype=np.float32)[0].reshape(16, 128, 256)
    in_maps = []
    for c in range(NCORES):
        m = dict(shared)
        sl = slice(2 * c, 2 * c + 2)
        m["xp"] = f(xp[sl])
        m["xs"] = f(xs[sl].reshape(32, D))
        m["cak"] = f(cak[sl])
        m["cav"] = f(cav[sl])
        m["cbk"] = f(cbk[sl])
        m["cbv"] = f(cbv[sl])
        in_maps.append(m)
    res = run_bass_kernel_spmd(nc, in_maps, core_ids=list(range(NCORES)))
    R = res.results
    cat = lambda k: np.concatenate([r[k] for r in R], axis=0)
    yp = cat("yp")
    ys = cat("ys").reshape(16, 16, D)
    akp = cat("akp").reshape(1, 16, 512, 16, 64)
    avp = cat("avp").reshape(1, 16, 512, 16, 64)
    bkp = cat("bkp").reshape(1, 16, 128, 4, 64)
    bvp = cat("bvp").reshape(1, 16, 128, 4, 64)
    aks = cat("aks").reshape(1, 16, 16, 16, 64)
    avs = cat("avs").reshape(1, 16, 16, 16, 64)
    bks = cat("bks").reshape(1, 16, 16, 4, 64)
    bvs = cat("bvs").reshape(1, 16, 16, 4, 64)
    return (yp, ys, akp, avp, bkp, bvp, aks, avs, bks, bvs)
```

```python
from contextlib import ExitStack
import types
import numpy as np
import concourse.bass as bass
import concourse.mybir as mybir
from concourse.bass_utils import run_bass_kernel_spmd

F32 = mybir.dt.float32
BF16 = mybir.dt.bfloat16
AF = mybir.ActivationFunctionType
ALU = mybir.AluOpType
AX = mybir.AxisListType

D = 1024
DFF = 2816
NFF = 22
SEQ = 2048
NCORES = 8
ALPHA = 2.0 ** 0.5
EPSP = 1e-5 / (ALPHA * ALPHA)
CF = 0.5 / ALPHA
CM = 1.0 / ALPHA
NEG = -30000.0
TW = 2080
FFG = [(0, 7), (7, 14), (14, 22)]


class Res:
    __slots__ = ("name", "w", "r")

    def __init__(self, name=""):
        self.name = name
        self.w = None
        self.r = []


def _freeze(fn):
    if fn is None or fn.__closure__ is None:
        return fn
    cells = []
    for c in fn.__closure__:
        try:
            cells.append(types.CellType(c.cell_contents))
        except ValueError:
            cells.append(c)
    return types.FunctionType(fn.__code__, fn.__globals__, fn.__name__, fn.__defaults__, tuple(cells))


class Prog:
    ENGS = ("pe", "act", "dve", "pool", "sp")
    EPOCH = 30000

    def __init__(self, nc):
        self.nc = nc
        self.es = ExitStack()
        self.lists = {e: [] for e in self.ENGS}
        self.sems = {}
        self.cnt = {}
        self.epoch = {e: 0 for e in self.ENGS}
        self.waited = {}
        self.pending = {e: False for e in self.ENGS}
        self.n_ops = {e: 0 for e in self.ENGS}
        for e in self.ENGS:
            self._mksem(("c", e, 0))

    def _mksem(self, key):
        name = "s_" + "_".join(str(k) for k in key)
        self.sems[key] = self.es.enter_context(self.nc.semaphore(name))
        self.cnt[key] = 0

    def sbuf(self, name, shape, dt):
        return self.es.enter_context(self.nc.sbuf_tensor(name, list(shape), dt))

    def psum(self, name, shape, dt):
        return self.es.enter_context(self.nc.psum_tensor(name, list(shape), dt))

    def _deps(self, eng, reads, writes, self_sync):
        evs = {}

        def add(ev):
            if ev is None:
                return
            k, v = ev
            if not self_sync and k[0] == "c" and k[1] == eng:
                return
            if evs.get(k, 0) < v:
                evs[k] = v
        for r in reads:
            add(r.w)
        for w in writes:
            add(w.w)
            for ev in w.r:
                add(ev)
        waits = []
        for k, v in evs.items():
            if self.waited.get((eng, k), 0) < v:
                self.waited[(eng, k)] = v
                waits.append((k, v))
        return waits

    def _update(self, ev, reads, writes):
        for r in reads:
            r.r.append(ev)
            if len(r.r) > 64:
                best = {}
                for k, v in r.r:
                    if best.get(k, 0) < v:
                        best[k] = v
                r.r = list(best.items())
        for w in writes:
            w.w = ev
            w.r = []

    def op(self, eng, fn, reads=(), writes=(), inc=True):
        waits = self._deps(eng, reads, writes, eng != "pe")
        key = ("c", eng, self.epoch[eng])
        if inc:
            self.cnt[key] += 1
            ev = (key, self.cnt[key])
            self.pending[eng] = False
        else:
            ev = (key, self.cnt[key] + 1)
            self.pending[eng] = True
        self._update(ev, reads, writes)
        self.lists[eng].append((waits, _freeze(fn), key if inc else None, 1))
        self.n_ops[eng] += 1
        if inc and self.cnt[key] >= self.EPOCH:
            self.epoch[eng] += 1
            self._mksem(("c", eng, self.epoch[eng]))
        return ev

    def dma(self, q, out, in_, reads=(), writes=(), sem=None):
        key = ("d", sem)
        if key not in self.sems:
            self._mksem(key)
        waits = self._deps(q, reads, writes, True)
        prev = self.cnt[key]
        if prev > 0 and self.waited.get((q, key), 0) < prev:
            self.waited[(q, key)] = prev
            waits.append((key, prev))
        self.cnt[key] += 16
        ev = (key, self.cnt[key])
        self._update(ev, reads, writes)
        fn = lambda e, out=out, in_=in_: e.dma_start(out=out, in_=in_)
        self.lists[q].append((waits, fn, key, 16))
        self.n_ops[q] += 1
        return ev

    def barrier(self):
        for e in self.ENGS:
            assert not self.pending[e]
        cur = [(k, v) for k, v in self.cnt.items() if v > 0]
        for e in self.ENGS:
            waits = []
            for k, v in cur:
                if self.waited.get((e, k), 0) < v:
                    self.waited[(e, k)] = v
                    waits.append((k, v))
            if waits:
                self.lists[e].append((waits, None, None, 0))

    def finish(self):
        self.barrier()
        block = self.es.enter_context(self.nc.Block())
        sems = self.sems

        def run(engine, lst):
            for waits, fn, inckey, n in lst:
                for k, v in waits:
                    engine.wait_ge(sems[k], v)
                if fn is not None:
                    ins = fn(engine)
                    if inckey is not None:
                        ins.then_inc(sems[inckey], n)

        @block.tensor
        def _(e):
            run(e, self.lists["pe"])

        @block.scalar
        def _(e):
            run(e, self.lists["act"])

        @block.vector
        def _(e):
            run(e, self.lists["dve"])

        @block.gpsimd
        def _(e):
            run(e, self.lists["pool"])

        @block.sync
        def _(e):
            run(e, self.lists["sp"])

        self.es.close()


class _Stop(Exception):
    pass


def build(do_sample=True, n_prompt=2, stop_after=None):
    nc = bass.Bass("TRN2", target_bir_lowering=False)
    P = Prog(nc)

    def ck(n):
        if stop_after == n:
            raise _Stop()

    def din(name, shape):
        return nc.dram_tensor(name, list(shape), F32, kind="ExternalInput").ap()

    def dout(name, shape):
        return nc.dram_tensor(name, list(shape), F32, kind="ExternalOutput").ap()

    xp_d = din("xp", [2, SEQ, D])
    xs_d = din("xs", [32, D])
    cak_d = din("cak", [2, 512, D])
    cav_d = din("cav", [2, 512, D])
    cbk_d = din("cbk", [2, 128, 256])
    cbv_d = din("cbv", [2, 128, 256])
    lng_d = din("lng", [6, D])
    lnb_d = din("lnb", [6, D])
    win_d = din("win", [4, D, 2 * DFF])
    wdn_d = din("wdn", [4, DFF, D])
    wqa_d = din("wqa", [D, 3 * D])
    woa_d = din("woa", [D, D])
    wqb_d = din("wqb", [D, 1536])
    wob_d = din("wob", [D, D])
    ext_d = din("ext", [16, 768])
    snk_d = din("snk", [1, 16])
    idn_d = din("idn", [128, 128])
    jrev_d = din("jrev", [128, 128])
    cosT_d = din("cosT", [128, SEQ])
    sinT_d = din("sinT", [128, SEQ])
    cosTs_d = din("cosTs", [128, 32])
    sinTs_d = din("sinTs", [128, 32])
    ctok_d = din("ctok", [128, 256])
    stok_d = din("stok", [128, 256])
    ctoks_d = din("ctoks", [16, 256])
    stoks_d = din("stoks", [16, 256])

    yp_d = dout("yp", [2, SEQ, D])
    ys_d = dout("ys", [32, D])
    akp_d = dout("akp", [2, 512, D])
    avp_d = dout("avp", [2, 512, D])
    bkp_d = dout("bkp", [2, 128, 256])
    bvp_d = dout("bvp", [2, 128, 256])
    aks_d = dout("aks", [32, D])
    avs_d = dout("avs", [32, D])
    bks_d = dout("bks", [32, 256])
    bvs_d = dout("bvs", [32, 256])

    X = P.sbuf("X", [128, 17, D], F32)
    XT = P.sbuf("XT", [128, 8, TW], BF16)
    SB = P.sbuf("SB", [128, 34688], BF16)
    SF = P.sbuf("SF", [128, 7776], F32)
    XB = P.sbuf("XB", [128, 2, D], BF16)
    IDENT = P.sbuf("IDENT", [128, 128], BF16)
    JREV = P.sbuf("JREV", [128, 128], BF16)
    ST = P.sbuf("ST", [128, 4, 2, 6], F32)
    MV = P.sbuf("MV", [128, 4, 4], F32)
    EPS = P.sbuf("EPS", [128, 1], F32)
    AM = P.sbuf("AM", [128, 4, 4], F32)
    SNK = P.sbuf("SNK", [128, 16], F32)
    PA = P.psum("PA", [128, 2, 1024], F32)
    PS1 = P.psum("PS1", [128, 2, 512], F32)
    PTR = P.psum("PTR", [128, 2, 1024], BF16)

    RX = [Res() for _ in range(17)]
    RXT = [Res() for _ in range(17)]
    RXB = [Res(), Res()]
    RST = [Res(), Res(), Res(), Res()]
    RPA = [Res(), Res()]
    RPS = [Res(), Res()]
    RPT = [Res(), Res()]
    RAM = [Res(), Res(), Res(), Res()]
    RCONST = Res()
    RLN = Res()
    cnt = {"pa": 0, "ps": 0, "pt": 0, "xb": 0, "st": 0, "am": 0, "sg": 0}

    def nxt(k, n=2):
        v = cnt[k]
        cnt[k] = (v + 1) % n
        return v

    def sbv(off, n):
        return SB[:, off:off + n]

    def sfv(off, n):
        return SF[:, off:off + n]

    P.dma("pool", IDENT[:], idn_d, writes=[RCONST], sem="c1")
    P.dma("pool", JREV[:], jrev_d, writes=[RCONST], sem="c2")
    P.dma("sp", SNK[:], bass.AP(tensor=snk_d.tensor, offset=0, ap=[[0, 128], [1, 16]]), writes=[RCONST], sem="c3")
    P.op("dve", lambda e: e.memset(EPS[:], EPSP), writes=[RCONST])

    def make_xT(t, ts):
        s = nxt("xb")
        P.op("act", lambda e: e.activation(out=XB[:ts, s, :], in_=X[:ts, t, :], func=AF.Copy),
             reads=[RX[t]], writes=[RXB[s]])
        b = nxt("pt")
        for c in range(8):
            P.op("pe", lambda e, c=c: e.transpose(out=PTR[:, b, c * 128:c * 128 + ts], in_=XB[:ts, s, c * 128:(c + 1) * 128],
                                                   identity=IDENT[:ts, :ts]),
                 reads=[RXB[s], RCONST], writes=[RPT[b]], inc=(c == 7))
        src = PTR[:, b, :].rearrange("p (c f) -> p c f", c=8)[:, :, 0:ts]
        P.op("act", lambda e: e.activation(out=XT[:, :, t * 128:t * 128 + ts], in_=src, func=AF.Copy),
             reads=[RPT[b]], writes=[RXT[t]])

    def load_ln(idx):
        P.dma("sp", SF[:, 0:1024], bass.AP(tensor=lng_d.tensor, offset=idx * D, ap=[[0, 128], [1, D]]), writes=[RLN], sem="lng")
        P.dma("sp", SF[:, 1024:2048], bass.AP(tensor=lnb_d.tensor, offset=idx * D, ap=[[0, 128], [1, D]]), writes=[RLN], sem="lnb")

    def ln_core(t, ts):
        s = nxt("st", 4)
        xt = X[:ts, t, :]
        P.op("dve", lambda e: e.bn_stats(out=ST[:ts, s, 0, :], in_=X[:ts, t, 0:512]), reads=[RX[t]], writes=[RST[s]])
        P.op("dve", lambda e: e.bn_stats(out=ST[:ts, s, 1, :], in_=X[:ts, t, 512:1024]), reads=[RX[t]], writes=[RST[s]])
        P.op("dve", lambda e: e.bn_aggr(out=MV[:ts, s, 0:2], in_=ST[:ts, s, :, :]), reads=[RST[s]], writes=[RST[s]])
        P.op("act", lambda e: e.activation(out=MV[:ts, s, 2:3], in_=MV[:ts, s, 1:2], func=AF.Sqrt, bias=EPS[:ts, :], scale=1.0),
             reads=[RST[s], RCONST], writes=[RST[s]])
        P.op("dve", lambda e: e.reciprocal(out=MV[:ts, s, 3:4], in_=MV[:ts, s, 2:3]), reads=[RST[s]], writes=[RST[s]])
        P.op("dve", lambda e: e.scalar_tensor_tensor(out=xt, in0=xt, scalar=MV[:ts, s, 0:1], in1=SF[:ts, 0:1024], op0=ALU.subtract, op1=ALU.mult),
             reads=[RST[s], RLN, RX[t]], writes=[RX[t]])
        P.op("dve", lambda e: e.scalar_tensor_tensor(out=xt, in0=xt, scalar=MV[:ts, s, 3:4], in1=SF[:ts, 1024:2048], op0=ALU.mult, op1=ALU.add),
             reads=[RST[s], RLN, RX[t]], writes=[RX[t]])

    def ln_tail(t, ts, out_ap=None, out_sem=None):
        if out_ap is not None:
            P.dma("sp", out_ap, X[:ts, t, :], reads=[RX[t]], sem=out_sem)
        else:
            make_xT(t, ts)

    def layer_norm(t, ts, out_ap=None, out_sem=None):
        ln_core(t, ts)
        ln_tail(t, ts, out_ap, out_sem)

    def ln_loop(NT, tsl):
        for t in range(NT + 1):
            if t < NT:
                ln_core(t, tsl if t == NT - 1 else 128)
            if t >= 1:
                ln_tail(t - 1, tsl if t - 1 == NT - 1 else 128)

    ACT_OFF, WI_OFF, WD_OFF, SG_OFF = 0, 16640, 24832, 33024
    RACT = [[Res() for _ in range(5)] for _ in range(8)]
    RWI = [Res(), Res()]
    RWD = [Res() for _ in range(8)]
    RSG = [Res(), Res()]
    wi_cnt = [0]

    def ffn(li, ln_idx, NT, ts, final_out=None):
        T = (NT - 1) * 128 + ts
        sts = [(o, min(512, T - o)) for o in range(0, T, 512)]
        actT = sbv(ACT_OFF, 8 * TW).rearrange("p (c t) -> p c t", c=8)
        wdv = sbv(WD_OFF, 8192).rearrange("p (c f) -> p c f", c=8)
        w_in = win_d[li].rearrange("(c p) f -> p c f", p=128)
        w_dn = wdn_d[li].rearrange("(c p) f -> p c f", p=128)
        load_ln(ln_idx)
        pend = []

        def ln_fin(t, tt):
            if final_out is not None:
                ln_tail(t, tt, out_ap=final_out(t, tt), out_sem=("yo", t % 2))
            else:
                ln_tail(t, tt)
        for gi, (j0, j1) in enumerate(FFG):
            ng = j1 - j0
            units = [(j, min(2, j1 - j)) for j in range(j0, j1, 2)]
            slots = []
            for _ in units:
                slots.append(wi_cnt[0] % 2)
                wi_cnt[0] += 1

            def wi_view(sl):
                return sbv(WI_OFF + sl * 4096, 4096).rearrange("p (c g f) -> p c g f", c=8, g=2)

            def wi_load(ui):
                ju, nu = units[ui]
                sl = slots[ui]
                wiv = wi_view(sl)
                P.dma("pool", wiv[:, :, 0, 0:nu * 128], w_in[:, :, ju * 128:(ju + nu) * 128], writes=[RWI[sl]], sem=("wi", sl, 0))
                P.dma("pool", wiv[:, :, 1, 0:nu * 128], w_in[:, :, DFF + ju * 128:DFF + (ju + nu) * 128], writes=[RWI[sl]], sem=("wi", sl, 1))
            for ui in range(min(2, len(units))):
                wi_load(ui)
            for jl in range(ng):
                P.dma("pool", wdv[:, jl, :], w_dn[:, j0 + jl, :], writes=[RWD[jl]], sem=("wd", jl))
            for ui, (ju, nu) in enumerate(units):
                sl = slots[ui]
                wiv = wi_view(sl)
                if ui >= 2:
                    wi_load(ui)
                for jj in range(nu):
                    jl = ju + jj - j0
                    for si, (so, sn) in enumerate(sts):
                        b = nxt("pa")
                        xtr = [RXT[t] for t in range(so // 128, (so + sn + 127) // 128)]
                        for half in range(2):
                            for k in range(8):
                                P.op("pe", lambda e, k=k, half=half, b=b, jj=jj, so=so, sn=sn, wiv=wiv:
                                     e.matmul(PA[:, b, half * 512:half * 512 + sn], lhsT=wiv[:, k, half, jj * 128:(jj + 1) * 128],
                                              rhs=XT[:, k, so:so + sn], start=(k == 0), stop=(k == 7)),
                                     reads=[RWI[sl]] + xtr, writes=[RPA[b]], inc=(half == 1 and k == 7))
                        sg = nxt("sg")
                        sgv = sbv(SG_OFF + sg * 512, 512)
                        P.op("act", lambda e, b=b, sn=sn, sgv=sgv: e.activation(out=sgv[:, 0:sn], in_=PA[:, b, 0:sn], func=AF.Silu),
                             reads=[RPA[b]], writes=[RSG[sg]])
                        P.op("dve", lambda e, b=b, sn=sn, sgv=sgv, jl=jl, so=so:
                             e.tensor_tensor(out=actT[:, jl, so:so + sn], in0=sgv[:, 0:sn], in1=PA[:, b, 512:512 + sn], op=ALU.mult),
                             reads=[RPA[b], RSG[sg]], writes=[RACT[jl][si]])
            for t in range(NT):
                tt = ts if t == NT - 1 else 128
                b = nxt("pa")
                for jl in range(ng):
                    for half in range(2):
                        P.op("pe", lambda e, jl=jl, half=half, b=b, t=t, tt=tt:
                             e.matmul(PA[:tt, b, half * 512:(half + 1) * 512], lhsT=actT[:, jl, t * 128:t * 128 + tt],
                                      rhs=wdv[:, jl, half * 512:(half + 1) * 512], start=(jl == 0), stop=(jl == ng - 1)),
                             reads=[RACT[jl][t // 4], RWD[jl]], writes=[RPA[b]], inc=(jl == ng - 1 and half == 1))
                P.op("dve", lambda e, b=b, t=t, tt=tt: e.scalar_tensor_tensor(out=X[:tt, t, :], in0=PA[:tt, b, :], scalar=CF, in1=X[:tt, t, :],
                                                                             op0=ALU.mult, op1=ALU.add),
                     reads=[RPA[b], RX[t]], writes=[RX[t]])
                if gi == len(FFG) - 1:
                    pend.append((t, tt))
                    if len(pend) >= 2:
                        ln_core(*pend[-2])
                    if len(pend) >= 3:
                        ln_fin(*pend[-3])
            if gi == len(FFG) - 1:
                if len(pend) >= 1:
                    ln_core(*pend[-1])
                if len(pend) >= 2:
                    ln_fin(*pend[-2])
                ln_fin(*pend[-1])

    att_q = []
    att_n = [0]

    def attend(nq, qT, ksegs, vblocks, bias_ap, nk, sink_h, o_out):
        n_ = att_n[0]
        att_n[0] += 1
        a = n_ % 4
        sbt, pbt, ptt, rsb, rpb, rptt = att_bufs[n_ % len(att_bufs)]
        r_qk, r_bias, r_v, w_o = list(att_reads), list(att_bias_reads), list(att_v_reads), list(att_o_writes)
        ne = nk + (1 if sink_h is not None else 0)
        nb = len(vblocks)

        def stage1a():
            b = nxt("pa")
            for i, (kap, off, n) in enumerate(ksegs):
                P.op("pe", lambda e, kap=kap, off=off, n=n: e.matmul(PA[:nq, b, off:off + n], lhsT=qT, rhs=kap, start=True, stop=True),
                     reads=r_qk, writes=[RPA[b]], inc=(i == len(ksegs) - 1))
            P.op("dve", lambda e: e.scalar_tensor_tensor(out=sbt[:nq, 0:ne], in0=PA[:nq, b, 0:ne], scalar=0.125, in1=bias_ap,
                                                         op0=ALU.mult, op1=ALU.add),
                 reads=[RPA[b]] + r_bias, writes=[rsb])

        def stage1b():
            P.op("dve", lambda e: e.tensor_reduce(out=AM[:nq, a, 1:2], in_=sbt[:nq, 0:ne], axis=AX.X, op=ALU.max, negate=True),
                 reads=[rsb], writes=[RAM[a]])
            P.op("act", lambda e: e.activation(out=pbt[:nq, 0:ne], in_=sbt[:nq, 0:ne], func=AF.Exp, bias=AM[:nq, a, 1:2], scale=1.0,
                                               accum_out=AM[:nq, a, 2:3]),
                 reads=[rsb, RAM[a]], writes=[rpb, RAM[a]])

        def stage2():
            tb = nxt("pt")
            off = 0
            for i, (vap, n) in enumerate(vblocks):
                P.op("pe", lambda e, i=i, off=off, n=n: e.transpose(out=PTR[:n, tb, i * 128:i * 128 + nq], in_=pbt[:nq, off:off + n],
                                                                      identity=IDENT[:nq, :nq]),
                     reads=[rpb, RCONST], writes=[RPT[tb]], inc=(i == nb - 1))
                off += n
            nfull = sum(1 for (_, n) in vblocks if n == 128)
            if nfull > 0:
                srcv = PTR[:, tb, 0:nfull * 128].rearrange("p (c f) -> p c f", c=nfull)[:, :, 0:nq]
                dstv = ptt[:, 0:nfull * 128].rearrange("p (c f) -> p c f", c=nfull)[:, :, 0:nq]
                P.op("act", lambda e: e.activation(out=dstv, in_=srcv, func=AF.Copy), reads=[RPT[tb]], writes=[rptt])
            for i, (vap, n) in enumerate(vblocks):
                if n != 128:
                    P.op("act", lambda e, i=i, n=n: e.activation(out=ptt[:n, i * 128:i * 128 + nq], in_=PTR[:n, tb, i * 128:i * 128 + nq], func=AF.Copy),
                         reads=[RPT[tb]], writes=[rptt])

        def stage3():
            ob = nxt("ps")
            for i, (vap, n) in enumerate(vblocks):
                P.op("pe", lambda e, i=i, vap=vap, n=n: e.matmul(PS1[:nq, ob, 0:64], lhsT=ptt[:n, i * 128:i * 128 + nq], rhs=vap,
                                                                 start=(i == 0), stop=(i == nb - 1)),
                     reads=[rptt] + r_v, writes=[RPS[ob]], inc=(i == nb - 1))
            P.op("dve", lambda e: e.reciprocal(out=AM[:nq, a, 3:4], in_=AM[:nq, a, 2:3]), reads=[RAM[a]], writes=[RAM[a]])
            P.op("dve", lambda e: e.tensor_scalar(out=o_out, in0=PS1[:nq, ob, 0:64], scalar1=AM[:nq, a, 3:4], scalar2=None, op0=ALU.mult),
                 reads=[RPS[ob], RAM[a]], writes=w_o)

        stage1a()
        att_q.append([stage1b, stage2, stage3])
        if len(att_q) >= 2:
            att_q[-2][0]()
        if len(att_q) >= 3:
            att_q[-3][1]()
        if len(att_q) >= 4:
            att_q[-4][2]()
            att_q.pop(0)

    def att_flush():
        k = len(att_q)
        done = [k - 1 - idx for idx in range(k)]
        while any(d < 3 for d in done):
            for idx in range(k - 1, -1, -1):
                if done[idx] < 3:
                    att_q[idx][done[idx]]()
                    done[idx] += 1
        att_q.clear()

    att_bufs = []
    att_reads = []
    att_bias_reads = []
    att_v_reads = []
    att_o_writes = []

    def out_proj_multi(tiles, OT_list, wo_list, r_ots, r_wos):
        n = len(OT_list)
        for (t, tt, co) in tiles:
            b = nxt("pa")
            for half in range(2):
                for i in range(n):
                    P.op("pe", lambda e, half=half, b=b, i=i, tt=tt, co=co: e.matmul(PA[:tt, b, half * 512:(half + 1) * 512], lhsT=OT_list[i][:, co:co + tt],
                                                                                     rhs=wo_list[i][:, half * 512:(half + 1) * 512], start=(i == 0), stop=(i == n - 1)),
                         reads=list(r_ots) + list(r_wos), writes=[RPA[b]], inc=(half == 1 and i == n - 1))
            P.op("dve", lambda e, b=b, t=t, tt=tt: e.scalar_tensor_tensor(out=X[:tt, t, :], in0=PA[:tt, b, :], scalar=CM, in1=X[:tt, t, :],
                                                                         op0=ALU.mult, op1=ALU.add),
                 reads=[RPA[b], RX[t]], writes=[RX[t]])

    def build_bias(h, dst, rdst, hk, rhk):
        hank, hi, lo = hk
        src = bass.AP(tensor=ext_d.tensor, offset=h * 768, ap=[[1, 128], [1, 640]])
        P.dma("sp", hank, src, writes=[rhk], sem="hank")
        P.op("act", lambda e: e.activation(out=hi, in_=hank, func=AF.Copy), reads=[rhk], writes=[rhk])
        P.op("dve", lambda e: e.tensor_tensor(out=lo, in0=hank, in1=hi, op=ALU.subtract), reads=[rhk], writes=[rhk])
        b = nxt("pa")
        for (o, n) in ((0, 512), (512, 128)):
            P.op("pe", lambda e, o=o, n=n: e.matmul(PA[:, b, o:o + n], lhsT=JREV[:], rhs=hi[:, o:o + n], start=True, stop=False),
                 reads=[rhk, RCONST], writes=[RPA[b]], inc=False)
            P.op("pe", lambda e, o=o, n=n: e.matmul(PA[:, b, o:o + n], lhsT=JREV[:], rhs=lo[:, o:o + n], start=False, stop=True),
                 reads=[rhk, RCONST], writes=[RPA[b]], inc=(o == 512))
        P.op("dve", lambda e: e.tensor_copy(out=dst, in_=PA[:, b, 0:640]), reads=[RPA[b]], writes=[rdst])

    def mixer_a_prompt(seq, ws=False):
        NT, ts, T = 16, 128, SEQ
        P.barrier()
        wq_v = [[sbv(s * 3072 + i * 1024, 1024).rearrange("p (c f) -> p c f", c=8) for i in range(3)] for s in range(2)]
        wo_v = [sbv(6144 + s * 1024, 1024) for s in range(2)]
        qT_v = [sbv(8192 + s * 2048, 2048) for s in range(2)]
        kT_v = [sbv(12288 + s * 2048, 2048) for s in range(2)]
        V_v = [sbv(16384 + s * 2048, 2048).rearrange("p (t f) -> p t f", t=16) for s in range(2)]
        O_v = sbv(20480, 2048).rearrange("p (t f) -> p t f", t=16)
        OT_v = [sbv(22528, 2048), sbv(24576, 2048)]
        att_bufs.clear()
        for a in range(3):
            att_bufs.append((sfv((2560, 3200, 5504)[a], 640), sbv(26624 + a * 640, 640), sbv(29824 + a * 640, 640), Res(), Res(), Res()))
        bias_v = [[sfv(s * 1280 + h2 * 640, 640) for h2 in range(2)] for s in range(2)]
        kst = sfv(3840, 512).rearrange("p (t f) -> p t f", t=4)
        vst = sfv(4352, 512).rearrange("p (t f) -> p t f", t=4)
        hk = (sfv(4864, 640), sbv(28544, 640), sbv(29184, 640))
        rhk = Res()
        RW = [Res(), Res()]
        RWO = [Res(), Res()]
        RQ = [Res(), Res()]
        RK = [Res(), Res()]
        RV = [Res(), Res()]
        RB = [[Res(), Res()], [Res(), Res()]]
        RO, RKST, RVST = Res(), Res(), Res()
        ROT = [Res(), Res()]
        wq3 = wqa_d.rearrange("(c p) f -> p c f", p=128)
        SO = 31744
        qTs, kTn = sbv(SO, 32), sbv(SO + 32, 32)
        kTs = [sbv(SO + 64 + s2 * 528, 528) for s2 in range(2)]
        vn = [sbv(SO + 1120 + s2 * 128, 128) for s2 in range(2)]
        Os = [sbv(SO + 1376 + s2 * 128, 128) for s2 in range(2)]
        OTs = [sbv(SO + 1632 + i * 32, 32) for i in range(2)]
        XBf = XB[:, :, :].rearrange("p a b -> p (a b)")
        kc = [XBf[:, s2 * 512:(s2 + 1) * 512].rearrange("p (t f) -> p t f", t=4) for s2 in range(2)]
        vc = [XBf[:, 1024 + s2 * 512:1024 + (s2 + 1) * 512].rearrange("p (t f) -> p t f", t=4) for s2 in range(2)]
        kst_s, vst_s = sfv(6784, 128), sfv(6912, 128)
        RQs, RKN, RKSTs, RVSTs = Res(), Res(), Res(), Res()
        ROTs = [Res(), Res()]
        RKSs, RKCs, RVCs, RVNs, ROs = ([Res(), Res()] for _ in range(5))

        def samp(hp):
            s = hp % 2
            for s2 in range(2):
                P.dma("pool", kc[s2], cak_d[s2, :, hp * 128:(hp + 1) * 128].rearrange("(t p) f -> p t f", p=128), writes=[RKCs[s2]], sem=("kca", s2))
                P.dma("pool", vc[s2], cav_d[s2, :, hp * 128:(hp + 1) * 128].rearrange("(t p) f -> p t f", p=128), writes=[RVCs[s2]], sem=("vca", s2))
            yield
            for i, (dstv, rr) in enumerate(((qTs, RQs), (kTn, RKN))):
                b = nxt("ps")
                for k in range(8):
                    P.op("pe", lambda e, k=k, b=b, i=i: e.matmul(PS1[:, b, 0:32], lhsT=wq_v[s][i][:, k, :], rhs=XT[:, k, 2048:2080], start=(k == 0), stop=(k == 7)),
                         reads=[RW[s], RXT[16]], writes=[RPS[b]], inc=(k == 7))
                P.op("act", lambda e, b=b, dstv=dstv: e.activation(out=dstv, in_=PS1[:, b, 0:32], func=AF.Copy), reads=[RPS[b]], writes=[rr])
                yield
            for s2 in range(2):
                tb = nxt("pt")
                for t in range(4):
                    P.op("pe", lambda e, tb=tb, t=t, s2=s2: e.transpose(out=PTR[:, tb, t * 128:(t + 1) * 128], in_=kc[s2][:, t, :], identity=IDENT[:]),
                         reads=[RKCs[s2], RCONST], writes=[RPT[tb]], inc=(t == 3))
                P.op("act", lambda e, tb=tb, s2=s2: e.activation(out=kTs[s2][:, 0:512], in_=PTR[:, tb, 0:512], func=AF.Copy), reads=[RPT[tb]], writes=[RKSs[s2]])
                P.op("act", lambda e, s2=s2: e.activation(out=kTs[s2][:, 512:528], in_=kTn[:, 16 * s2:16 * s2 + 16], func=AF.Copy), reads=[RKN], writes=[RKSs[s2]])
                yield
                for (wi_, stg, rstg, dst_d, semn) in ((2, vst_s, RVSTs, avs_d, "vsa"), (1, kst_s, RKSTs, aks_d, "ksa")):
                    b = nxt("ps")
                    for k in range(8):
                        P.op("pe", lambda e, k=k, b=b, wi_=wi_, s2=s2: e.matmul(PS1[:16, b, 0:128], lhsT=XT[:, k, 2048 + 16 * s2:2048 + 16 * s2 + 16], rhs=wq_v[s][wi_][:, k, :],
                                                                               start=(k == 0), stop=(k == 7)),
                             reads=[RW[s], RXT[16]], writes=[RPS[b]], inc=(k == 7))
                    P.op("dve", lambda e, b=b, stg=stg: e.tensor_copy(out=stg[:16, :], in_=PS1[:16, b, 0:128]), reads=[RPS[b]], writes=[rstg])
                    if wi_ == 2:
                        P.op("dve", lambda e, b=b, s2=s2: e.tensor_copy(out=vn[s2][:16, :], in_=PS1[:16, b, 0:128]), reads=[RPS[b]], writes=[RVNs[s2]])
                    P.dma("sp", dst_d[16 * s2:16 * s2 + 16, hp * 128:(hp + 1) * 128], stg[:16, :], reads=[rstg], sem=semn)
                    yield

        def prep(hp):
            s = hp % 2
            for i in range(3):
                P.dma("pool", wq_v[s][i], wq3[:, :, i * 1024 + hp * 128:i * 1024 + (hp + 1) * 128], writes=[RW[s]], sem=("wqa", s, i))
            for h2 in range(2):
                build_bias(2 * hp + h2, bias_v[s][h2], RB[s][h2], hk, rhk)
                bv = bias_v[s][h2]
                P.op("dve", lambda e, bv=bv: e.memset(bv[0:64, 576:640], NEG), writes=[RB[s][h2]])
                P.op("dve", lambda e, bv=bv: e.memset(bv[64:128, 0:64], NEG), writes=[RB[s][h2]])
            yield
            for st in range(4):
                xtr = [RXT[t] for t in range(4 * st, 4 * st + 4)]
                for i, (dstv, rr) in enumerate(((qT_v[s], RQ[s]), (kT_v[s], RK[s]))):
                    b = nxt("ps")
                    for k in range(8):
                        P.op("pe", lambda e, k=k, b=b, i=i, st=st: e.matmul(PS1[:, b, :], lhsT=wq_v[s][i][:, k, :], rhs=XT[:, k, st * 512:(st + 1) * 512],
                                                                            start=(k == 0), stop=(k == 7)),
                             reads=[RW[s]] + xtr, writes=[RPS[b]], inc=(k == 7))
                    if i == 0:
                        P.op("act", lambda e, b=b, dstv=dstv, st=st: e.activation(out=dstv[:, st * 512:(st + 1) * 512], in_=PS1[:, b, :], func=AF.Copy),
                             reads=[RPS[b]], writes=[rr])
                    else:
                        P.op("dve", lambda e, b=b, dstv=dstv, st=st: e.tensor_copy(out=dstv[:, st * 512:(st + 1) * 512], in_=PS1[:, b, :]),
                             reads=[RPS[b]], writes=[rr])
            yield
            for g4 in range(4):
                b = nxt("ps")
                for tt_ in range(4):
                    t = g4 * 4 + tt_
                    for k in range(8):
                        P.op("pe", lambda e, k=k, b=b, t=t, tt_=tt_: e.matmul(PS1[:, b, tt_ * 128:(tt_ + 1) * 128], lhsT=XT[:, k, t * 128:(t + 1) * 128],
                                                                              rhs=wq_v[s][2][:, k, :], start=(k == 0), stop=(k == 7)),
                             reads=[RW[s], RXT[t]], writes=[RPS[b]], inc=(tt_ == 3 and k == 7))
                P.op("act", lambda e, b=b, g4=g4: e.activation(out=V_v[s][:, g4 * 4:(g4 + 1) * 4, :], in_=PS1[:, b, :].rearrange("p (t f) -> p t f", t=4), func=AF.Copy),
                     reads=[RPS[b]], writes=[RV[s]])
                if g4 == 3:
                    P.op("dve", lambda e, b=b: e.tensor_copy(out=vst, in_=PS1[:, b, :].rearrange("p (t f) -> p t f", t=4)),
                         reads=[RPS[b], RV[s]], writes=[RVST])
                    P.dma("sp", avp_d[seq, :, hp * 128:(hp + 1) * 128].rearrange("(t p) f -> p t f", p=128), vst, reads=[RVST], sem="vst")
            b = nxt("ps")
            for tt_ in range(4):
                t = 12 + tt_
                for k in range(8):
                    P.op("pe", lambda e, k=k, b=b, t=t, tt_=tt_: e.matmul(PS1[:, b, tt_ * 128:(tt_ + 1) * 128], lhsT=XT[:, k, t * 128:(t + 1) * 128],
                                                                          rhs=wq_v[s][1][:, k, :], start=(k == 0), stop=(k == 7)),
                         reads=[RW[s], RXT[t]], writes=[RPS[b]], inc=(tt_ == 3 and k == 7))
            P.op("dve", lambda e, b=b: e.tensor_copy(out=kst, in_=PS1[:, b, :].rearrange("p (t f) -> p t f", t=4)), reads=[RPS[b]], writes=[RKST])
            P.dma("sp", akp_d[seq, :, hp * 128:(hp + 1) * 128].rearrange("(t p) f -> p t f", p=128), kst, reads=[RKST], sem="kst")
            yield

        def attn(hp, gen, pg=None):
            s = hp % 2
            sg = samp(hp) if ws else None
            for j in range(16):
                klo = max(0, (j - 4) * 128)
                khi = (j + 1) * 128
                nk = khi - klo
                c_lo = 640 - nk
                for h2 in range(2):
                    qT = qT_v[s][h2 * 64:(h2 + 1) * 64, j * 128:(j + 1) * 128]
                    ksegs = []
                    o = 0
                    while o < nk:
                        n = min(512 - (o % 512), nk - o)
                        ksegs.append((kT_v[s][h2 * 64:(h2 + 1) * 64, klo + o:klo + o + n], o, n))
                        o += n
                    vblocks = [(V_v[s][:, klo // 128 + i, h2 * 64:(h2 + 1) * 64], 128) for i in range(nk // 128)]
                    att_reads[:] = [RQ[s], RK[s]]
                    att_bias_reads[:] = [RB[s][h2]]
                    att_v_reads[:] = [RV[s]]
                    att_o_writes[:] = [RO]
                    attend(128, qT, ksegs, vblocks, bias_v[s][h2][:, c_lo:640], nk, None, O_v[:, j, h2 * 64:(h2 + 1) * 64])
                    if gen is not None:
                        next(gen, None)
                    if sg is not None:
                        next(sg, None)
                    if pg is not None:
                        next(pg, None)
            if pg is not None:
                for _ in pg:
                    pass
            if sg is not None:
                for _ in sg:
                    pass
                for s2 in range(2):
                    for h2 in range(2):
                        hs = slice(h2 * 64, (h2 + 1) * 64)
                        ksegs = [(kTs[s2][hs, 0:512], 0, 512), (kTs[s2][hs, 512:528], 512, 16)]
                        vblocks = [(vc[s2][:, t, hs], 128) for t in range(4)] + [(vn[s2][:16, hs], 16)]
                        att_reads[:] = [RQs, RKSs[s2]]
                        att_bias_reads[:] = [RB[s][h2]]
                        att_v_reads[:] = [RVCs[s2], RVNs[s2]]
                        att_o_writes[:] = [ROs[s2]]
                        attend(16, qTs[hs, 16 * s2:16 * s2 + 16], ksegs, vblocks, bias_v[s][h2][0:16, 0:528], 528, None, Os[s2][:16, hs])
            if gen is not None:
                for _ in gen:
                    pass
            att_flush()

        def post(hp):
            s = hp % 2
            for g8 in range(2):
                tb = nxt("pt")
                for i in range(8):
                    t = g8 * 8 + i
                    P.op("pe", lambda e, tb=tb, i=i, t=t: e.transpose(out=PTR[:, tb, i * 128:(i + 1) * 128], in_=O_v[:, t, :], identity=IDENT[:]),
                         reads=[RO, RCONST], writes=[RPT[tb]], inc=(i == 7))
                P.op("act", lambda e, tb=tb, g8=g8: e.activation(out=OT_v[s][:, g8 * 1024:(g8 + 1) * 1024], in_=PTR[:, tb, :], func=AF.Copy),
                     reads=[RPT[tb]], writes=[ROT[s]])
            if ws:
                for s2 in range(2):
                    tb = nxt("pt")
                    P.op("pe", lambda e, tb=tb, s2=s2: e.transpose(out=PTR[:, tb, 0:16], in_=Os[s2][:16, :], identity=IDENT[:16, :16]), reads=[ROs[s2], RCONST], writes=[RPT[tb]])
                    P.op("act", lambda e, tb=tb, s2=s2: e.activation(out=OTs[s][:, 16 * s2:16 * s2 + 16], in_=PTR[:, tb, 0:16], func=AF.Copy), reads=[RPT[tb]], writes=[ROTs[s]])
            yield
            if s == 1:
                for t in range(16):
                    out_proj_multi([(t, 128, t * 128)], OT_v, wo_v, ROT, RWO)
                    if t % 8 == 7:
                        yield
                if ws:
                    out_proj_multi([(16, 32, 0)], OTs, wo_v, ROTs, RWO)
                for i in range(2):
                    if hp + 1 + i < 8:
                        P.dma("pool", wo_v[i], woa_d[(hp + 1 + i) * 128:(hp + 2 + i) * 128, :], writes=[RWO[i]], sem=("woa", i))

        for i in range(2):
            P.dma("pool", wo_v[i], woa_d[i * 128:(i + 1) * 128, :], writes=[RWO[i]], sem=("woa", i))
        for _ in prep(0):
            pass
        pg = None
        for hp in range(8):
            gen = prep(hp + 1) if hp < 7 else None
            attn(hp, gen, pg)
            pg = post(hp)
        for _ in pg:
            pass
        P.barrier()

    def mixer_b_prompt(seq, ws=False, ws2=True):
        NT, ts, T = 16, 128, SEQ
        P.barrier()
        wk_raw = sbv(0, 2048).rearrange("p (c f) -> p c f", c=8)
        wv_raw = sbv(2048, 2048).rearrange("p (c f) -> p c f", c=8)
        kT_pair = [sbv(4096 + pr * 2048, 2048) for pr in range(2)]
        kp_raw = sbv(8192, 2048).rearrange("p (c f) -> p c f", c=8)
        kTn_pair = [sbv(10240 + pr * 32, 32) for pr in range(2)]
        RKP = [Res(), Res()]
        RKNP = [Res(), Res()]
        kT_v = [sbv(12288 + g * 2048, 2048) for g in range(4)]
        V_v = sbv(20480, 4096).rearrange("p (t f) -> p t f", t=16)
        wq_v = [sbv(24576 + s * 1024, 1024).rearrange("p (c f) -> p c f", c=8) for s in range(2)]
        wqp_v = [sbv(26624 + s * 1024, 1024).rearrange("p (c f) -> p c f", c=8) for s in range(2)]
        wo_v = [sbv(28672 + s * 1024, 1024) for s in range(2)]
        cosT = sfv(0, 2048)
        sinT = sfv(2048, 2048)
        tmp = [sfv(4096 + i * 512, 512) for i in range(2)]
        maskB = sfv(5120, 256)
        kst = sfv(5376, 256)
        vst = sfv(5632, 256)
        ctok = sfv(5888, 256)
        stok = sfv(6144, 256)
        ktmp = sfv(6400, 256)
        RWK, RWV, RKD, RKT, RVV, RTAB, RTMP, RMASK = Res(), Res(), Res(), [Res() for _ in range(4)], Res(), Res(), [Res(), Res()], Res()
        RKST, RVST = Res(), Res()
        wq3 = wqb_d.rearrange("(c p) f -> p c f", p=128)
        P.dma("pool", wk_raw, wq3[:, :, 1024:1280], writes=[RWK], sem="wkb")
        P.dma("pool", wv_raw, wq3[:, :, 1280:1536], writes=[RWV], sem="wvb")
        P.dma("sp", cosT, cosT_d, writes=[RTAB], sem="t1")
        P.dma("sp", sinT, sinT_d, writes=[RTAB], sem="t2")
        P.dma("sp", ctok, ctok_d, writes=[RTAB], sem="t3")
        P.dma("sp", stok, stok_d, writes=[RTAB], sem="t4")
        P.op("dve", lambda e: e.memset(maskB, 0.0), writes=[RMASK])
        P.op("dve", lambda e: e.memset(maskB[0:64, 192:256], NEG), writes=[RMASK])
        P.op("dve", lambda e: e.memset(maskB[64:128, 0:64], NEG), writes=[RMASK])
        for g in range(4):
            P.op("act", lambda e, g=g: e.activation(out=kp_raw[:, :, g * 64:g * 64 + 32], in_=wk_raw[:, :, g * 64 + 32:g * 64 + 64], func=AF.Copy),
                 reads=[RWK], writes=[RKD])
            P.op("act", lambda e, g=g: e.activation(out=kp_raw[:, :, g * 64 + 32:g * 64 + 64], in_=wk_raw[:, :, g * 64:g * 64 + 32], func=AF.Copy),
                 reads=[RWK], writes=[RKD])

        def rope_proj(w_a, w_b, rw, dst, rdst, ncols, cT, sT, xcols, tco=None, dco=None):
            so, sn = xcols
            tco = so if tco is None else tco
            dco = so if dco is None else dco
            xtr = [RXT[t] for t in range(so // 128, (so + sn + 127) // 128)]
            bb = []
            for w in (w_a, w_b):
                b = nxt("ps")
                bb.append(b)
                for k in range(8):
                    P.op("pe", lambda e, k=k, b=b, w=w: e.matmul(PS1[:, b, 0:sn], lhsT=w[:, k, :], rhs=XT[:, k, so:so + sn], start=(k == 0), stop=(k == 7)),
                         reads=rw + xtr, writes=[RPS[b]], inc=(k == 7))
            P.op("dve", lambda e: e.tensor_tensor(out=tmp[0][:, 0:sn], in0=PS1[:, bb[0], 0:sn], in1=cT[:, tco:tco + sn], op=ALU.mult),
                 reads=[RPS[bb[0]], RTAB], writes=[RTMP[0]])
            P.op("dve", lambda e: e.tensor_tensor(out=tmp[1][:, 0:sn], in0=PS1[:, bb[1], 0:sn], in1=sT[:, tco:tco + sn], op=ALU.mult),
                 reads=[RPS[bb[1]], RTAB], writes=[RTMP[1]])
            P.op("pool", lambda e: e.tensor_tensor(out=dst[:, dco:dco + sn], in0=tmp[0][:, 0:sn], in1=tmp[1][:, 0:sn], op=ALU.add),
                 reads=[RTMP[0], RTMP[1]], writes=[rdst])

        for pr in range(2):
            for st in range(4):
                rope_proj(wk_raw[:, :, pr * 128:(pr + 1) * 128], kp_raw[:, :, pr * 128:(pr + 1) * 128], [RWK, RKD], kT_pair[pr], RKP[pr], 128, cosT, sinT,
                          (st * 512, 512))
        for g in range(4):
            src = kT_pair[g // 2][(g % 2) * 64:(g % 2) * 64 + 64, :]
            for hh in range(2):
                P.dma("sp", kT_v[g][hh * 64:(hh + 1) * 64, :], src, reads=[RKP[g // 2]], writes=[RKT[g]], sem=("kdup", g, hh))
        for t2 in range(8):
            b = nxt("ps")
            for i in range(2):
                t = t2 * 2 + i
                for k in range(8):
                    P.op("pe", lambda e, k=k, b=b, t=t, i=i: e.matmul(PS1[:, b, i * 256:(i + 1) * 256], lhsT=XT[:, k, t * 128:(t + 1) * 128], rhs=wv_raw[:, k, :],
                                                                      start=(k == 0), stop=(k == 7)),
                         reads=[RWV, RXT[t]], writes=[RPS[b]], inc=(i == 1 and k == 7))
            P.op("act", lambda e, b=b, t2=t2: e.activation(out=V_v[:, t2 * 2:t2 * 2 + 2, :], in_=PS1[:, b, :].rearrange("p (t f) -> p t f", t=2), func=AF.Copy),
                 reads=[RPS[b]], writes=[RVV])
            if t2 == 7:
                P.op("dve", lambda e, b=b: e.tensor_copy(out=vst, in_=PS1[:, b, 256:512]), reads=[RPS[b], RVV], writes=[RVST])
                P.dma("sp", bvp_d[seq], vst, reads=[RVST], sem="vstb")
        b = nxt("ps")
        for k in range(8):
            P.op("pe", lambda e, k=k, b=b: e.matmul(PS1[:, b, 0:256], lhsT=XT[:, k, 15 * 128:16 * 128], rhs=wk_raw[:, k, :], start=(k == 0), stop=(k == 7)),
                 reads=[RWK, RXT[15]], writes=[RPS[b]], inc=(k == 7))
        rope_tok(128, PS1[:, b, 0:256], RPS[b], ctok, stok, RTAB, kst, ktmp, RKST)
        P.dma("sp", bkp_d[seq], kst, reads=[RKST], sem="kstb")
        cosTs, sinTs = sfv(7712, 32), sfv(7744, 32)
        XBf = XB[:, :, :].rearrange("p a b -> p (a b)")
        kTn_s = [XBf[:, g * 32:(g + 1) * 32] for g in range(4)]
        kTs_s = [[XBf[:, 128 + (s2 * 4 + g) * 144:128 + (s2 * 4 + g + 1) * 144] for g in range(4)] for s2 in range(2)]
        vc_s = [XBf[:, 1280 + s2 * 256:1280 + (s2 + 1) * 256] for s2 in range(2)]
        vn_s = [sbv(32896 + s2 * 256, 256) for s2 in range(2)]
        kcd = [sbv(33408 + i * 128, 128) for i in range(8)]
        RKCDs = [Res() for _ in range(8)]
        RKNs = [Res() for _ in range(4)]
        RKSs = [[Res() for _ in range(4)] for _ in range(2)]
        RVCs, RVNs, ROs = [Res(), Res()], [Res(), Res()], [Res(), Res()]
        RKCD, RQs = Res(), Res()
        ROTs = [Res(), Res()]
        if ws:
            P.dma("sp", cosTs, cosTs_d, writes=[RTAB], sem="t1")
            P.dma("sp", sinTs, sinTs_d, writes=[RTAB], sem="t2")
            P.dma("sp", ctok[:16, :], ctoks_d, writes=[RTAB], sem="t3")
            P.dma("sp", stok[:16, :], stoks_d, writes=[RTAB], sem="t4")
            for pr in range(2):
                rope_proj(wk_raw[:, :, pr * 128:(pr + 1) * 128], kp_raw[:, :, pr * 128:(pr + 1) * 128], [RWK, RKD], kTn_pair[pr], RKNP[pr], 128, cosTs, sinTs,
                          (2048, 32), tco=0, dco=0)
            for g in range(4):
                srcn = kTn_pair[g // 2][(g % 2) * 64:(g % 2) * 64 + 64, :]
                for hh in range(2):
                    P.dma("sp", kTn_s[g][hh * 64:(hh + 1) * 64, :], srcn, reads=[RKNP[g // 2]], writes=[RKNs[g]], sem=("kdupn", g, hh))
            for s2 in range(2):
                P.dma("pool", vc_s[s2], cbv_d[s2], writes=[RVCs[s2]], sem=("vcb", s2))
                for g in range(4):
                    for hh in range(2):
                        P.dma("pool", kcd[s2 * 4 + g][:, hh * 64:(hh + 1) * 64], cbk_d[s2, :, g * 64:(g + 1) * 64], writes=[RKCDs[s2 * 4 + g]],
                              sem=("kcb", s2 * 4 + g, hh))
            for s2 in range(2):
                for g in range(4):
                    tb = nxt("pt")
                    P.op("pe", lambda e, tb=tb, s2=s2, g=g: e.transpose(out=PTR[:, tb, 0:128], in_=kcd[s2 * 4 + g], identity=IDENT[:]),
                         reads=[RKCDs[s2 * 4 + g], RCONST], writes=[RPT[tb]])
                    P.op("act", lambda e, tb=tb, s2=s2, g=g: e.activation(out=kTs_s[s2][g][:, 0:128], in_=PTR[:, tb, 0:128], func=AF.Copy), reads=[RPT[tb]], writes=[RKSs[s2][g]])
                    P.op("act", lambda e, s2=s2, g=g: e.activation(out=kTs_s[s2][g][:, 128:144], in_=kTn_s[g][:, 16 * s2:16 * s2 + 16], func=AF.Copy), reads=[RKNs[g]], writes=[RKSs[s2][g]])
                b = nxt("ps")
                for k in range(8):
                    P.op("pe", lambda e, k=k, b=b, s2=s2: e.matmul(PS1[:16, b, 0:256], lhsT=XT[:, k, 2048 + 16 * s2:2048 + 16 * s2 + 16], rhs=wv_raw[:, k, :], start=(k == 0), stop=(k == 7)),
                         reads=[RWV, RXT[16]], writes=[RPS[b]], inc=(k == 7))
                P.op("dve", lambda e, b=b: e.tensor_copy(out=vst[:16, :], in_=PS1[:16, b, 0:256]), reads=[RPS[b]], writes=[RVST])
                P.op("dve", lambda e, b=b, s2=s2: e.tensor_copy(out=vn_s[s2][:16, :], in_=PS1[:16, b, 0:256]), reads=[RPS[b]], writes=[RVNs[s2]])
                P.dma("sp", bvs_d[16 * s2:16 * s2 + 16, :], vst[:16, :], reads=[RVST], sem="vsb")
                b = nxt("ps")
                for k in range(8):
                    P.op("pe", lambda e, k=k, b=b, s2=s2: e.matmul(PS1[:16, b, 0:256], lhsT=XT[:, k, 2048 + 16 * s2:2048 + 16 * s2 + 16], rhs=wk_raw[:, k, :], start=(k == 0), stop=(k == 7)),
                         reads=[RWK, RXT[16]], writes=[RPS[b]], inc=(k == 7))
                rope_tok(16, PS1[:16, b, 0:256], RPS[b], ctok, stok, RTAB, kst, ktmp, RKST)
                P.dma("sp", bks_d[16 * s2:16 * s2 + 16, :], kst[:16, :], reads=[RKST], sem="ksb")
        P.barrier()
        qTs = sbv(0, 32)
        Os_s = [sbv(64 + s2 * 128, 128) for s2 in range(2)]
        OTs = [sbv(320, 32), sbv(352, 32)]
        ws = ws and ws2
        maskH = [[sfv(5376 + (s_ * 2 + h2) * 257, 257) for h2 in range(2)] for s_ in range(2)]
        zbH = [sfv(6404 + h2 * 145, 145) for h2 in range(2)]
        RMH = [[Res(), Res()], [Res(), Res()]]
        RZB = [Res(), Res()]
        kz = sbv(400, 2)
        RKZ = Res()
        P.op("pool", lambda e: e.memset(kz, 0.0), writes=[RKZ])
        for s_ in range(2):
            for h2 in range(2):
                P.op("pool", lambda e, s_=s_, h2=h2: e.tensor_copy(out=maskH[s_][h2][:, 0:256], in_=maskB), reads=[RMASK], writes=[RMH[s_][h2]])
        if ws:
            for h2 in range(2):
                P.op("pool", lambda e, h2=h2: e.memset(zbH[h2], 0.0), writes=[RZB[h2]])
        qT_v = [sbv(4096 + s * 2048, 2048) for s in range(2)]
        O_v = sbv(8192, 2048).rearrange("p (t f) -> p t f", t=16)
        OT_v = [sbv(10240, 2048), sbv(2048, 2048)]
        att_bufs.clear()
        for a in range(3):
            att_bufs.append((sfv(6700 + a * 257, 257), sbv(30720 + a * 288, 288), sbv(31872 + a * 256, 256), Res(), Res(), Res()))
        RW = [Res(), Res()]
        RWP = [Res(), Res()]
        RWO = [Res(), Res()]
        RQ = [Res(), Res()]
        RO = Res()
        ROT = [Res(), Res()]
        def prep(hp):
            s = hp % 2
            P.dma("pool", wq_v[s], wq3[:, :, hp * 128:(hp + 1) * 128], writes=[RW[s]], sem=("wqb", s))
            for h2 in range(2):
                P.op("act", lambda e, h2=h2: e.activation(out=maskH[s][h2][:, 256:257], in_=SNK[:, 2 * hp + h2:2 * hp + h2 + 1], func=AF.Copy),
                     reads=[RCONST], writes=[RMH[s][h2]])
            for hh in range(2):
                P.op("act", lambda e, hh=hh: e.activation(out=wqp_v[s][:, :, hh * 64:hh * 64 + 32], in_=wq_v[s][:, :, hh * 64 + 32:hh * 64 + 64], func=AF.Copy),
                     reads=[RW[s]], writes=[RWP[s]])
                P.op("act", lambda e, hh=hh: e.activation(out=wqp_v[s][:, :, hh * 64 + 32:hh * 64 + 64], in_=wq_v[s][:, :, hh * 64:hh * 64 + 32], func=AF.Copy),
                     reads=[RW[s]], writes=[RWP[s]])
            yield
            for st in range(4):
                rope_proj(wq_v[s], wqp_v[s], [RW[s], RWP[s]], qT_v[s], RQ[s], 128, cosT, sinT, (st * 512, 512))
            yield

        def attn(hp, gen, pg=None):
            s = hp % 2
            g = hp // 2
            if ws:
                for h2 in range(2):
                    P.op("act", lambda e, h2=h2: e.activation(out=zbH[h2][:16, 144:145], in_=SNK[:16, 2 * hp + h2:2 * hp + h2 + 1], func=AF.Copy),
                         reads=[RCONST], writes=[RZB[h2]])
                rope_proj(wq_v[s], wqp_v[s], [RW[s], RWP[s]], qTs, RQs, 128, cosTs, sinTs, (2048, 32), tco=0, dco=0)
            for j in range(16):
                klo = max(0, (j - 1) * 128)
                khi = (j + 1) * 128
                nk = khi - klo
                c_lo = 256 - nk
                for h2 in range(2):
                    qT = qT_v[s][h2 * 64:(h2 + 1) * 64, j * 128:(j + 1) * 128]
                    ksegs = [(kT_v[g][h2 * 64:(h2 + 1) * 64, klo:khi], 0, nk), (kz[h2 * 64:(h2 + 1) * 64, 0:1], nk, 1)]
                    vblocks = [(V_v[:, klo // 128 + i, g * 64:(g + 1) * 64], 128) for i in range(nk // 128)]
                    att_reads[:] = [RQ[s], RKT[g], RKZ]
                    att_bias_reads[:] = [RMH[s][h2]]
                    att_v_reads[:] = [RVV]
                    att_o_writes[:] = [RO]
                    attend(128, qT, ksegs, vblocks, maskH[s][h2][:, c_lo:257], nk, 2 * hp + h2, O_v[:, j, h2 * 64:(h2 + 1) * 64])
                    if gen is not None and (j * 2 + h2) % 4 == 3:
                        next(gen, None)
                    if pg is not None:
                        next(pg, None)
            if pg is not None:
                for _ in pg:
                    pass
            if ws:
                for s2 in range(2):
                    for h2 in range(2):
                        hs = slice(h2 * 64, (h2 + 1) * 64)
                        gs = slice(g * 64, (g + 1) * 64)
                        ksegs = [(kTs_s[s2][g][hs, 0:144], 0, 144), (kz[hs, 0:1], 144, 1)]
                        vblocks = [(vc_s[s2][:, gs], 128), (vn_s[s2][:16, gs], 16)]
                        att_reads[:] = [RQs, RKSs[s2][g], RKZ]
                        att_bias_reads[:] = [RZB[h2]]
                        att_v_reads[:] = [RVCs[s2], RVNs[s2]]
                        att_o_writes[:] = [ROs[s2]]
                        attend(16, qTs[hs, 16 * s2:16 * s2 + 16], ksegs, vblocks, zbH[h2][0:16, 0:145], 144, 2 * hp + h2, Os_s[s2][:16, hs])
            if gen is not None:
                for _ in gen:
                    pass
            att_flush()

        def post(hp):
            s = hp % 2
            for g8 in range(2):
                tb = nxt("pt")
                for i in range(8):
                    t = g8 * 8 + i
                    P.op("pe", lambda e, tb=tb, i=i, t=t: e.transpose(out=PTR[:, tb, i * 128:(i + 1) * 128], in_=O_v[:, t, :], identity=IDENT[:]),
                         reads=[RO, RCONST], writes=[RPT[tb]], inc=(i == 7))
                P.op("act", lambda e, tb=tb, g8=g8: e.activation(out=OT_v[s][:, g8 * 1024:(g8 + 1) * 1024], in_=PTR[:, tb, :], func=AF.Copy),
                     reads=[RPT[tb]], writes=[ROT[s]])
            if ws:
                for s2 in range(2):
                    tb = nxt("pt")
                    P.op("pe", lambda e, tb=tb, s2=s2: e.transpose(out=PTR[:, tb, 0:16], in_=Os_s[s2][:16, :], identity=IDENT[:16, :16]), reads=[ROs[s2], RCONST], writes=[RPT[tb]])
                    P.op("act", lambda e, tb=tb, s2=s2: e.activation(out=OTs[s][:, 16 * s2:16 * s2 + 16], in_=PTR[:, tb, 0:16], func=AF.Copy), reads=[RPT[tb]], writes=[ROTs[s]])
            yield
            if s == 1:
                for t in range(16):
                    out_proj_multi([(t, 128, t * 128)], OT_v, wo_v, ROT, RWO)
                    if t % 8 == 7:
                        yield
                if ws:
                    out_proj_multi([(16, 32, 0)], OTs, wo_v, ROTs, RWO)
                for i in range(2):
                    if hp + 1 + i < 8:
                        P.dma("pool", wo_v[i], wob_d[(hp + 1 + i) * 128:(hp + 2 + i) * 128, :], writes=[RWO[i]], sem=("wob", i))

        for i in range(2):
            P.dma("pool", wo_v[i], wob_d[i * 128:(i + 1) * 128, :], writes=[RWO[i]], sem=("wob", i))
        for _ in prep(0):
            pass
        pg = None
        for hp in range(8):
            gen = prep(hp + 1) if hp < 7 else None
            attn(hp, gen, pg)
            pg = post(hp)
        for _ in pg:
            pass
        P.barrier()

    def rope_tok(n, kps, rkps, ctok, stok, rtab, dst, ktmp, rdst):
        k4 = kps.rearrange("p (h a d) -> p h a d", h=4, a=2)
        s4 = stok[:n, :].rearrange("p (h a d) -> p h a d", h=4, a=2)
        t4 = ktmp[:n, :].rearrange("p (h a d) -> p h a d", h=4, a=2)
        P.op("dve", lambda e: e.tensor_tensor(out=dst[:n, :], in0=kps, in1=ctok[:n, :], op=ALU.mult), reads=[rkps, rtab], writes=[rdst])
        P.op("dve", lambda e: e.tensor_tensor(out=t4[:, :, 0, :], in0=k4[:, :, 1, :], in1=s4[:, :, 0, :], op=ALU.mult), reads=[rkps, rtab], writes=[rdst])
        P.op("dve", lambda e: e.tensor_tensor(out=t4[:, :, 1, :], in0=k4[:, :, 0, :], in1=s4[:, :, 1, :], op=ALU.mult), reads=[rkps, rtab], writes=[rdst])
        P.op("dve", lambda e: e.tensor_tensor(out=dst[:n, :], in0=dst[:n, :], in1=ktmp[:n, :], op=ALU.add), reads=[rdst], writes=[rdst])

    def load_x(src, NT, ts, t0=0):
        for t in range(NT):
            tt = ts if t == NT - 1 else 128
            P.dma("sp", X[:tt, t0 + t, :], src[t * 128:t * 128 + tt, :], writes=[RX[t0 + t]], sem=("xin", t % 4))
            make_xT(t0 + t, tt)

    def prompt_pass(seq, ws):
        NT, tsl = (17, 32) if ws else (16, 128)
        load_x(xp_d[seq], 16, 128)
        if ws:
            load_x(xs_d, 1, 32, t0=16)

        def fo(t, tt):
            return ys_d[0:32, :] if t == 16 else yp_d[seq, t * 128:t * 128 + tt, :]
        ffn(0, 0, NT, tsl)
        mixer_a_prompt(seq, ws and stop_after != 'wsb')
        load_ln(1)
        ln_loop(NT, tsl)
        ffn(1, 2, NT, tsl)
        ffn(2, 3, NT, tsl)
        mixer_b_prompt(seq, ws and stop_after != 'wsa', ws2=(stop_after != 'wsb1'))
        load_ln(4)
        ln_loop(NT, tsl)
        ffn(3, 5, NT, tsl, final_out=fo)

    try:
        for seq in range(n_prompt):
            prompt_pass(seq, do_sample and seq == 0)
    except _Stop:
        pass

    P.finish()
    return nc, P


def _rope_tables():
    half = 32
    inv = (10000.0 ** (-np.arange(half, dtype=np.float32) / half)).astype(np.float32)

    def feat(pos):
        ang = pos.astype(np.float32)[None, :] * inv[:, None]
        c = np.cos(ang).astype(np.float32)
        s = np.sin(ang).astype(np.float32)
        cT = np.concatenate([c, c, c, c], axis=0)
        sT = np.concatenate([-s, s, -s, s], axis=0)
        return np.ascontiguousarray(cT), np.ascontiguousarray(sT)

    def tok(pos):
        ang = pos.astype(np.float32)[:, None] * inv[None, :]
        c = np.cos(ang).astype(np.float32)
        s = np.sin(ang).astype(np.float32)
        ct = np.tile(np.concatenate([c, c], axis=1), (1, 4))
        st = np.tile(np.concatenate([-s, s], axis=1), (1, 4))
        return np.ascontiguousarray(ct), np.ascontiguousarray(st)
    return feat, tok


_CACHE = {}


def kernel(x_prompt, x_sample, cache_a_k, cache_a_v, cache_b_k, cache_b_v, ln_g, ln_b, w_ffn_in, w_ffn_down,
           w_qkv_a, w_o_a, rel_bias_a, w_qkv_b, w_o_b, sinks_b):
    f = lambda a: np.ascontiguousarray(np.asarray(a, dtype=np.float32))
    if "nc" not in _CACHE:
        _CACHE["nc"] = build()[0]
    nc = _CACHE["nc"]
    feat, tok = _rope_tables()
    cosT, sinT = feat(np.arange(SEQ))
    spos = 2048 + np.concatenate([np.arange(16), np.arange(16)])
    cosTs, sinTs = feat(spos)
    ctok, stok = tok(np.arange(SEQ - 128, SEQ))
    ctoks, stoks = tok(2048 + np.arange(16))
    idx = np.clip(639 - np.arange(768), -128, 128) + 128
    ext = f(np.asarray(rel_bias_a)[0][:, idx])
    shared = {
        "lng": f(np.asarray(ln_g).reshape(6, D)), "lnb": f(np.asarray(ln_b).reshape(6, D)),
        "win": f(np.asarray(w_ffn_in).reshape(4, D, 2 * DFF)), "wdn": f(np.asarray(w_ffn_down).reshape(4, DFF, D)),
        "wqa": f(np.asarray(w_qkv_a)[0]), "woa": f(np.asarray(w_o_a)[0]), "wqb": f(np.asarray(w_qkv_b)[0]), "wob": f(np.asarray(w_o_b)[0]),
        "ext": ext, "snk": f(np.asarray(sinks_b).reshape(1, 16)),
        "idn": np.eye(128, dtype=np.float32), "jrev": np.ascontiguousarray(np.eye(128, dtype=np.float32)[::-1]),
        "cosT": cosT, "sinT": sinT, "cosTs": cosTs, "sinTs": sinTs, "ctok": ctok, "stok": stok, "ctoks": ctoks, "stoks": stoks,
    }
    xp = np.asarray(x_prompt, dtype=np.float32)
    xs = np.asarray(x_sample, dtype=np.float32)
    cak = np.asarray(cache_a_k, dtype=np.float32)[0].reshape(16, 512, D)
    cav = np.asarray(cache_a_v, dtype=np.float32)[0].reshape(16, 512, D)
    cbk = np.asarray(cache_b_k, dtype=np.float32)[0].reshape(16, 128, 256)
    cbv = np.asarray(cache_b_v, dtype=np.float32)[0].reshape(16, 128, 256)
    in_maps = []
    for c in range(NCORES):
        m = dict(shared)
        sl = slice(2 * c, 2 * c + 2)
        m["xp"] = f(xp[sl])
        m["xs"] = f(xs[sl].reshape(32, D))
        m["cak"] = f(cak[sl])
        m["cav"] = f(cav[sl])
        m["cbk"] = f(cbk[sl])
        m["cbv"] = f(cbv[sl])
        in_maps.append(m)
    res = run_bass_kernel_spmd(nc, in_maps, core_ids=list(range(NCORES)))
    R = res.results
    cat = lambda k: np.concatenate([r[k] for r in R], axis=0)
    yp = cat("yp")
    ys = cat("ys").reshape(16, 16, D)
    akp = cat("akp").reshape(1, 16, 512, 16, 64)
    avp = cat("avp").reshape(1, 16, 512, 16, 64)
    bkp = cat("bkp").reshape(1, 16, 128, 4, 64)
    bvp = cat("bvp").reshape(1, 16, 128, 4, 64)
    aks = cat("aks").reshape(1, 16, 16, 16, 64)
    avs = cat("avs").reshape(1, 16, 16, 16, 64)
    bks = cat("bks").reshape(1, 16, 16, 4, 64)
    bvs = cat("bvs").reshape(1, 16, 16, 4, 64)
    return (yp, ys, akp, avp, bkp, bvp, aks, avs, bks, bvs)
```
